# Optimizing a Trainium2 kernel written in Bass

```python
import jax, jax.numpy as jnp
from jax import lax
import numpy as np

D_MODEL = 1024
BATCH = 8
SEQ = 2048
DEPTH = 2

GRID_W = 64
CTX_LEN = 256
N_DIR = 2
CONV_W = 4
EPS = 1e-6
D_MIX = D_MODEL
D_FF = 4 * D_MODEL
GDN_HD = 64
GDN_WIDTH = 3 * D_MODEL // 8
GDN_HEADS = GDN_WIDTH // GDN_HD
GDN_CHUNK = 64
LRU_WIDTH = D_MODEL // 4
LRU_BLOCKS = 4
LRU_BW = LRU_WIDTH // LRU_BLOCKS
LRU_C = 8.0
RWKV_WIDTH = D_MIX - GDN_WIDTH - LRU_WIDTH
RWKV_HD = 64
RWKV_HEADS = RWKV_WIDTH // RWKV_HD
RWKV_DECAY_RANK = 64
RWKV_A_RANK = 64
RWKV_G_RANK = 128
RWKV_GN_EPS = 6.4e-4
RWKV_SIZES = (RWKV_WIDTH, RWKV_WIDTH, RWKV_WIDTH, N_DIR * RWKV_DECAY_RANK, N_DIR * RWKV_A_RANK, RWKV_G_RANK)
RWKV_IN = sum(RWKV_SIZES)
IN_SIZES = (3 * GDN_WIDTH, GDN_WIDTH, N_DIR * GDN_HEADS, N_DIR * GDN_HEADS, LRU_WIDTH, LRU_WIDTH, RWKV_IN)
D_IN = sum(IN_SIZES)

kernel_name = "hybrid_gdn_rglru_rwkv7_prefix_dit"

F32 = jnp.float32


def split_last(x, sizes):
    idx, acc = [], 0
    for s in sizes[:-1]:
        acc += s
        idx.append(acc)
    return jnp.split(x, idx, axis=-1)


def rms_norm(x, g):
    xf = x.astype(F32)
    y = xf * lax.rsqrt(jnp.mean(xf * xf, axis=-1, keepdims=True) + EPS)
    return (y * g.astype(F32)).astype(x.dtype)


def l2_normalize(x):
    xf = x.astype(F32)
    return xf * lax.rsqrt(jnp.sum(xf * xf, axis=-1, keepdims=True) + 1e-6)


def modulate(h, shift, scale):
    return h * (1.0 + scale) + shift


def centred_conv(x, w):
    pad_l = CONV_W // 2
    return lax.conv_general_dilated(
        x, w[:, None, :].astype(x.dtype), window_strides=(1,),
        padding=[(pad_l, CONV_W - 1 - pad_l)],
        dimension_numbers=("NWC", "WIO", "NWC"), feature_group_count=x.shape[-1])


def neighbour_lerp(p, mu):
    pp = jnp.pad(p, ((0, 0), (1, 1), (0, 0)))
    nb = 0.5 * (pp[:, :-2] + pp[:, 2:])
    return p + mu * (nb - p)


def flip_t(a):
    return jnp.flip(a, axis=1)


def bidirectional(run, ctx_fwd, lat_fwd, ctx_bwd, lat_bwd, s0):
    oc_f, sc_f = run(ctx_fwd, s0)
    ol_f, _ = run(lat_fwd, sc_f)
    oc_b, sc_b = run(tuple(flip_t(a) for a in ctx_bwd), s0)
    ol_b, _ = run(tuple(flip_t(a) for a in lat_bwd), sc_b)
    return oc_f + flip_t(oc_b), ol_f + flip_t(ol_b)


def gated_delta_chunked(q, k, v, beta, g, s0):
    Bs, T, H, DK = q.shape
    DV = v.shape[-1]
    C = GDN_CHUNK
    n = T // C

    def blocks(a):
        a = a.reshape((Bs, n, C, H) + a.shape[3:])
        return jnp.moveaxis(a, 2, 3).swapaxes(0, 1)

    q, k, v, beta, g = blocks(q), blocks(k), blocks(v), blocks(beta), blocks(g)
    q = q * (DK ** -0.5)
    gc = jnp.cumsum(g, axis=-1)
    i = jnp.arange(C)
    lower = i[:, None] >= i[None, :]
    strict = i[:, None] > i[None, :]
    diff = gc[..., :, None] - gc[..., None, :]
    decay = jnp.where(lower, jnp.exp(jnp.where(lower, diff, 0.0)), 0.0)
    kb = k * beta[..., None]
    L = jnp.where(strict, jnp.einsum('nbhid,nbhjd->nbhij', kb, k) * decay, 0.0)
    rhs = jnp.concatenate([v * beta[..., None], kb * jnp.exp(gc)[..., None]], axis=-1)
    sol = lax.linalg.triangular_solve(L + jnp.eye(C, dtype=L.dtype), rhs,
                                      left_side=True, lower=True, unit_diagonal=True)
    u, w = sol[..., :DV], sol[..., DV:]
    attn = jnp.einsum('nbhid,nbhjd->nbhij', q, k) * decay
    qg = q * jnp.exp(gc)[..., None]
    g_last = gc[..., -1]
    kd = k * jnp.exp(g_last[..., None] - gc)[..., None]

    def step(S, xs):
        u_c, w_c, a_c, qg_c, kd_c, gl_c = xs
        v_new = u_c - jnp.einsum('bhcd,bhde->bhce', w_c, S)
        o = jnp.einsum('bhcd,bhde->bhce', qg_c, S) + jnp.einsum('bhij,bhje->bhie', a_c, v_new)
        S = S * jnp.exp(gl_c)[..., None, None] + jnp.einsum('bhcd,bhce->bhde', kd_c, v_new)
        return S, o

    S, o = lax.scan(step, s0, (u, w, attn, qg, kd, g_last))
    o = jnp.moveaxis(o.swapaxes(0, 1), 3, 2).reshape(Bs, T, H, DV)
    return o, S


def gdn_run(inp, s0):
    return gated_delta_chunked(*inp, s0)


def gdn_prep(qkv, beta_raw, alpha_raw, conv_w, a_log, dt_bias):
    Bs, T, _ = qkv.shape
    qkv = jax.nn.silu(centred_conv(qkv, conv_w).astype(F32))
    q, k, v = jnp.split(qkv, 3, axis=-1)
    heads = lambda t: t.reshape(Bs, T, GDN_HEADS, GDN_HD)
    q, k, v = l2_normalize(heads(q)), l2_normalize(heads(k)), heads(v)
    beta = jax.nn.sigmoid(beta_raw.astype(F32)).reshape(Bs, T, N_DIR, GDN_HEADS)
    g = -jnp.exp(a_log) * jax.nn.softplus(alpha_raw.astype(F32).reshape(Bs, T, N_DIR, GDN_HEADS) + dt_bias)
    return (q, k, v, beta[:, :, 0], g[:, :, 0]), (q, k, v, beta[:, :, 1], g[:, :, 1])


def gdn_output(o, z, norm_w):
    Bs, T = z.shape[:2]
    o = o * lax.rsqrt(jnp.mean(o * o, axis=-1, keepdims=True) + EPS) * norm_w
    return (o * jax.nn.silu(z.astype(F32).reshape(Bs, T, GDN_HEADS, GDN_HD))).reshape(Bs, T, GDN_WIDTH)


def gdn_mixer(ctx_in, lat_in, conv_w, a_log, dt_bias, norm_w):
    qkv_c, z_c, b_c, al_c = ctx_in
    qkv_l, z_l, b_l, al_l = lat_in
    fc, bc = gdn_prep(qkv_c, b_c, al_c, conv_w, a_log, dt_bias)
    fl, bl = gdn_prep(qkv_l, b_l, al_l, conv_w, a_log, dt_bias)
    s0 = jnp.zeros((qkv_l.shape[0], GDN_HEADS, GDN_HD, GDN_HD), F32)
    oc, ol = bidirectional(gdn_run, fc, fl, bc, bl, s0)
    return gdn_output(oc, z_c, norm_w), gdn_output(ol, z_l, norm_w)


def to_col_major(a, rows):
    Bs, T, C = a.shape
    return a.reshape(Bs, rows, GRID_W, C).swapaxes(1, 2).reshape(Bs, T, C)


def from_col_major(a, rows):
    Bs, T, C = a.shape
    return a.reshape(Bs, GRID_W, rows, C).swapaxes(1, 2).reshape(Bs, T, C)


def block_diag(x, w):
    Bs, T, _ = x.shape
    xb = x.reshape(Bs, T, LRU_BLOCKS, LRU_BW)
    return jnp.einsum('btnk,nkj->btnj', xb, w).reshape(Bs, T, LRU_WIDTH)


def lru_gates(x, w_a, b_a, w_x, b_x, lam):
    r = jax.nn.sigmoid(block_diag(x, w_a) + b_a)
    i = jax.nn.sigmoid(block_diag(x, w_x) + b_x)
    log_a = -LRU_C * r * jax.nn.softplus(-lam)
    a = jnp.exp(log_a)
    mult = jnp.sqrt(-jnp.expm1(2.0 * log_a))
    return (a, mult * (i * x))


def lru_run(inp, h0):
    a, bx = inp

    def combine(l, r):
        return (l[0] * r[0], r[0] * l[1] + r[1])

    a_cum, h = lax.associative_scan(combine, (a, bx), axis=1)
    h = h + a_cum * h0[:, None, :]
    return h, h[:, -1]


def lru_mixer(x_c, gate_c, x_l, gate_l, rows, conv_w, conv_b, w_a, b_a, w_x, b_x, lam):
    xc = (centred_conv(x_c, conv_w) + conv_b).astype(F32)
    xl = (centred_conv(to_col_major(x_l, rows), conv_w) + conv_b).astype(F32)
    fc = lru_gates(xc, w_a[0], b_a[0], w_x[0], b_x[0], lam[0])
    bc = lru_gates(xc, w_a[1], b_a[1], w_x[1], b_x[1], lam[1])
    fl = lru_gates(xl, w_a[0], b_a[0], w_x[0], b_x[0], lam[0])
    bl = lru_gates(xl, w_a[1], b_a[1], w_x[1], b_x[1], lam[1])
    s0 = jnp.zeros((x_l.shape[0], LRU_WIDTH), F32)
    hc, hl = bidirectional(lru_run, fc, fl, bc, bl, s0)
    hl = from_col_major(hl, rows)
    gelu = lambda t: jax.nn.gelu(t.astype(F32), approximate=True)
    return hc * gelu(gate_c), hl * gelu(gate_l)


def rwkv_run(inp, s0):
    xs = tuple(jnp.swapaxes(t, 0, 1) for t in inp)

    def step(S, xt):
        w, kk, a, k, v, r = xt
        S = (S * w[:, :, None, :]
             - jnp.einsum('bhvk,bhk->bhv', S, kk)[..., None] * (kk * a)[:, :, None, :]
             + v[..., :, None] * k[:, :, None, :])
        y = jnp.einsum('bhvk,bhk->bhv', S, r)
        return S, y

    S, y = lax.scan(step, s0, xs)
    return jnp.swapaxes(y, 0, 1), S


def rwkv_prep(p, mu, w0, w_up, a0, a_up, g_up, k_k, k_a, r_k):
    Bs, T, _ = p.shape
    p = neighbour_lerp(p.astype(F32), mu)
    r, k, v, wd, ad, gd = split_last(p, RWKV_SIZES)
    heads = lambda t: t.reshape(Bs, T, RWKV_HEADS, RWKV_HD)
    dirs = lambda t: t.reshape(Bs, T, N_DIR, -1)
    logw = -jax.nn.softplus(-(w0 + jnp.einsum('btdr,drc->btdc', jnp.tanh(dirs(wd)), w_up))) - 0.5
    w = jnp.exp(-jnp.exp(logw))
    a = jax.nn.sigmoid(a0 + jnp.einsum('btdr,drc->btdc', dirs(ad), a_up))
    g = jax.nn.sigmoid(gd) @ g_up
    kk = l2_normalize(heads(k * k_k))
    k_dir = k[:, :, None, :] * (1.0 + (a - 1.0) * k_a)
    r, v = heads(r), heads(v)
    per_dir = lambda d: (heads(w[:, :, d]), kk, heads(a[:, :, d]), heads(k_dir[:, :, d]), v, r)
    bonus = jnp.sum(r[:, :, None] * k_dir.reshape(Bs, T, N_DIR, RWKV_HEADS, RWKV_HD) * r_k, axis=(2, 4))
    return per_dir(0), per_dir(1), bonus, v, g


def rwkv_output(y, bonus, v, g, gn_w, gn_b):
    Bs, T = y.shape[:2]
    mean = jnp.mean(y, axis=-1, keepdims=True)
    var = jnp.mean(jnp.square(y - mean), axis=-1, keepdims=True)
    yn = ((y - mean) * lax.rsqrt(var + RWKV_GN_EPS)).reshape(Bs, T, RWKV_WIDTH) * gn_w + gn_b
    return (yn + (bonus[..., None] * v).reshape(Bs, T, RWKV_WIDTH)) * g


def rwkv_mixer(p_c, p_l, mu, w0, w_up, a0, a_up, g_up, k_k, k_a, r_k, gn_w, gn_b):
    fc, bc, bonus_c, v_c, g_c = rwkv_prep(p_c, mu, w0, w_up, a0, a_up, g_up, k_k, k_a, r_k)
    fl, bl, bonus_l, v_l, g_l = rwkv_prep(p_l, mu, w0, w_up, a0, a_up, g_up, k_k, k_a, r_k)
    s0 = jnp.zeros((p_l.shape[0], RWKV_HEADS, RWKV_HD, RWKV_HD), F32)
    yc, yl = bidirectional(rwkv_run, fc, fl, bc, bl, s0)
    return (rwkv_output(yc, bonus_c, v_c, g_c, gn_w, gn_b),
            rwkv_output(yl, bonus_l, v_l, g_l, gn_w, gn_b))


def trunk_layer(xl, xc, mod_l, mod_c, rows, update_ctx, p):
    ml = jnp.split(mod_l[:, None, :], 6, axis=-1)
    mc = jnp.split(mod_c[None, None, :], 6, axis=-1)
    hl = modulate(rms_norm(xl, p["norm_mix_pre"]), ml[0], ml[1])
    hc = modulate(rms_norm(xc, p["norm_mix_pre"]), mc[0], mc[1])
    qkv_l, z_l, beta_l, alpha_l, lx_l, lg_l, rw_l = split_last(hl @ p["w_in"], IN_SIZES)
    qkv_c, z_c, beta_c, alpha_c, lx_c, lg_c, rw_c = split_last(hc @ p["w_in"], IN_SIZES)

    gdn_c, gdn_l = gdn_mixer((qkv_c, z_c, beta_c, alpha_c), (qkv_l, z_l, beta_l, alpha_l),
                             p["gdn_conv"], p["gdn_a_log"], p["gdn_dt_bias"], p["gdn_norm"])
    lru_c, lru_l = lru_mixer(lx_c, lg_c, lx_l, lg_l, rows, p["lru_conv"], p["lru_conv_b"],
                             p["lru_wa"], p["lru_ba"], p["lru_wx"], p["lru_bx"], p["lru_lambda"])
    rwk_c, rwk_l = rwkv_mixer(rw_c, rw_l, p["rwkv_mu"], p["rwkv_w0"], p["rwkv_w_up"], p["rwkv_a0"],
                              p["rwkv_a_up"], p["rwkv_g_up"], p["rwkv_k_k"], p["rwkv_k_a"],
                              p["rwkv_r_k"], p["rwkv_gn_w"], p["rwkv_gn_b"])

    def finish(x, parts, m):
        o = jnp.concatenate(parts, axis=-1).astype(x.dtype) @ p["w_out"]
        x = x + m[2] * rms_norm(o, p["norm_mix_post"])
        h = modulate(rms_norm(x, p["norm_ffn_pre"]), m[3], m[4])
        f = jnp.square(jax.nn.relu(h @ p["ffn_up"])) @ p["ffn_down"]
        return x + m[5] * rms_norm(f, p["norm_ffn_post"])

    xl = finish(xl, (gdn_l, lru_l, rwk_l), ml)
    if update_ctx:
        xc = finish(xc, (gdn_c, lru_c, rwk_c), mc)
    return xl, xc


def setup_inputs(seed: int = 0) -> dict:
    key = jax.random.key(seed)
    ks = iter(jax.random.split(key, 48))
    nrm = lambda shape, s: jax.random.normal(next(ks), shape, F32) * s
    uni = lambda shape, lo, hi: jax.random.uniform(next(ks), shape, F32, lo, hi)
    gain = lambda shape: 1.0 + nrm(shape, 0.02)
    L = DEPTH
    dt = jnp.exp(uni((L, N_DIR, GDN_HEADS), float(np.log(1e-3)), float(np.log(1e-1))))
    s = uni((L, N_DIR, LRU_WIDTH), 0.9, 0.999) ** (1.0 / LRU_C)
    return {
        "x": nrm((BATCH, SEQ, D_MODEL), 1.0),
        "c": nrm((BATCH, D_MODEL), 1.0),
        "ctx": nrm((BATCH, CTX_LEN, D_MODEL), 1.0),
        "c_ctx": nrm((D_MODEL,), 1.0),
        "ada_w": nrm((L, D_MODEL, 6 * D_MODEL), 0.5 * D_MODEL ** -0.5),
        "ada_b": nrm((L, 6 * D_MODEL), 0.02),
        "norm_mix_pre": gain((L, D_MODEL)),
        "norm_mix_post": gain((L, D_MODEL)),
        "norm_ffn_pre": gain((L, D_MODEL)),
        "norm_ffn_post": gain((L, D_MODEL)),
        "w_in": nrm((L, D_MODEL, D_IN), D_MODEL ** -0.5),
        "gdn_conv": nrm((L, CONV_W, 3 * GDN_WIDTH), CONV_W ** -0.5),
        "gdn_a_log": jnp.log(uni((L, N_DIR, GDN_HEADS), 1.0, 16.0)),
        "gdn_dt_bias": dt + jnp.log(-jnp.expm1(-dt)),
        "gdn_norm": gain((L, GDN_HD)),
        "lru_conv": nrm((L, CONV_W, LRU_WIDTH), CONV_W ** -0.5),
        "lru_conv_b": nrm((L, LRU_WIDTH), 0.02),
        "lru_wa": nrm((L, N_DIR, LRU_BLOCKS, LRU_BW, LRU_BW), LRU_BW ** -0.5),
        "lru_ba": nrm((L, N_DIR, LRU_WIDTH), 0.02),
        "lru_wx": nrm((L, N_DIR, LRU_BLOCKS, LRU_BW, LRU_BW), LRU_BW ** -0.5),
        "lru_bx": nrm((L, N_DIR, LRU_WIDTH), 0.02),
        "lru_lambda": jnp.log(s) - jnp.log1p(-s),
        "rwkv_mu": uni((L, RWKV_IN), 0.0, 1.0),
        "rwkv_w0": nrm((L, N_DIR, RWKV_WIDTH), 0.5),
        "rwkv_w_up": nrm((L, N_DIR, RWKV_DECAY_RANK, RWKV_WIDTH), 0.1),
        "rwkv_a0": nrm((L, N_DIR, RWKV_WIDTH), 0.5),
        "rwkv_a_up": nrm((L, N_DIR, RWKV_A_RANK, RWKV_WIDTH), 0.5 * RWKV_A_RANK ** -0.5),
        "rwkv_g_up": nrm((L, RWKV_G_RANK, RWKV_WIDTH), RWKV_G_RANK ** -0.5),
        "rwkv_k_k": 0.85 + nrm((L, RWKV_WIDTH), 0.02),
        "rwkv_k_a": gain((L, RWKV_WIDTH)),
        "rwkv_r_k": nrm((L, RWKV_HEADS, RWKV_HD), 0.1),
        "rwkv_gn_w": gain((L, RWKV_WIDTH)),
        "rwkv_gn_b": nrm((L, RWKV_WIDTH), 0.02),
        "w_out": nrm((L, D_MIX, D_MODEL), D_MIX ** -0.5),
        "ffn_up": nrm((L, D_MODEL, D_FF), D_MODEL ** -0.5),
        "ffn_down": nrm((L, D_FF, D_MODEL), D_FF ** -0.5),
    }


def reference(x, c, ctx, c_ctx, ada_w, ada_b, norm_mix_pre, norm_mix_post, norm_ffn_pre, norm_ffn_post,
              w_in, gdn_conv, gdn_a_log, gdn_dt_bias, gdn_norm, lru_conv, lru_conv_b, lru_wa, lru_ba,
              lru_wx, lru_bx, lru_lambda, rwkv_mu, rwkv_w0, rwkv_w_up, rwkv_a0, rwkv_a_up, rwkv_g_up,
              rwkv_k_k, rwkv_k_a, rwkv_r_k, rwkv_gn_w, rwkv_gn_b, w_out, ffn_up, ffn_down):
    rows = x.shape[1] // GRID_W
    silu_c = jax.nn.silu(c)
    silu_cc = jax.nn.silu(c_ctx)
    xl, xc = x, ctx
    for i in range(DEPTH):
        p = {
            "norm_mix_pre": norm_mix_pre[i], "norm_mix_post": norm_mix_post[i],
            "norm_ffn_pre": norm_ffn_pre[i], "norm_ffn_post": norm_ffn_post[i],
            "w_in": w_in[i], "gdn_conv": gdn_conv[i], "gdn_a_log": gdn_a_log[i],
            "gdn_dt_bias": gdn_dt_bias[i], "gdn_norm": gdn_norm[i],
            "lru_conv": lru_conv[i], "lru_conv_b": lru_conv_b[i], "lru_wa": lru_wa[i],
            "lru_ba": lru_ba[i], "lru_wx": lru_wx[i], "lru_bx": lru_bx[i], "lru_lambda": lru_lambda[i],
            "rwkv_mu": rwkv_mu[i], "rwkv_w0": rwkv_w0[i], "rwkv_w_up": rwkv_w_up[i],
            "rwkv_a0": rwkv_a0[i], "rwkv_a_up": rwkv_a_up[i], "rwkv_g_up": rwkv_g_up[i],
            "rwkv_k_k": rwkv_k_k[i], "rwkv_k_a": rwkv_k_a[i], "rwkv_r_k": rwkv_r_k[i],
            "rwkv_gn_w": rwkv_gn_w[i], "rwkv_gn_b": rwkv_gn_b[i],
            "w_out": w_out[i], "ffn_up": ffn_up[i], "ffn_down": ffn_down[i],
        }
        mod_l = silu_c @ ada_w[i] + ada_b[i]
        mod_c = silu_cc @ ada_w[i] + ada_b[i]
        xl, xc = trunk_layer(xl, xc, mod_l, mod_c, rows, i < DEPTH - 1, p)
    return xl
```

```python
import numpy as np
import concourse.bass as bass
import concourse.mybir as mybir
from concourse.bass_utils import run_bass_kernel_spmd

F32 = mybir.dt.float32
BF16 = mybir.dt.bfloat16
AF = mybir.ActivationFunctionType
ALU = mybir.AluOpType

SEM_CHUNK = 30000
COMPUTE = ("pe", "act", "dve", "pool")


class V:
    def __init__(self, tt, ap, box):
        self.tt, self.ap, self.box = tt, ap, box

    def w(self, fn):
        return V(self.tt, fn(self.ap), self.box)


class TT:
    def __init__(self, fw, name, handle, shape, is_dram=False, is_psum=False):
        self.fw, self.name, self.h, self.shape = fw, name, handle, list(shape)
        self.is_dram = is_dram
        self.is_psum = is_psum
        self.recs = []
        st = [1] * len(shape)
        for i in range(len(shape) - 2, 0, -1):
            st[i] = st[i + 1] * shape[i + 1]
        st[0] = 0
        self.strides = st

    def __getitem__(self, idx):
        if not isinstance(idx, tuple):
            idx = (idx,)
        idx = list(idx) + [slice(None)] * (len(self.shape) - len(idx))
        lo, hi = [], []
        for d, (i, n) in enumerate(zip(idx, self.shape)):
            if isinstance(i, int):
                a, b = i, i + 1
                idx[d] = slice(i, i + 1) if (d == 0 and not self.is_dram) else i
            else:
                a, b, s = i.indices(n)
                assert s == 1
            lo.append(a)
            hi.append(b)
        f0 = sum(lo[d] * self.strides[d] for d in range(1, len(self.shape)))
        f1 = sum((hi[d] - 1) * self.strides[d] for d in range(1, len(self.shape))) + 1
        base = self.h.ap() if self.is_dram else self.h
        ap = base[tuple(idx)]
        if self.is_psum:
            f0, f1 = 0, 1 << 30
            lo[0], hi[0] = (lo[0] // 32) * 32, ((hi[0] + 31) // 32) * 32
        return V(self, ap, (lo[0], hi[0], f0, f1))

    def full(self):
        return self[tuple(slice(None) for _ in self.shape)]


class TA_(TT):
    def __init__(self, parent, off, shape, dtype=None, pbase=0):
        self.parent = parent
        self.shape = list(shape)
        self.is_dram = False
        self.off = off
        self.pbase = pbase
        self.ratio = 2 if dtype is not None else 1
        n = 1
        for s_ in shape[1:]:
            n *= s_
        nf = (n + self.ratio - 1) // self.ratio
        flat = parent.h[pbase:pbase + shape[0], off:off + nf]
        if dtype is not None:
            flat = flat.bitcast(dtype)
        names = " ".join("d%d" % i for i in range(1, len(shape)))
        kw = {"d%d" % i: shape[i] for i in range(1, len(shape))}
        self.base = flat.rearrange("p (%s) -> p %s" % (names, names), **kw) if len(shape) > 2 else flat
        st = [1] * len(shape)
        for i in range(len(shape) - 2, 0, -1):
            st[i] = st[i + 1] * shape[i + 1]
        st[0] = 0
        self.strides = st

    @property
    def recs(self):
        return self.parent.recs

    @recs.setter
    def recs(self, v):
        self.parent.recs = v

    def __getitem__(self, idx):
        if not isinstance(idx, tuple):
            idx = (idx,)
        idx = list(idx) + [slice(None)] * (len(self.shape) - len(idx))
        lo, hi = [], []
        for d, (i, n) in enumerate(zip(idx, self.shape)):
            if isinstance(i, int):
                a, b = i, i + 1
                if d == 0:
                    idx[d] = slice(i, i + 1)
            else:
                a, b, s = i.indices(n)
                assert s == 1
            lo.append(a)
            hi.append(b)
        f0 = sum(lo[d] * self.strides[d] for d in range(1, len(self.shape)))
        f1 = sum((hi[d] - 1) * self.strides[d] for d in range(1, len(self.shape))) + 1
        ap = self.base[tuple(idx)]
        r = self.ratio
        return V(self, ap, (self.pbase + lo[0], self.pbase + hi[0], self.off + f0 // r, self.off + (f1 + r - 1) // r))


def _overlap(a, b):
    return a[0] < b[1] and b[0] < a[1] and a[2] < b[3] and b[2] < a[3]


def _covers(a, b):
    return a[0] <= b[0] and a[1] >= b[1] and a[2] <= b[2] and a[3] >= b[3]


class FW:
    def __init__(self, nc, n_dma_sems=12):
        self.nc = nc
        self.ops = {e: [] for e in ("pe", "act", "dve", "pool", "sp")}
        self.tick = {e: 0 for e in COMPUTE}
        self.waited = {e: {} for e in self.ops}
        self.sems = {}
        self.n_dma_sems = n_dma_sems
        self.dma_cnt = {}
        self.dma_uses = {}
        self.stack = None
        self.n_ops = 0
        self.out_tokens = []

    def sbuf(self, name, shape, dtype=F32):
        h = self.nc.alloc_sbuf_tensor(name, list(shape), dtype)
        return TT(self, name, h, shape)

    def psum(self, name, shape, dtype=F32):
        h = self.nc.alloc_psum_tensor(name, list(shape), dtype)
        return TT(self, name, h, shape, is_psum=True)

    def dram(self, name, shape, dtype=F32, kind="Internal"):
        h = self.nc.dram_tensor(name, list(shape), dtype, kind=kind)
        return TT(self, name, h, shape, is_dram=True)

    def _sem(self, key):
        if key not in self.sems:
            self.sems[key] = self.nc.alloc_semaphore("s_%s_%s" % key)
        return self.sems[key]

    def _token_wait(self, eng, tok, force=False):
        kind = tok[0]
        if kind == "c":
            _, pe, n = tok
            if pe == "pe" and eng == "pe" and not force:
                return []
            if self.waited[eng].get(("c", pe), 0) >= n:
                return []
            self.waited[eng][("c", pe)] = n
            return [((pe, (n - 1) // SEM_CHUNK), (n - 1) % SEM_CHUNK + 1)]
        else:
            _, q, j, m = tok
            if self.waited[eng].get(("d", q, j), 0) >= m:
                return []
            self.waited[eng][("d", q, j)] = m
            return [(("dma" + q, j), 16 * m)]

    def op(self, eng, emit, reads=(), writes=(), dma=False, force=()):
        self.n_ops += 1
        if dma:
            q = eng
            i = self.dma_cnt.get(q, 0)
            self.dma_cnt[q] = i + 1
            j = i % self.n_dma_sems
            m = self.dma_uses.get((q, j), 0) + 1
            self.dma_uses[(q, j)] = m
            token = ("d", q, j, m)
            inc = (("dma" + q, j), 16)
        else:
            self.tick[eng] += 1
            n = self.tick[eng]
            token = ("c", eng, n)
            inc = ((eng, (n - 1) // SEM_CHUNK), 1)
        deps = []
        for v in reads:
            psum = getattr(v.tt, "is_psum", False)
            for r in v.tt.recs:
                if (r[3] or (psum and r[2] != eng)) and _overlap(r[0], v.box):
                    deps.append(r[1])
        for v in writes:
            for r in v.tt.recs:
                if _overlap(r[0], v.box):
                    deps.append(r[1])
        if dma and token[3] > 1:
            deps.append(("d", token[1], token[2], token[3] - 1))
        waits = []
        for t in deps:
            waits += self._token_wait(eng, t)
        for t in force:
            waits += self._token_wait(eng, t, force=True)
        for v in writes:
            v.tt.recs = [r for r in v.tt.recs if not _covers(v.box, r[0])]
            v.tt.recs.append((v.box, token, eng, True))
        for v in reads:
            recs = v.tt.recs
            for k, r in enumerate(recs):
                if (not r[3]) and r[2] == eng and r[0] == v.box and r[1][0] == token[0]:
                    recs[k] = (v.box, token, eng, False)
                    break
            else:
                recs.append((v.box, token, eng, False))
        self.ops[eng].append((waits, emit, inc))
        return token

    def wait_tokens(self, eng, tokens):
        waits = []
        for t in tokens:
            waits += self._token_wait(eng, t)
        self.ops[eng].append((waits, None, None))

    def emit(self):
        nc = self.nc
        for e in self.ops:
            for waits, _, inc in self.ops[e]:
                for k, _v in waits:
                    self._sem(k)
                if inc is not None:
                    self._sem(inc[0])
        engmap = {"pe": "tensor", "act": "scalar", "dve": "vector", "pool": "gpsimd", "sp": "sync"}
        with nc.Block() as block:
            for e, ops in self.ops.items():
                def body(engine, ops=ops):
                    for waits, emit, inc in ops:
                        for k, val in waits:
                            engine.wait_ge(self.sems[k], val)
                        if emit is not None:
                            inst = emit(engine)
                            inst.then_inc(self.sems[inc[0]], inc[1])
                getattr(block, engmap[e])(body)

    def dma(self, out, in_, q="sp", **kw):
        return self.op(q, lambda e: e.dma_start(out=out.ap, in_=in_.ap, **kw),
                       reads=[in_], writes=[out], dma=True)

    def _pe_cfg(self, out, lhsT, kind="M"):
        cfg = (kind, lhsT.box[0], lhsT.box[1], out.box[0], out.box[1])
        tt = out.tt
        force = ()
        last = getattr(tt, "pe_last", None)
        if last is not None and last[0] != cfg:
            force = (last[1],)
        dcls = str(lhsT.ap.dtype)
        gl = getattr(self, "pe_glast", None)
        if gl is not None and gl[0] != dcls:
            force = force + (gl[1],)
        self._pe_dcls = dcls
        return cfg, force

    def mm(self, out, lhsT, rhs, start=True, stop=True):
        cfg, force = self._pe_cfg(out, lhsT)
        tok = self.op("pe", lambda e: e.matmul(out.ap, lhsT.ap, rhs.ap, start=start, stop=stop),
                      reads=[lhsT, rhs], writes=[out], force=force)
        out.tt.pe_last = (cfg, tok)
        self.pe_glast = (self._pe_dcls, tok)
        return tok

    def tr(self, out, in_, ident):
        cfg, force = self._pe_cfg(out, in_, kind="T")
        tok = self.op("pe", lambda e: e.transpose(out.ap, in_.ap, ident.ap),
                      reads=[in_, ident], writes=[out], force=force)
        out.tt.pe_last = (cfg, tok)
        self.pe_glast = (self._pe_dcls, tok)
        return tok

    def act(self, out, in_, func, bias=None, scale=None, accum=None):
        reads = [in_]
        kw = {}
        if isinstance(bias, V):
            reads.append(bias)
            kw["bias"] = bias.ap
        elif bias is not None:
            kw["bias"] = bias
        if isinstance(scale, V):
            reads.append(scale)
            kw["scale"] = scale.ap
        elif scale is not None:
            kw["scale"] = scale
        writes = [out]
        if accum is not None:
            writes.append(accum)
            kw["accum_out"] = accum.ap
        return self.op("act", lambda e: e.activation(out.ap, in_.ap, func, **kw),
                       reads=reads, writes=writes)

    def tt(self, out, a, b, op, eng="dve"):
        return self.op(eng, lambda e: e.tensor_tensor(out.ap, a.ap, b.ap, op),
                       reads=[a, b], writes=[out])

    def ts(self, out, a, s1, op0, s2=None, op1=None, eng="dve", accum=None):
        reads = [a]
        x1 = s1.ap if isinstance(s1, V) else s1
        x2 = s2.ap if isinstance(s2, V) else s2
        if isinstance(s1, V):
            reads.append(s1)
        if isinstance(s2, V):
            reads.append(s2)
        writes = [out]
        kw = {}
        if accum is not None:
            writes.append(accum)
            kw["accum_out"] = accum.ap
        if op1 is None:
            return self.op(eng, lambda e: e.tensor_scalar(out.ap, a.ap, x1, None, op0, **kw),
                           reads=reads, writes=writes)
        return self.op(eng, lambda e: e.tensor_scalar(out.ap, a.ap, x1, x2, op0, op1, **kw),
                       reads=reads, writes=writes)

    def stt(self, out, a, s, b, op0, op1, eng="dve"):
        reads = [a, b]
        x = s.ap if isinstance(s, V) else s
        if isinstance(s, V):
            reads.append(s)
        return self.op(eng, lambda e: e.scalar_tensor_tensor(out.ap, a.ap, x, b.ap, op0, op1),
                       reads=reads, writes=[out])

    def copy(self, out, in_, eng="dve"):
        if eng == "act":
            return self.op("act", lambda e: e.copy(out.ap, in_.ap), reads=[in_], writes=[out])
        return self.op(eng, lambda e: e.tensor_copy(out.ap, in_.ap), reads=[in_], writes=[out])

    def memset(self, out, val, eng="dve"):
        return self.op(eng, lambda e: e.memset(out.ap, val), writes=[out])

    def scan(self, out, d0, d1, init, op0=ALU.mult, op1=ALU.add):
        reads = [d0, d1]
        x = init.ap if isinstance(init, V) else init
        if isinstance(init, V):
            reads.append(init)
        return self.op("dve", lambda e: e.tensor_tensor_scan(out.ap, d0.ap, d1.ap, x, op0, op1),
                       reads=reads, writes=[out])

    def recip(self, out, in_):
        return self.op("dve", lambda e: e.reciprocal(out.ap, in_.ap), reads=[in_], writes=[out])

TA = 2304
CTXN = 256
NPIECE = 9
PW_ = 256
L = 2
IN_OFF = [j * 128 if j < 12 else 1560 + (j - 12) * 128 for j in range(28)]
NEG = 30000.0


def _rev(ap, n):
    return bass.AP(tensor=ap.tensor, offset=ap.offset + (n - 1), ap=[list(ap.ap[0]), [-1, n]])


PV_SPEC = [("g_pre", 8), ("g_post", 8), ("g_fpre", 8), ("g_fpost", 8), ("ada_b", 48),
           ("gconv", 36), ("gnorm", 1), ("lconv", 8), ("lconvb", 2), ("lba", 4), ("lbx", 4),
           ("llam", 4), ("mu", 12), ("w0", 6), ("a0", 6), ("kk", 3), ("ka", 3), ("rk", 3),
           ("gnw", 3), ("gnb", 3)]
PV_OFF = {}
_o = 0
for _n, _w in PV_SPEC:
    PV_OFF[_n] = (_o, _w)
    _o += _w
NPV = _o
NCST = 13


def _cst():
    r = np.arange(128)[:, None]
    c = np.arange(128)[None, :]
    UI = (r <= c).astype(np.float32)
    LI = (r >= c).astype(np.float32)
    ident = np.eye(128, dtype=np.float32)
    ones = np.ones((128, 128), np.float32)
    bo = np.zeros((128, 128), np.float32)
    bo[:64, :64] = 1
    bo[64:, 64:] = 1
    mats = [ident, ones, bo, UI, LI, NEG * UI, NEG * LI, -NEG * UI, -NEG * LI,
            -NEG * (1 - UI), -NEG * (1 - LI), 1 - UI, 1 - LI]
    return np.ascontiguousarray(np.stack(mats, axis=1))


C_ID, C_ONES, C_BO, C_UI, C_LI, C_PUI, C_PLI, C_NUI, C_NLI, C_NSL, C_NSU, C_SL, C_SU = range(13)


def _kc(w):
    K = w.shape[0]
    return np.ascontiguousarray(w.reshape(K // 128, 128, w.shape[1]).transpose(1, 0, 2))


def _colchunks(w, offs):
    K = w.shape[0]
    return np.ascontiguousarray(
        np.stack([_kc(w[:, o:o + 128]).reshape(128, (K // 128) * 128) for o in offs]))


def _pcol(v, n):
    return np.ascontiguousarray(np.asarray(v).reshape(n, 128).T)


def host_shared(inp):
    d = {}
    d["cst"] = _cst()
    d["adaw"] = np.stack([_colchunks(inp["ada_w"][i], [j * 128 for j in range(48)]) for i in range(L)]).reshape(L * 48, 128, 1024)
    d["win"] = np.stack([_colchunks(inp["w_in"][i], IN_OFF) for i in range(L)]).reshape(L * 28, 128, 1024)
    d["wba"] = np.stack([_kc(inp["w_in"][i][:, 1536:1560]).reshape(128, 8 * 24) for i in range(L)])
    d["wout"] = np.stack([_colchunks(inp["w_out"][i], [j * 128 for j in range(8)]) for i in range(L)]).reshape(L * 8, 128, 1024)
    d["wup"] = np.stack([_colchunks(inp["ffn_up"][i], [j * 128 for j in range(32)]) for i in range(L)]).reshape(L * 32, 128, 1024)
    d["wdn"] = np.stack([_colchunks(inp["ffn_down"][i], [j * 128 for j in range(8)]) for i in range(L)]).reshape(L * 8, 128, 4096)
    pv = np.zeros((L, 128, NPV), np.float32)
    rowt = np.zeros((L, 128, 24), np.float32)
    lw = np.zeros((L, 128, 8, 128), np.float32)
    rup = np.zeros((L, 128, 3, 384), np.float32)
    for i in range(L):
        def put(name, arr):
            o, w = PV_OFF[name]
            pv[i, :, o:o + w] = arr
        put("g_pre", _pcol(inp["norm_mix_pre"][i], 8))
        put("g_post", _pcol(inp["norm_mix_post"][i], 8))
        put("g_fpre", _pcol(inp["norm_ffn_pre"][i], 8))
        put("g_fpost", _pcol(inp["norm_ffn_post"][i], 8))
        put("ada_b", _pcol(inp["ada_b"][i], 48))
        gc = inp["gdn_conv"][i]
        put("gconv", np.stack([_pcol(gc[k], 9) for k in range(4)], axis=2).reshape(128, 36))
        put("gnorm", np.tile(inp["gdn_norm"][i], 2)[:, None])
        lc = inp["lru_conv"][i]
        put("lconv", np.stack([_pcol(lc[k], 2) for k in range(4)], axis=2).reshape(128, 8))
        put("lconvb", _pcol(inp["lru_conv_b"][i], 2))
        put("lba", np.concatenate([_pcol(inp["lru_ba"][i][dd], 2) for dd in range(2)], axis=1))
        put("lbx", np.concatenate([_pcol(inp["lru_bx"][i][dd], 2) for dd in range(2)], axis=1))
        put("llam", np.concatenate([_pcol(inp["lru_lambda"][i][dd], 2) for dd in range(2)], axis=1))
        put("mu", _pcol(inp["rwkv_mu"][i], 12))
        put("w0", np.concatenate([_pcol(inp["rwkv_w0"][i][dd], 3) for dd in range(2)], axis=1))
        put("a0", np.concatenate([_pcol(inp["rwkv_a0"][i][dd], 3) for dd in range(2)], axis=1))
        put("kk", _pcol(inp["rwkv_k_k"][i], 3))
        put("ka", _pcol(inp["rwkv_k_a"][i], 3))
        put("rk", _pcol(inp["rwkv_r_k"][i].reshape(-1), 3))
        put("gnw", _pcol(inp["rwkv_gn_w"][i], 3))
        put("gnb", _pcol(inp["rwkv_gn_b"][i], 3))
        rowt[i, :, 0:12] = inp["gdn_a_log"][i].reshape(1, 12)
        rowt[i, :, 12:24] = inp["gdn_dt_bias"][i].reshape(1, 12)
        for ax, nm in enumerate(("lru_wa", "lru_wx")):
            for dd in range(2):
                for jc in range(2):
                    for bl in range(2):
                        lw[i, bl * 64:(bl + 1) * 64, ax * 4 + dd * 2 + jc, bl * 64:(bl + 1) * 64] = inp[nm][i][dd, 2 * jc + bl]
        rup[i, :, 0, :] = inp["rwkv_w_up"][i].reshape(128, 384)
        rup[i, :, 1, :] = inp["rwkv_a_up"][i].reshape(128, 384)
        rup[i, :, 2, :] = inp["rwkv_g_up"][i]
    d["pv"] = pv
    d["rowt"] = rowt
    d["lw"] = lw.reshape(L, 128, 1024)
    d["rup"] = rup.reshape(L, 128, 3 * 384)
    return d


def host_core(inp, b):
    xt = np.concatenate([inp["ctx"][b], inp["x"][b]], axis=0)
    xin = np.ascontiguousarray(xt.reshape(NPIECE, PW_, 8, 128).transpose(0, 3, 2, 1)).reshape(NPIECE, 128, 8 * PW_)
    cc = np.stack([inp["c"][b], inp["c_ctx"]], axis=1)
    call = np.ascontiguousarray(cc.reshape(8, 128, 2).transpose(1, 0, 2)).reshape(128, 16)
    return {"xin": xin, "call": call}


NROW = 10
GDN_ORDER = [list(range(18)), [1, 0] + list(range(17, 1, -1))]
RWKV_ORDER = [list(range(36)), [3, 2, 1, 0] + list(range(35, 3, -1))]
TOK_TILES = [(0, 256), (256, 768), (768, 1280), (1280, 1792), (1792, 2304)]
SEGS = [(0, CTXN), (CTXN, TA)]


def _bc(view, dims):
    return view.w(lambda ap: bass.AP(tensor=ap.tensor, offset=ap.offset, ap=[list(ap.ap[0])] + [list(d) for d in dims]))


class _Stop(Exception):
    pass


class Prog:
    def stop(self, tag):
        if self.stop_tag == tag:
            raise _Stop()

    def __init__(self, nc, dbg=(), nlayers=L, stop_after=None):
        self.nc = nc
        self.fw = FW(nc)
        self.dbg = set(dbg)
        self.dbg_out = {}
        self.final = []
        self.nlayers = nlayers
        self.stop_after = stop_after
        self.stop_tag = None
        self.skip = set()
        self.alloc()

    def alloc(self):
        fw = self.fw
        EI = "ExternalInput"
        self.xin = fw.dram("xin", [NPIECE, 128, 8 * PW_], F32, EI)
        self.call = fw.dram("call", [128, 16], F32, EI)
        self.cst_d = fw.dram("cst", [128, NCST, 128], F32, EI)
        self.adaw = fw.dram("adaw", [L * 48, 128, 1024], F32, EI)
        self.win = fw.dram("win", [L * 28, 128, 1024], F32, EI)
        self.wba_d = fw.dram("wba", [L, 128, 8 * 24], F32, EI)
        self.wout = fw.dram("wout", [L * 8, 128, 1024], F32, EI)
        self.wup = fw.dram("wup", [L * 32, 128, 1024], F32, EI)
        self.wdn = fw.dram("wdn", [L * 8, 128, 4096], F32, EI)
        self.pv_d = fw.dram("pv", [L, 128, NPV], F32, EI)
        self.rowt_d = fw.dram("rowt", [L, 128, 24], F32, EI)
        self.lw_d = fw.dram("lw", [L, 128, 1024], F32, EI)
        self.rup_d = fw.dram("rup", [L, 128, 3 * 384], F32, EI)
        self.out_d = fw.dram("out", [8, 128, 8 * PW_], F32, "ExternalOutput")
        self.xr = fw.dram("xr", [NPIECE, 128, 8 * PW_], F32)
        self.od = fw.dram("od", [8, 128, TA], BF16)

        self.cst = fw.sbuf("cst_s", [128, NCST, 128], F32)
        self.H = fw.sbuf("H", [128, 8, TA], BF16)
        self.WS = fw.sbuf("WS", [128, NROW * TA], F32)
        self.pv = fw.sbuf("pv_s", [128, NPV], F32)
        self.rowt = fw.sbuf("rowt_s", [128, 24], F32)
        self.sc = fw.sbuf("sc", [128, 8, 2], F32)
        self.mod = fw.sbuf("mod", [128, 48, 2], F32)
        self.gs = fw.sbuf("gs", [128, 4, 8, 2], F32)
        self.epsc = fw.sbuf("epsc", [128, 4], F32)
        self.wst = [fw.sbuf("wst%d" % i, [128, 8, 128], F32) for i in range(3)]
        self.wst_i = 0
        self.obf = fw.sbuf("obf", [128, TA], BF16)
        self.wbf = [fw.sbuf("wbf%d" % i, [128, 8, 128], BF16) for i in range(3)]
        self.wbf_i = 0
        self.wba = fw.sbuf("wba_s", [128, 8, 24], BF16)
        self.small = fw.sbuf("small", [128, 4096], F32)
        self.misc = fw.sbuf("misc", [128, 2048], F32)
        self.gt = fw.sbuf("gt", [128, 10, 18, 12], F32)
        self.lwt = fw.sbuf("lwt", [128, 8, 128], F32)
        self.rup = fw.sbuf("rup_s", [128, 3, 384], F32)
        self.identb = fw.sbuf("identb", [128, 128], BF16)
        self.pb = [fw.psum("pb%d" % i, [128, 512], F32) for i in range(8)]

    def row(self, r):
        return self.WS[:, r * TA:(r + 1) * TA]

    def rowc(self, r, c0, c1, p0=0, p1=128):
        return self.WS[p0:p1, r * TA + c0:r * TA + c1]

    def pvc(self, name, j=0, p0=0, p1=128):
        o, w = PV_OFF[name]
        return self.pv[p0:p1, o + j:o + j + 1]

    def C(self, idx, p0=0, p1=128, c0=0, c1=128):
        return self.cst[p0:p1, idx, c0:c1]

    def dump(self, name, view, shape):
        if name not in self.dbg:
            return
        o = self.fw.dram("dbg_" + name, list(shape), F32, kind="ExternalOutput")
        self.dbg_out[name] = o
        self.final.append(self.fw.dma(o.full(), view))

    def dump_row(self, name, j, nj, view):
        if name not in self.dbg:
            return
        if name not in self.dbg_out:
            self.dbg_out[name] = self.fw.dram("dbg_" + name, [nj, 128, TA], F32, kind="ExternalOutput")
        self.final.append(self.fw.dma(self.dbg_out[name][j], view))

    def next_wbf(self):
        t = self.wbf[self.wbf_i % 3]
        self.wbf_i += 1
        return t

    def load_w(self, dst, src):
        st = self.wst[self.wst_i % 3]
        self.wst_i += 1
        self.fw.dma(st.full().w(lambda ap: ap.rearrange("p a b -> p (a b)")), src)
        self.fw.copy(dst, st.full(), eng="pool")

    def load_wc(self, dst, src, slot, first):
        r8 = lambda ap: ap.rearrange("p (a b) -> p a b", a=8)
        if first:
            self.load_w(dst, src)
            self.fw.dma(self.wcache[slot].w(r8), dst)
        else:
            self.fw.dma(dst, self.wcache[slot].w(r8))

    def flush_wc(self):
        r8 = lambda ap: ap.rearrange("p (a b) -> p a b", a=8)
        while self.pending:
            v, s = self.pending.pop(0)
            self.fw.dma(self.wcache[s].w(r8), v)

    def build(self):
        fw = self.fw
        fw.dma(self.cst.full(), self.cst_d.full())
        fw.dma(self.sc.full().w(lambda ap: ap.rearrange("p a b -> p (a b)")), self.call.full())
        fw.act(self.sc.full(), self.sc.full(), AF.Silu)
        fw.copy(self.identb.full(), self.C(C_ID))
        fw.memset(self.epsc[:, 0:1], 1e-6)
        fw.memset(self.epsc[:, 1:2], 1.0)
        fw.memset(self.epsc[:, 2:3], 6.4e-4)
        fw.memset(self.epsc[:, 3:4], 0.0)
        try:
            for layer in range(self.nlayers):
                self.layer(layer)
                if self.stop_after is not None and self.stop_after[0] == layer:
                    break
        except _Stop:
            pass
        fw.wait_tokens("sp", self.final)
        fw.emit()

    def layer(self, layer):
        fw = self.fw
        last = (layer == L - 1)
        fw.dma(self.pv.full(), self.pv_d[layer])
        fw.dma(self.rowt.full(), self.rowt_d[layer])
        fw.dma(self.lwt.full().w(lambda ap: ap.rearrange("p a b -> p (a b)")), self.lw_d[layer])
        fw.dma(self.rup.full().w(lambda ap: ap.rearrange("p a b -> p (a b)")), self.rup_d[layer])
        self.modulation(layer)
        self.phase_a(layer)
        if self.stop_after == (layer, "a"):
            return
        st = self.wst[self.wst_i % 3]
        self.wst_i += 1
        stv = st.full().w(lambda ap: ap.rearrange("p a b -> p (a b)")[:, 0:192])
        self.fw.dma(stv, self.wba_d[layer])
        self.fw.copy(self.wba.full().w(lambda ap: ap.rearrange("p a b -> p (a b)")), stv, eng="pool")
        if "gdn" not in self.skip:
            self.gdn(layer)
        if self.stop_after == (layer, "gdn"):
            return
        if "lru" not in self.skip:
            self.lru(layer)
        if self.stop_after == (layer, "lru"):
            return
        if "rwkv" not in self.skip:
            self.rwkv(layer)
        if self.stop_after == (layer, "rwkv"):
            return
        self.phase_c(layer, last)

    def modulation(self, layer):
        fw = self.fw
        mp = self.pb[7]
        modp = mp[:, 0:96].w(lambda ap: ap.rearrange("p (a b) -> p a b", b=2))
        for j in range(48):
            wt = self.wst[self.wst_i % 3]
            self.wst_i += 1
            fw.dma(wt.full().w(lambda ap: ap.rearrange("p a b -> p (a b)")), self.adaw[layer * 48 + j])
            for kc in range(8):
                fw.mm(mp[:, 2 * j:2 * j + 2], wt[:, kc, :], self.sc[:, kc, :], start=(kc == 0), stop=(kc == 7))
        o, w = PV_OFF["ada_b"]
        bb = _bc(self.pv[:, o:o + 48], [[1, 48], [0, 2]])
        fw.tt(self.mod.full(), modp, bb, ALU.add)
        self.dump("mod%d" % layer, self.mod.full(), [128, 48, 2])

        def gcol(name):
            o, w = PV_OFF[name]
            return _bc(self.pv[:, o:o + 8], [[1, 8], [0, 2]])
        fw.stt(self.gs[:, 0, :, :], self.mod[:, 8:16, :], 1.0, gcol("g_pre"), ALU.add, ALU.mult)
        fw.tt(self.gs[:, 1, :, :], self.mod[:, 16:24, :], gcol("g_post"), ALU.mult)
        fw.stt(self.gs[:, 2, :, :], self.mod[:, 32:40, :], 1.0, gcol("g_fpre"), ALU.add, ALU.mult)
        fw.tt(self.gs[:, 3, :, :], self.mod[:, 40:48, :], gcol("g_fpost"), ALU.mult)

    def rms_rstd(self, src3, sq3, n, dst, ps):
        fw = self.fw
        fw.act(sq3, src3, AF.Square)
        for kc in range(8):
            fw.mm(ps, self.C(C_ONES), self._k(sq3, kc), start=(kc == 0), stop=(kc == 7))
        fw.act(dst, ps, AF.Ln, bias=self.epsc[:, 0:1], scale=1.0 / 1024.0)
        fw.act(dst, dst, AF.Exp, scale=-0.5)

    def _k(self, v3, kc):
        return v3.w(lambda ap: ap[:, kc, :])

    def phase_a(self, layer):
        fw = self.fw
        src = self.xin if layer == 0 else self.xr
        xt = self.small[:, 0:2048].w(lambda ap: ap.rearrange("p (a b) -> p a b", a=8))
        sq = self.small[:, 2048:4096].w(lambda ap: ap.rearrange("p (a b) -> p a b", a=8))
        for pc in range(NPIECE):
            which = 1 if pc == 0 else 0
            fw.dma(xt, src[pc].w(lambda ap: ap.rearrange("p (a b) -> p a b", a=8)))
            rstd = self.misc[:, 0:PW_]
            self.rms_rstd(xt, sq, PW_, rstd, self.pb[6][:, 0:PW_])
            for kc in range(8):
                tmp = self.misc[:, (1 + kc % 2) * PW_:(2 + kc % 2) * PW_]
                fw.tt(tmp, self._k(xt, kc), rstd, ALU.mult)
                fw.ts(self.H[:, kc, pc * PW_:(pc + 1) * PW_], tmp, self.gs[:, 0, kc, which:which + 1], ALU.mult,
                      self.mod[:, kc, which:which + 1], ALU.add, eng=("pool" if kc % 2 else "dve"))
        if ("h%d" % layer) in self.dbg:
            for kc in range(8):
                fw.copy(self.row(0), self.H[:, kc, :])
                self.dump_row("h%d" % layer, kc, 8, self.row(0))

    def inproj(self, layer, j, dst_row):
        fw = self.fw
        wt = self.next_wbf()
        self.load_w(wt.full(), self.win[layer * 28 + j])
        for ti, (c0, c1) in enumerate(TOK_TILES):
            ps = self.pb[ti % 2][:, 0:c1 - c0]
            for kc in range(8):
                fw.mm(ps, wt[:, kc, :], self.H[:, kc, c0:c1], start=(kc == 0), stop=(kc == 7))
            fw.copy(self.rowc(dst_row, c0, c1), ps, eng=("act" if ti % 2 == 0 else "dve"))

    def store_o(self, ch, row_view):
        self.fw.copy(self.obf.full(), row_view, eng="pool")
        self.fw.dma(self.od[ch], self.obf.full())

    def conv4(self, dst_row, src_row, wname, j, segs=SEGS):
        fw = self.fw
        o, w = PV_OFF[wname]
        wc = lambda k: self.pv[:, o + 4 * j + k:o + 4 * j + k + 1]
        fw.ts(self.row(dst_row), self.row(src_row), wc(2), ALU.mult)
        for (a, b) in segs:
            for k, sh in ((0, -2), (1, -1), (3, 1)):
                lo = max(a, a - sh)
                hi = min(b, b - sh)
                fw.stt(self.rowc(dst_row, lo, hi), self.rowc(src_row, lo + sh, hi + sh), wc(k),
                       self.rowc(dst_row, lo, hi), ALU.mult, ALU.add)

    def headstat(self, dst_row, src_row, eps_col, scale=1.0, square=True):
        fw = self.fw
        for ti, (c0, c1) in enumerate(TOK_TILES):
            n = c1 - c0
            sq = self.small[:, (ti % 2) * 512:(ti % 2) * 512 + n]
            fw.act(sq, self.rowc(src_row, c0, c1), AF.Square)
            ps = self.pb[2 + (ti % 2)][:, 0:n]
            fw.mm(ps, self.C(C_BO), sq)
            fw.act(self.rowc(dst_row, c0, c1), ps, AF.Ln, bias=self.epsc[:, eps_col:eps_col + 1], scale=scale)
        fw.act(self.row(dst_row), self.row(dst_row), AF.Exp, scale=-0.5)

    def neumann(self, P, nb, C, sign, nlev, XY, G, bf=False):
        fw = self.fw
        PX, PY, PG = self.pb[2], self.pb[3], self.pb[4]
        pv3 = lambda pbk: pbk[0:P, 0:nb * C].w(lambda ap: ap.rearrange("p (a b) -> p a b", b=C))
        identb = _bc(self.C(C_ID, 0, P, 0, C), [[0, nb], [1, C]])
        fw.tt(G[:, 0, :, :], identb, XY[:, 0, 1, :, :], ALU.add if sign > 0 else ALU.subtract)
        for p in range(1, nlev + 1):
            s, d = (p - 1) % 2, p % 2
            for u in range(nb):
                fw.mm(PX[0:P, u * C:(u + 1) * C], XY[:, s, 1, u, :], XY[:, s, 0, u, :])
            fw.copy(XY[:, d, 0, :, :], pv3(PX), eng="dve")
            if p < nlev:
                for u in range(nb):
                    fw.mm(PY[0:P, u * C:(u + 1) * C], XY[:, s, 0, u, :], XY[:, s, 1, u, :])
                fw.copy(XY[:, d, 1, :, :], pv3(PY), eng="act")
            for u in range(nb):
                fw.mm(PG[0:P, u * C:(u + 1) * C], XY[:, d, 0, u, :], G[:, s, u, :])
            fw.tt(G[:, d, :, :], pv3(PG), G[:, s, :, :], ALU.add)
        return nlev % 2

    def gdn_tables(self, layer):
        fw = self.fw
        gt = self.gt
        pba = self.pb[7][:, 0:432].w(lambda ap: ap.rearrange("p (a b) -> p a b", b=24))
        for i in range(18):
            for kc in range(8):
                fw.mm(self.pb[7][:, i * 24:(i + 1) * 24], self.H[:, kc, i * 128:(i + 1) * 128], self.wba[:, kc, :],
                      start=(kc == 0), stop=(kc == 7))
        braw = pba.w(lambda ap: ap[:, :, 0:12])
        araw = pba.w(lambda ap: ap[:, :, 12:24])
        T_G, T_LB, T_GC, T_GCL, T_NGC, T_B, T_BG, T_KD, T_EGL, T_TMP = range(10)
        one = self.epsc[:, 1:2]
        fw.act(gt[:, T_TMP], braw, AF.Exp, scale=-1.0)
        fw.act(gt[:, T_TMP], gt[:, T_TMP], AF.Ln, bias=one)
        fw.ts(gt[:, T_LB], gt[:, T_TMP], -1.0, ALU.mult)
        fw.tt(gt[:, T_TMP], araw, _bc(self.rowt[:, 12:24], [[0, 18], [1, 12]]), ALU.add)
        fw.act(gt[:, T_TMP], gt[:, T_TMP], AF.Exp)
        fw.act(gt[:, T_TMP], gt[:, T_TMP], AF.Ln, bias=one)
        na = self.misc[:, 0:12]
        fw.act(na, self.rowt[:, 0:12], AF.Exp)
        fw.stt(gt[:, T_G], gt[:, T_TMP], -1.0, _bc(na, [[0, 18], [1, 12]]), ALU.mult, ALU.mult)
        pg = self.pb[6][:, 0:216].w(lambda ap: ap.rearrange("p (a b) -> p a b", b=12))
        pl = self.pb[6][:, 256:472].w(lambda ap: ap.rearrange("p (a b) -> p a b", b=12))
        for i in range(18):
            for d in range(2):
                fw.mm(self.pb[6][:, i * 12 + d * 6:i * 12 + d * 6 + 6], self.C(C_UI if d == 0 else C_LI),
                      gt[:, T_G, i, d * 6:d * 6 + 6])
            fw.mm(self.pb[6][:, 256 + i * 12:256 + i * 12 + 12], self.C(C_ONES), gt[:, T_G, i, :])
        fw.copy(gt[:, T_GC], pg)
        fw.tt(gt[:, T_GCL], gt[:, T_GC], gt[:, T_LB], ALU.add)
        fw.ts(gt[:, T_NGC], gt[:, T_GC], -1.0, ALU.mult)
        fw.act(gt[:, T_B], gt[:, T_LB], AF.Exp)
        fw.act(gt[:, T_BG], gt[:, T_GCL], AF.Exp)
        fw.tt(gt[:, T_KD], pl, gt[:, T_GC], ALU.subtract)
        fw.act(gt[:, T_KD], gt[:, T_KD], AF.Exp)
        fw.act(gt[:, T_EGL], pl, AF.Exp)
        if ("gdn_g%d" % layer) in self.dbg:
            self.dump("gdn_g%d" % layer, gt[:, T_G], [128, 18, 12])
            self.dump("gdn_lb%d" % layer, gt[:, T_LB], [128, 18, 12])
            self.dump("gdn_gc%d" % layer, gt[:, T_GC], [128, 18, 12])

    def gdn(self, layer):
        fw = self.fw
        self.gdn_tables(layer)
        self.stop("gdn_t")
        T_G, T_LB, T_GC, T_GCL, T_NGC, T_B, T_BG, T_KD, T_EGL, T_TMP = range(10)
        gt = self.gt
        RQ, RK, RV, RT, RO0, RO1 = 0, 1, 2, 3, 4, 5
        base = 6 * TA
        WSt = self.WS
        gb = TA_(WSt, base, [128, 4, 128])
        lbb = TA_(WSt, base + 512, [128, 4, 128])
        dec = TA_(WSt, base + 1024, [128, 3, 4, 128])
        XY = TA_(WSt, base + 2560, [128, 2, 2, 4, 128])
        G = TA_(WSt, base + 4608, [128, 2, 4, 128])
        AT = TA_(WSt, base + 5632, [128, 2, 4, 128])
        TOK = TA_(WSt, base + 6656, [128, 2, 3, 4, 64])
        TOKB = TOK
        U = TA_(WSt, base + 8192, [128, 2, 4, 64])
        WT = TA_(WSt, base + 8704, [128, 2, 2, 128])
        QG = TA_(self.small, 0, [128, 2, 2, 128])
        EG = TA_(self.small, 512, [128, 4, 128])
        S = TA_(self.small, 1024, [128, 64])
        VN = TA_(self.small, 1088, [128, 2, 128])
        pb = self.pb
        for hp in range(3):
            for (j, dst) in ((hp, RQ), (3 + hp, RK), (6 + hp, RV)):
                self.inproj(layer, j, RT)
                self.conv4(dst, RT, "gconv", j)
                fw.act(self.row(dst), self.row(dst), AF.Silu)
            self.headstat(RT, RQ, 0)
            fw.stt(self.row(RQ), self.row(RQ), 0.125, self.row(RT), ALU.mult, ALU.mult)
            self.headstat(RT, RK, 0)
            fw.tt(self.row(RK), self.row(RK), self.row(RT), ALU.mult)
            if layer == 0 or True:
                self.dump_row("gdn_q%d" % layer, hp, 3, self.row(RQ))
                self.dump_row("gdn_k%d" % layer, hp, 3, self.row(RK))
                self.dump_row("gdn_v%d" % layer, hp, 3, self.row(RV))
            self.stop("gdn_p")
            for d in range(2):
                RO = RO0 + d
                fw.memset(S.full(), 0.0)
                order = GDN_ORDER[d]
                mincl = C_UI if d == 0 else C_LI
                m1 = C_PUI if d == 0 else C_PLI
                m2 = C_NLI if d == 0 else C_NUI
                m3 = C_NSL if d == 0 else C_NSU
                for b in range(9):
                    rr = b % 2
                    tiles = (order[2 * b], order[2 * b + 1])
                    units = [(hh, ts) for hh in range(2) for ts in range(2)]

                    def cidx(hh):
                        return d * 6 + 2 * hp + hh
                    for hh in range(2):
                        for ts in range(2):
                            i = tiles[ts]
                            u = hh * 2 + ts
                            ci = cidx(hh)
                            fw.copy(gb[:, u, :], _bc(gt[:, T_G, i, ci:ci + 1], [[0, 128]]), eng="pool")
                            fw.copy(lbb[:, u, :], _bc(gt[:, T_LB, i, ci:ci + 1], [[0, 128]]), eng="pool")
                    for u, (hh, ts) in enumerate(units):
                        sl = slice(u * 128, (u + 1) * 128)
                        fw.mm(pb[2][:, sl], gb[:, u, :], self.C(mincl), start=True, stop=False)
                        fw.mm(pb[2][:, sl], self.C(C_ID), self.C(m1), start=False, stop=True)
                        fw.mm(pb[3][:, sl], gb[:, u, :], self.C(mincl), start=True, stop=False)
                        fw.mm(pb[3][:, sl], lbb[:, u, :], self.C(C_ID), start=False, stop=False)
                        fw.mm(pb[3][:, sl], self.C(C_ID), self.C(m2), start=False, stop=True)
                        fw.mm(pb[4][:, sl], gb[:, u, :], self.C(mincl), start=True, stop=False)
                        fw.mm(pb[4][:, sl], self.C(C_ID), self.C(m3), start=False, stop=True)
                        fw.mm(pb[5][:, sl], gb[:, u, :], self.C(mincl), start=True, stop=True)
                    for u, (hh, ts) in enumerate(units):
                        i = tiles[ts]
                        sl = slice(u * 128, (u + 1) * 128)
                        ci = cidx(hh)
                        fw.act(dec[:, 0, u, :], pb[2][:, sl], AF.Exp, bias=gt[:, T_GCL, i, ci:ci + 1], scale=-1.0)
                        fw.act(dec[:, 1, u, :], pb[3][:, sl], AF.Exp, bias=gt[:, T_NGC, i, ci:ci + 1], scale=1.0)
                        fw.act(dec[:, 2, u, :], pb[4][:, sl], AF.Exp, bias=gt[:, T_NGC, i, ci:ci + 1], scale=1.0)
                    fw.act(EG.full(), pb[5][:, 0:512].w(lambda ap: ap.rearrange("p (a b) -> p a b", b=128)), AF.Exp)
                    for u, (hh, ts) in enumerate(units):
                        i = tiles[ts]
                        hb = hh * 64
                        cs = (i * 128, (i + 1) * 128)
                        kT = self.rowc(RK, cs[0], cs[1], hb, hb + 64)
                        qT = self.rowc(RQ, cs[0], cs[1], hb, hb + 64)
                        vT = self.rowc(RV, cs[0], cs[1], hb, hb + 64)
                        idn = self.C(C_ID, hb, hb + 64, hb, hb + 64)
                        fw.mm(pb[6 + hh][:, ts * 128:(ts + 1) * 128], kT, kT)
                        fw.mm(pb[6 + hh][:, 256 + ts * 128:256 + (ts + 1) * 128], kT, qT)
                        fw.tr(pb[hh][:, ts * 64:(ts + 1) * 64], kT, idn)
                        fw.tr(pb[hh][:, 128 + ts * 64:128 + (ts + 1) * 64], vT, idn)
                    for hh in range(2):
                        kk3 = pb[6 + hh][:, 0:256].w(lambda ap: ap.rearrange("p (a b) -> p a b", b=128))
                        kq3 = pb[6 + hh][:, 256:512].w(lambda ap: ap.rearrange("p (a b) -> p a b", b=128))
                        us = slice(hh * 2, hh * 2 + 2)
                        fw.tt(XY[:, 0, 0, us, :], kk3, dec[:, 0, us, :], ALU.mult)
                        fw.tt(XY[:, 0, 1, us, :], kk3, dec[:, 1, us, :], ALU.mult)
                        fw.tt(AT[:, rr, us, :], kq3, dec[:, 2, us, :], ALU.mult)
                    for u, (hh, ts) in enumerate(units):
                        i = tiles[ts]
                        ci = cidx(hh)
                        ktk = pb[hh][:, ts * 64:(ts + 1) * 64]
                        vtk = pb[hh][:, 128 + ts * 64:128 + (ts + 1) * 64]
                        fw.ts(TOKB[:, rr, 0, u, :], vtk, gt[:, T_B, i, ci:ci + 1], ALU.mult)
                        fw.ts(TOKB[:, rr, 1, u, :], ktk, gt[:, T_BG, i, ci:ci + 1], ALU.mult)
                        fw.ts(TOK[:, rr, 2, u, :], ktk, gt[:, T_KD, i, ci:ci + 1], ALU.mult)
                    for u, (hh, ts) in enumerate(units):
                        i = tiles[ts]
                        hb = hh * 64
                        fw.tt(QG[hb:hb + 64, rr, ts, :], self.rowc(RQ, i * 128, (i + 1) * 128, hb, hb + 64),
                              EG[hb:hb + 64, u, :], ALU.mult, eng="pool")
                    gi = self.neumann(128, 4, 128, -1, 6, XY, G)
                    for u, (hh, ts) in enumerate(units):
                        hb = hh * 64
                        fw.mm(pb[5][:, u * 64:(u + 1) * 64], G[:, gi, u, :], TOKB[:, rr, 0, u, :])
                        fw.mm(pb[6][hb:hb + 64, ts * 128:(ts + 1) * 128], TOKB[:, rr, 1, u, :], G[:, gi, u, :])
                    fw.copy(U[:, rr, :, :], pb[5][:, 0:256].w(lambda ap: ap.rearrange("p (a b) -> p a b", b=64)), eng="act")
                    fw.copy(WT[:, rr, :, :], pb[6][:, 0:256].w(lambda ap: ap.rearrange("p (a b) -> p a b", b=128)), eng="dve")
                    self.stop("gdn_b0")
                    for ts in range(2):
                        i = tiles[ts]
                        vr = ts
                        for hh in range(2):
                            hb = hh * 64
                            fw.mm(pb[hh][:, 0:64], WT[hb:hb + 64, rr, ts, :], S[hb:hb + 64, :])
                        for hh in range(2):
                            fw.tt(VN[:, vr, hh * 64:(hh + 1) * 64], U[:, rr, hh * 2 + ts, :], pb[hh][:, 0:64], ALU.subtract)
                        self.stop("gdn_ra")
                        for hh in range(2):
                            hb = hh * 64
                            u = hh * 2 + ts
                            fw.mm(pb[2][hb:hb + 64, 0:128], S[hb:hb + 64, :], QG[hb:hb + 64, rr, ts, :])
                            fw.mm(pb[3][hb:hb + 64, 0:128], VN[:, vr, hh * 64:(hh + 1) * 64], AT[:, rr, u, :])
                        fw.copy(self.rowc(RO, i * 128, (i + 1) * 128), pb[2][:, 0:128], eng="act")
                        fw.tt(self.rowc(RO, i * 128, (i + 1) * 128), self.rowc(RO, i * 128, (i + 1) * 128), pb[3][:, 0:128], ALU.add)
                        self.stop("gdn_rb")
                        for hh in range(2):
                            hb = hh * 64
                            u = hh * 2 + ts
                            fw.mm(pb[4][hb:hb + 64, 0:64], TOK[:, rr, 2, u, :], VN[:, vr, hh * 64:(hh + 1) * 64])
                        self.stop("gdn_rc")
                        for hh in range(2):
                            hb = hh * 64
                            ci = cidx(hh)
                            fw.stt(S[hb:hb + 64, :], S[hb:hb + 64, :], gt[hb:hb + 64, T_EGL, i, ci:ci + 1],
                                   pb[4][hb:hb + 64, 0:64], ALU.mult, ALU.add)
                    self.stop("gdn_r0")
                self.stop("gdn_d0")
            fw.tt(self.row(RO0), self.row(RO0), self.row(RO1), ALU.add)
            self.dump_row("gdn_o%d" % layer, hp, 3, self.row(RO0))
            self.headstat(RT, RO0, 0, scale=1.0 / 64.0)
            fw.stt(self.row(RO0), self.row(RO0), self.pvc("gnorm"), self.row(RT), ALU.mult, ALU.mult)
            self.inproj(layer, 9 + hp, RQ)
            fw.act(self.row(RQ), self.row(RQ), AF.Silu)
            fw.tt(self.row(RO0), self.row(RO0), self.row(RQ), ALU.mult)
            self.dump_row("mix%d" % layer, hp, 8, self.row(RO0))
            self.store_o(hp, self.row(RO0))

    def lru(self, layer):
        fw = self.fw
        RT, RP, RX, RA, RB, RH, RG, RI = 0, 1, 2, 3, 4, 5, 6, 7
        o, w = PV_OFF["llam"]
        c8 = self.misc[:, 1024:1028]
        c16 = self.misc[:, 1028:1032]
        fw.act(c8, self.pv[:, o:o + 4], AF.Exp, scale=-1.0)
        fw.act(c8, c8, AF.Ln, bias=self.epsc[:, 1:2])
        fw.ts(c16, c8, -16.0, ALU.mult)
        fw.ts(c8, c8, -8.0, ALU.mult)
        NL = TA - CTXN
        for jc in range(2):
            self.inproj(layer, 12 + jc, RT)
            fw.copy(self.rowc(RP, 0, CTXN), self.rowc(RT, 0, CTXN), eng="pool")
            fw.copy(self.rowc(RP, CTXN, TA), _bc(self.rowc(RT, CTXN, TA), [[1, 64], [64, 32]]), eng="pool")
            self.conv4(RX, RP, "lconv", jc)
            fw.ts(self.row(RX), self.row(RX), self.pvc("lconvb", jc), ALU.add)
            for d in range(2):
                for ti, (c0, c1) in enumerate(TOK_TILES):
                    n = c1 - c0
                    pa = self.pb[2 + ti % 2][:, 0:n]
                    px = self.pb[4 + ti % 2][:, 0:n]
                    fw.mm(pa, self.lwt[:, 0 * 4 + d * 2 + jc, :], self.rowc(RX, c0, c1))
                    fw.mm(px, self.lwt[:, 1 * 4 + d * 2 + jc, :], self.rowc(RX, c0, c1))
                    fw.act(self.rowc(RA, c0, c1), pa, AF.Sigmoid, bias=self.pvc("lba", d * 2 + jc))
                    fw.act(self.rowc(RI, c0, c1), px, AF.Sigmoid, bias=self.pvc("lbx", d * 2 + jc))
                k = d * 2 + jc
                fw.act(self.row(RB), self.row(RA), AF.Exp, scale=self.misc[:, 1028 + k:1029 + k])
                fw.ts(self.row(RB), self.row(RB), -1.0, ALU.mult, 1.0, ALU.add)
                fw.act(self.row(RB), self.row(RB), AF.Sqrt)
                fw.act(self.row(RA), self.row(RA), AF.Exp, scale=self.misc[:, 1024 + k:1025 + k])
                fw.tt(self.row(RI), self.row(RI), self.row(RX), ALU.mult, eng="pool")
                fw.tt(self.row(RB), self.row(RB), self.row(RI), ALU.mult)
                dst = RH if d == 0 else RG
                if d == 0:
                    fw.scan(self.row(dst), self.row(RA), self.row(RB), 0.0)
                else:
                    fw.scan(self.rowc(dst, 0, CTXN).w(lambda ap: _rev(ap, CTXN)),
                            self.rowc(RA, 0, CTXN).w(lambda ap: _rev(ap, CTXN)),
                            self.rowc(RB, 0, CTXN).w(lambda ap: _rev(ap, CTXN)), 0.0)
                    fw.scan(self.rowc(dst, CTXN, TA).w(lambda ap: _rev(ap, NL)),
                            self.rowc(RA, CTXN, TA).w(lambda ap: _rev(ap, NL)),
                            self.rowc(RB, CTXN, TA).w(lambda ap: _rev(ap, NL)), self.rowc(dst, 0, 1))
            fw.tt(self.row(RH), self.row(RH), self.row(RG), ALU.add)
            fw.copy(self.rowc(RP, 0, CTXN), self.rowc(RH, 0, CTXN), eng="pool")
            fw.copy(self.rowc(RP, CTXN, TA), _bc(self.rowc(RH, CTXN, TA), [[1, 32], [32, 64]]), eng="pool")
            self.inproj(layer, 14 + jc, RG)
            fw.act(self.row(RT), self.row(RG), AF.Square)
            fw.ts(self.row(RT), self.row(RT), 0.044715, ALU.mult, 1.0, ALU.add)
            fw.tt(self.row(RT), self.row(RT), self.row(RG), ALU.mult)
            fw.act(self.row(RT), self.row(RT), AF.Sigmoid, scale=1.5957691216057308)
            fw.tt(self.row(RT), self.row(RT), self.row(RG), ALU.mult)
            fw.tt(self.row(RP), self.row(RP), self.row(RT), ALU.mult)
            self.dump_row("mix%d" % layer, 3 + jc, 8, self.row(RP))
            self.store_o(3 + jc, self.row(RP))

    def phase_c(self, layer, last):
        fw = self.fw
        WSt = self.WS
        NS = 768
        hid = TA_(WSt, 0, [128, 32, NS], dtype=BF16)
        Y = TA_(WSt, 12288, [128, 8, NS])
        wdnb = [TA_(WSt, 18432 + 2048 * i, [128, 32, 128], dtype=BF16) for i in range(2)]
        xt = self.small[:, 0:2048].w(lambda ap: ap.rearrange("p (a b) -> p a b", a=8))
        sq = self.small[:, 2048:4096].w(lambda ap: ap.rearrange("p (a b) -> p a b", a=8))
        src = self.xin if layer == 0 else self.xr
        self.wcache = fw.dram("wcache%d" % layer, [72, 128, 1024], BF16)
        self.pending = []
        for st in range(3):
            first = (st == 0)
            pcs = [pc for pc in range(3 * st, 3 * st + 3) if not (last and pc == 0)]
            base = pcs[0] * PW_
            ncol = len(pcs) * PW_
            nt = []
            c = pcs[0] * PW_
            end = (pcs[-1] + 1) * PW_
            while c < end:
                n = min(512, end - c)
                if c < CTXN:
                    n = min(n, CTXN - c)
                nt.append((c - base, c - base + n))
                c += n
            Ot = self.H[:, :, 0:ncol]
            fw.dma(Ot, self.od[:, :, base:base + ncol].w(lambda ap: ap.rearrange("k p t -> p k t")))
            for m in range(8):
                wt = self.next_wbf()
                self.load_wc(wt.full(), self.wout[layer * 8 + m], m, first)
                for ti, (a, b) in enumerate(nt):
                    ps = self.pb[ti % 2][:, 0:b - a]
                    for kc in range(8):
                        fw.mm(ps, wt[:, kc, :], self.H[:, kc, a:b], start=(kc == 0), stop=(kc == 7))
                    fw.copy(Y[:, m, a:b], ps, eng=("act" if ti % 2 == 0 else "dve"))
            for pi, pc in enumerate(pcs):
                which = 1 if pc == 0 else 0
                lo = pi * PW_
                fw.dma(xt, src[pc].w(lambda ap: ap.rearrange("p (a b) -> p a b", a=8)))
                rstd = self.misc[:, 0:PW_]
                self.rms_rstd(Y[:, :, lo:lo + PW_], sq, PW_, rstd, self.pb[6][:, 0:PW_])
                for kc in range(8):
                    tmp = self.misc[:, (1 + kc % 2) * PW_:(2 + kc % 2) * PW_]
                    fw.tt(tmp, Y[:, kc, lo:lo + PW_], rstd, ALU.mult)
                    fw.stt(self._k(xt, kc), tmp, self.gs[:, 1, kc, which:which + 1], self._k(xt, kc), ALU.mult, ALU.add)
                if ("xmid%d" % layer) in self.dbg:
                    if ("xmid%d" % layer) not in self.dbg_out:
                        self.dbg_out["xmid%d" % layer] = fw.dram("dbg_xmid%d" % layer, [NPIECE, 128, 8 * PW_], F32, kind="ExternalOutput")
                    self.final.append(fw.dma(self.dbg_out["xmid%d" % layer][pc].w(lambda ap: ap.rearrange("p (a b) -> p a b", a=8)), xt))
                fw.dma(self.xr[pc].w(lambda ap: ap.rearrange("p (a b) -> p a b", a=8)), xt)
                rstd2 = self.misc[:, 3 * PW_:4 * PW_]
                self.rms_rstd(xt, sq, PW_, rstd2, self.pb[6][:, 0:PW_])
                for kc in range(8):
                    tmp = self.misc[:, (1 + kc % 2) * PW_:(2 + kc % 2) * PW_]
                    fw.tt(tmp, self._k(xt, kc), rstd2, ALU.mult)
                    fw.ts(self.H[:, kc, NS + lo:NS + lo + PW_], tmp, self.gs[:, 2, kc, which:which + 1], ALU.mult,
                          self.mod[:, 24 + kc, which:which + 1], ALU.add, eng=("pool" if kc % 2 else "dve"))
            for m in range(32):
                wt = self.next_wbf()
                self.load_wc(wt.full(), self.wup[layer * 32 + m], 8 + m, first)
                for ti, (a, b) in enumerate(nt):
                    ps = self.pb[ti % 2][:, 0:b - a]
                    for kc in range(8):
                        fw.mm(ps, wt[:, kc, :], self.H[:, kc, NS + a:NS + b], start=(kc == 0), stop=(kc == 7))
                    rl = self.small[:, 0:b - a] if ti % 2 == 0 else self.small[:, 512:512 + b - a]
                    fw.act(rl, ps, AF.Relu)
                    fw.tt(hid[:, m, a:b], rl, rl, ALU.mult, eng=("pool" if ti % 2 == 0 else "dve"))
            for m in range(8):
                wt = wdnb[m % 2]
                for kq in range(4):
                    self.load_wc(wt[:, 8 * kq:8 * kq + 8, :], self.wdn[layer * 8 + m, :, 1024 * kq:1024 * kq + 1024], 40 + 4 * m + kq, first)
                for ti, (a, b) in enumerate(nt):
                    ps = self.pb[ti % 2][:, 0:b - a]
                    for kc in range(32):
                        fw.mm(ps, wt[:, kc, :], hid[:, kc, a:b], start=(kc == 0), stop=(kc == 31))
                    fw.copy(Y[:, m, a:b], ps, eng=("act" if ti % 2 == 0 else "dve"))
            self.flush_wc()
            for pi, pc in enumerate(pcs):
                which = 1 if pc == 0 else 0
                lo = pi * PW_
                fw.dma(xt, self.xr[pc].w(lambda ap: ap.rearrange("p (a b) -> p a b", a=8)))
                rstd = self.misc[:, 0:PW_]
                self.rms_rstd(Y[:, :, lo:lo + PW_], sq, PW_, rstd, self.pb[6][:, 0:PW_])
                for kc in range(8):
                    tmp = self.misc[:, (1 + kc % 2) * PW_:(2 + kc % 2) * PW_]
                    fw.tt(tmp, Y[:, kc, lo:lo + PW_], rstd, ALU.mult)
                    fw.stt(self._k(xt, kc), tmp, self.gs[:, 3, kc, which:which + 1], self._k(xt, kc), ALU.mult, ALU.add)
                if last:
                    self.final.append(fw.dma(self.out_d[pc - 1].w(lambda ap: ap.rearrange("p (a b) -> p a b", a=8)), xt))
                else:
                    fw.dma(self.xr[pc].w(lambda ap: ap.rearrange("p (a b) -> p a b", a=8)), xt)

    def lerp(self, dst_row, raw_row, muidx):
        fw = self.fw
        for (a, b) in SEGS:
            fw.tt(self.rowc(dst_row, a + 1, b - 1), self.rowc(raw_row, a, b - 2), self.rowc(raw_row, a + 2, b), ALU.add)
            fw.copy(self.rowc(dst_row, a, a + 1), self.rowc(raw_row, a + 1, a + 2), eng="pool")
            fw.copy(self.rowc(dst_row, b - 1, b), self.rowc(raw_row, b - 2, b - 1), eng="pool")
        fw.ts(self.row(dst_row), self.row(dst_row), self.misc[:, 1040 + 12 + muidx:1040 + 13 + muidx], ALU.mult)
        fw.stt(self.row(dst_row), self.row(raw_row), self.misc[:, 1040 + muidx:1040 + muidx + 1], self.row(dst_row), ALU.mult, ALU.add)

    def rwkv(self, layer):
        fw = self.fw
        pb = self.pb
        RR, RV, RKK, RKA, RCW, RKD, RY, RBON, T1, T2 = range(10)
        o, w = PV_OFF["mu"]
        fw.ts(self.misc[:, 1040 + 0:1040 + 12], self.pv[:, o:o + 12], -1.0, ALU.mult, 1.0, ALU.add)
        fw.ts(self.misc[:, 1040 + 12:1040 + 24], self.pv[:, o:o + 12], 0.5, ALU.mult)
        o2, w2 = PV_OFF["ka"]
        fw.ts(self.misc[:, 1040 + 24:1040 + 27], self.pv[:, o2:o2 + 3], -1.0, ALU.mult, 1.0, ALU.add)
        spill = self.fw.dram("spill%d" % layer, [4, 128, TA], F32)
        WSt = self.WS
        bT = T1 * TA
        DER = TA_(WSt, bT, [128, 6, 4, 64])
        E1 = TA_(WSt, bT + 1536, [128, 4, 64])
        E2 = TA_(WSt, bT + 1792, [128, 4, 64])
        XY = TA_(WSt, bT + 2048, [64, 2, 2, 8, 64], dtype=BF16)
        ATKB = TA_(WSt, bT + 3072, [64, 8, 64], dtype=BF16)
        W1B = TA_(WSt, bT + 3328, [64, 8, 64], dtype=BF16)
        PC = TA_(WSt, bT + 4096, [128, 4])
        M = TA_(WSt, bT + 4104, [128, 64])
        US = TA_(WSt, bT + 4168, [64, 128])
        AHT = TA_(WSt, bT + 4296, [128, 4, 64])
        G = TA_(self.small, 0, [64, 2, 8, 64], dtype=BF16)
        MK = TA_(self.small, 1024, [64, 3, 8, 64])
        TK = TA_(self.small, 2560, [64, 3, 8, 64])
        VT = TA_(self.misc, 0, [64, 8, 64])
        W1 = TA_(self.misc, 512, [64, 8, 64])
        UV = TA_(self.misc, 1200, [64, 8, 64])
        p3 = lambda pbk: pbk[0:64, 0:512].w(lambda ap: ap.rearrange("p (a b) -> p a b", b=64))
        self.inproj(layer, 26, T1)
        self.lerp(T2, T1, 10)
        fw.dma(spill[1], self.row(T2))
        self.inproj(layer, 25, T1)
        self.lerp(T2, T1, 9)
        fw.act(self.row(T2), self.row(T2), AF.Tanh)
        fw.dma(spill[2], self.row(T2))
        self.inproj(layer, 27, T1)
        self.lerp(T2, T1, 11)
        fw.act(self.row(T2), self.row(T2), AF.Sigmoid)
        fw.dma(spill[3], self.row(T2))
        for hp in range(3):
            self.inproj(layer, 16 + hp, T1)
            self.lerp(RR, T1, hp)
            self.inproj(layer, 22 + hp, T1)
            self.lerp(RV, T1, 6 + hp)
            self.inproj(layer, 19 + hp, T1)
            self.lerp(T2, T1, 3 + hp)
            fw.dma(spill[0], self.row(T2))
            fw.ts(self.row(T1), self.row(T2), self.pvc("kk", hp), ALU.mult)
            self.headstat(RKK, T1, 0)
            fw.tt(self.row(RKK), self.row(RKK), self.row(T1), ALU.mult)
            for d in range(2):
                db = d * 64
                fw.dma(self.row(T2), spill[1])
                for ti, (c0, c1) in enumerate(TOK_TILES):
                    ps = pb[2 + ti % 2][:, 0:c1 - c0]
                    fw.mm(ps, self.rup[db:db + 64, 1, hp * 128:(hp + 1) * 128], self.rowc(T2, c0, c1, db, db + 64))
                    fw.act(self.rowc(RKA, c0, c1), ps, AF.Sigmoid, bias=self.pvc("a0", d * 3 + hp))
                fw.dma(self.row(T2), spill[0])
                fw.ts(self.row(T1), self.row(RKA), self.pvc("ka", hp), ALU.mult, self.misc[:, 1040 + 24 + hp:1040 + 25 + hp], ALU.add)
                fw.tt(self.row(RKD), self.row(T2), self.row(T1), ALU.mult)
                fw.tt(self.row(RKA), self.row(RKA), self.row(RKK), ALU.mult)
                if d == 0:
                    fw.stt(self.row(RBON), self.row(RR), self.pvc("rk", hp), self.row(RKD), ALU.mult, ALU.mult)
                else:
                    fw.stt(self.row(T1), self.row(RR), self.pvc("rk", hp), self.row(RKD), ALU.mult, ALU.mult)
                    fw.tt(self.row(RBON), self.row(RBON), self.row(T1), ALU.add)
                fw.dma(self.row(T2), spill[2])
                for ti, (c0, c1) in enumerate(TOK_TILES):
                    ps = pb[2 + ti % 2][:, 0:c1 - c0]
                    fw.mm(ps, self.rup[db:db + 64, 0, hp * 128:(hp + 1) * 128], self.rowc(T2, c0, c1, db, db + 64))
                    fw.act(self.rowc(T1, c0, c1), ps, AF.Sigmoid, bias=self.pvc("w0", d * 3 + hp))
                fw.ts(self.row(T1), self.row(T1), -0.6065306597126334, ALU.mult)
                onesb = _bc(self.epsc[:, 1:2], [[0, TA]])
                r3 = lambda ap: ap.rearrange("p (a b) -> p a b", b=64)
                if d == 0:
                    fw.scan(self.row(T2), onesb, self.row(T1), 0.0)
                    fw.tt(self.rowc(RCW, 64, TA).w(r3), self.rowc(T2, 64, TA).w(r3),
                          _bc(self.rowc(T2, 63, 64), [[64, 35], [0, 64]]), ALU.subtract)
                    fw.copy(self.rowc(RCW, 0, 64), self.rowc(T2, 0, 64), eng="pool")
                else:
                    fw.scan(self.row(T2).w(lambda ap: _rev(ap, TA)), onesb, self.row(T1).w(lambda ap: _rev(ap, TA)), 0.0)
                    fw.tt(self.rowc(RCW, 0, TA - 64).w(r3), self.rowc(T2, 0, TA - 64).w(r3),
                          _bc(self.rowc(T2, 64, 65), [[64, 35], [0, 64]]), ALU.subtract)
                    fw.copy(self.rowc(RCW, TA - 64, TA), self.rowc(T2, TA - 64, TA), eng="pool")
                if d == 0 and hp == 0:
                    self.dump_row("rw_cw%d" % layer, 0, 1, self.row(RCW))
                    self.dump_row("rw_kd%d" % layer, 0, 1, self.row(RKD))
                    self.dump_row("rw_ka%d" % layer, 0, 1, self.row(RKA))
                fw.memset(M.full(), 0.0)
                sgn_strict_ts = C_SL if d == 0 else C_SU
                sgn_strict_st = C_SU if d == 0 else C_SL
                incl_st = C_UI if d == 0 else C_LI
                for b in range(9):
                    if d == 0:
                        w0c = 256 * b
                    else:
                        w0c = 0 if b == 0 else TA - 256 * b
                    win = lambda r: self.rowc(r, w0c, w0c + 256).w(lambda ap: ap.rearrange("p (a b) -> p a b", b=64))
                    lastp = 63 if d == 0 else 0
                    cwl = _bc(self.rowc(RCW, w0c + lastp, w0c + lastp + 1), [[64, 4], [0, 64]])
                    fw.act(E1.full(), win(RCW), AF.Exp)
                    fw.act(E2.full(), win(RCW), AF.Exp, scale=-1.0)
                    fw.tt(DER[:, 3], win(RR), E1.full(), ALU.mult)
                    fw.stt(DER[:, 1], win(RKA), -1.0, E2.full(), ALU.mult, ALU.mult)
                    fw.tt(DER[:, 2], win(RKD), E2.full(), ALU.mult, eng="pool")
                    if d == 0:
                        fw.tt(DER[:, 0, :, 1:64], self.rowc(RKK, w0c, w0c + 256).w(lambda ap: ap.rearrange("p (a b) -> p a b", b=64)[:, :, 1:64]),
                              E1[:, :, 0:63], ALU.mult)
                        fw.copy(DER[:, 0, :, 0:1], self.rowc(RKK, w0c, w0c + 256).w(lambda ap: ap.rearrange("p (a b) -> p a b", b=64)[:, :, 0:1]), eng="pool")
                    else:
                        fw.tt(DER[:, 0, :, 0:63], self.rowc(RKK, w0c, w0c + 256).w(lambda ap: ap.rearrange("p (a b) -> p a b", b=64)[:, :, 0:63]),
                              E1[:, :, 1:64], ALU.mult)
                        fw.copy(DER[:, 0, :, 63:64], self.rowc(RKK, w0c, w0c + 256).w(lambda ap: ap.rearrange("p (a b) -> p a b", b=64)[:, :, 63:64]), eng="pool")
                    fw.tt(E2.full(), cwl, win(RCW), ALU.subtract)
                    fw.act(E2.full(), E2.full(), AF.Exp)
                    fw.tt(DER[:, 4], win(RKD), E2.full(), ALU.mult, eng="pool")
                    fw.stt(DER[:, 5], win(RKA), -1.0, E2.full(), ALU.mult, ALU.mult)
                    fw.act(PC.full(), _bc(self.rowc(RCW, w0c + lastp, w0c + lastp + 1), [[64, 4]]), AF.Exp)
                    wcs = [cs if d == 0 else 3 - cs for cs in range(4)]
                    units = [(cs, hh) for cs in range(4) for hh in range(2)]
                    fm = lambda q, wc, hb: DER[hb:hb + 64, q, wc, :]
                    for u, (cs, hh) in sorted(enumerate(units), key=lambda t: (t[1][1], t[1][0])):
                        wc, hb = wcs[cs], hh * 64
                        sl = slice(u * 64, (u + 1) * 64)
                        fw.mm(pb[5][0:64, sl], fm(0, wc, hb), fm(1, wc, hb))
                        fw.mm(pb[6][0:64, sl], fm(1, wc, hb), fm(0, wc, hb))
                        fw.mm(pb[7][0:64, sl], fm(2, wc, hb), fm(0, wc, hb))
                    mk = lambda ci: _bc(self.C(ci, 0, 64, 0, 64), [[0, 8], [1, 64]])
                    fw.tt(XY[:, 0, 0, :, :], p3(pb[5]), mk(sgn_strict_ts), ALU.mult)
                    fw.tt(XY[:, 0, 1, :, :], p3(pb[6]), mk(sgn_strict_st), ALU.mult)
                    fw.tt(MK[:, 0, :, :], p3(pb[7]), mk(sgn_strict_st), ALU.mult)
                    for u, (cs, hh) in sorted(enumerate(units), key=lambda t: (t[1][1], t[1][0])):
                        wc, hb = wcs[cs], hh * 64
                        sl = slice(u * 64, (u + 1) * 64)
                        fw.mm(pb[5][0:64, sl], fm(1, wc, hb), fm(3, wc, hb))
                        fw.mm(pb[6][0:64, sl], fm(2, wc, hb), fm(3, wc, hb))
                    fw.tt(MK[:, 1, :, :], p3(pb[5]), mk(incl_st), ALU.mult)
                    fw.tt(MK[:, 2, :, :], p3(pb[6]), mk(incl_st), ALU.mult)
                    for u, (cs, hh) in sorted(enumerate(units), key=lambda t: (t[1][1], t[1][0])):
                        wc, hb = wcs[cs], hh * 64
                        sl = slice(u * 64, (u + 1) * 64)
                        idn = self.C(C_ID, hb, hb + 64, hb, hb + 64)
                        fw.tr(pb[0][0:64, sl], fm(0, wc, hb), idn)
                        fw.tr(pb[1][0:64, sl], fm(4, wc, hb), idn)
                        fw.tr(pb[7][0:64, sl], fm(5, wc, hb), idn)
                        c0 = w0c + wc * 64
                        fw.tr(pb[5][0:64, sl], self.rowc(RV, c0, c0 + 64, hb, hb + 64), idn)
                    fw.copy(ATKB.full(), p3(pb[0]), eng="act")
                    fw.copy(TK[:, 1, :, :], p3(pb[1]), eng="dve")
                    fw.copy(TK[:, 2, :, :], p3(pb[7]), eng="act")
                    fw.copy(VT.full(), p3(pb[5]), eng="dve")
                    gi = self.neumann(64, 8, 64, +1, 5, XY, G, bf=True)
                    for u in range(8):
                        fw.mm(pb[5][0:64, u * 64:(u + 1) * 64], MK[:, 0, u, :], VT[:, u, :])
                    fw.copy(W1B.full(), p3(pb[5]), eng="act")
                    for u, (cs, hh) in sorted(enumerate(units), key=lambda t: (t[1][1], t[1][0])):
                        hb = hh * 64
                        fw.mm(pb[6][0:64, u * 64:(u + 1) * 64], G[:, gi, u, :], W1B[:, u, :])
                        fw.mm(pb[7][hb:hb + 64, cs * 64:(cs + 1) * 64], ATKB[:, u, :], G[:, gi, u, :])
                    fw.copy(UV.full(), p3(pb[6]), eng="dve")
                    fw.copy(AHT.full(), pb[7][:, 0:256].w(lambda ap: ap.rearrange("p (a b) -> p a b", b=64)), eng="act")
                    for cs in range(4):
                        wc = wcs[cs]
                        c0 = w0c + wc * 64
                        for hh in range(2):
                            hb = hh * 64
                            fw.mm(pb[0][0:64, hh * 64:(hh + 1) * 64], AHT[hb:hb + 64, cs, :], M[hb:hb + 64, :])
                        fw.tt(US.full(), UV[:, 2 * cs:2 * cs + 2, :].w(lambda ap: ap.rearrange("p a b -> p (a b)")), pb[0][0:64, 0:128], ALU.add)
                        for hh in range(2):
                            hb = hh * 64
                            u = cs * 2 + hh
                            fw.mm(pb[1][hb:hb + 64, 0:64], M[hb:hb + 64, :], fm(3, wc, hb), start=True, stop=False)
                            fw.mm(pb[1][hb:hb + 64, 0:64], US[:, hh * 64:(hh + 1) * 64], MK[:, 1, u, :], start=False, stop=False)
                            fw.mm(pb[1][hb:hb + 64, 0:64], VT[:, u, :], MK[:, 2, u, :], start=False, stop=True)
                        if d == 0:
                            fw.copy(self.rowc(RY, c0, c0 + 64), pb[1][:, 0:64], eng="act")
                        else:
                            fw.tt(self.rowc(RY, c0, c0 + 64), self.rowc(RY, c0, c0 + 64), pb[1][:, 0:64], ALU.add)
                        for hh in range(2):
                            hb = hh * 64
                            u = cs * 2 + hh
                            fw.mm(pb[0][hb:hb + 64, 256:320], TK[:, 1, u, :], VT[:, u, :], start=True, stop=False)
                            fw.mm(pb[0][hb:hb + 64, 256:320], TK[:, 2, u, :], US[:, hh * 64:(hh + 1) * 64], start=False, stop=True)
                        fw.stt(M.full(), M.full(), PC[:, wc:wc + 1], pb[0][:, 256:320], ALU.mult, ALU.add)
            self.dump_row("rw_y%d" % layer, hp, 3, self.row(RY))
            for ti, (c0, c1) in enumerate(TOK_TILES):
                n = c1 - c0
                ps = pb[2 + ti % 2][:, 0:n]
                fw.mm(ps, self.C(C_BO), self.rowc(RY, c0, c1))
                fw.stt(self.rowc(T1, c0, c1), ps, -1.0 / 64.0, self.rowc(RY, c0, c1), ALU.mult, ALU.add)
            self.headstat(T2, T1, 2, scale=1.0 / 64.0)
            fw.tt(self.row(T1), self.row(T1), self.row(T2), ALU.mult)
            fw.ts(self.row(T1), self.row(T1), self.pvc("gnw", hp), ALU.mult, self.pvc("gnb", hp), ALU.add)
            for ti, (c0, c1) in enumerate(TOK_TILES):
                n = c1 - c0
                ps = pb[2 + ti % 2][:, 0:n]
                fw.mm(ps, self.C(C_BO), self.rowc(RBON, c0, c1))
                fw.tt(self.rowc(T2, c0, c1), ps, self.rowc(RV, c0, c1), ALU.mult)
            fw.tt(self.row(T1), self.row(T1), self.row(T2), ALU.add)
            fw.dma(self.row(T2), spill[3])
            for ti, (c0, c1) in enumerate(TOK_TILES):
                n = c1 - c0
                ps = pb[2 + ti % 2][:, 0:n]
                fw.mm(ps, self.rup[:, 2, hp * 128:(hp + 1) * 128], self.rowc(T2, c0, c1))
                fw.tt(self.rowc(T1, c0, c1), self.rowc(T1, c0, c1), ps, ALU.mult)
            self.dump_row("mix%d" % layer, 5 + hp, 8, self.row(T1))
            self.store_o(5 + hp, self.row(T1))


def kernel(**inputs):
    inp = {k: np.asarray(v) for k, v in inputs.items()}
    shared = host_shared(inp)
    nc = bass.Bass("TRN2", target_bir_lowering=False)
    prog = Prog(nc)
    prog.build()
    in_maps = []
    for b in range(8):
        m = dict(shared)
        m.update(host_core(inp, b))
        in_maps.append(m)
    res = run_bass_kernel_spmd(nc, in_maps, core_ids=list(range(8)))
    out = np.empty((8, 2048, 1024), np.float32)
    for b in range(8):
        o = np.asarray(res.results[b]["out"]).reshape(8, 128, 8, PW_)
        out[b] = o.transpose(0, 3, 2, 1).reshape(2048, 1024)
    return out
```

```python
import numpy as np
import concourse.bass as bass
import concourse.mybir as mybir
from concourse.bass_utils import run_bass_kernel_spmd

F32 = mybir.dt.float32
BF16 = mybir.dt.bfloat16
AF = mybir.ActivationFunctionType
ALU = mybir.AluOpType

SEM_CHUNK = 30000
COMPUTE = ("pe", "act", "dve", "pool")


class V:
    def __init__(self, tt, ap, box):
        self.tt, self.ap, self.box = tt, ap, box

    def w(self, fn):
        return V(self.tt, fn(self.ap), self.box)


class TT:
    def __init__(self, fw, name, handle, shape, is_dram=False, is_psum=False):
        self.fw, self.name, self.h, self.shape = fw, name, handle, list(shape)
        self.is_dram = is_dram
        self.is_psum = is_psum
        self.recs = []
        st = [1] * len(shape)
        for i in range(len(shape) - 2, 0, -1):
            st[i] = st[i + 1] * shape[i + 1]
        st[0] = 0
        self.strides = st

    def __getitem__(self, idx):
        if not isinstance(idx, tuple):
            idx = (idx,)
        idx = list(idx) + [slice(None)] * (len(self.shape) - len(idx))
        lo, hi = [], []
        for d, (i, n) in enumerate(zip(idx, self.shape)):
            if isinstance(i, int):
                a, b = i, i + 1
                idx[d] = slice(i, i + 1) if (d == 0 and not self.is_dram) else i
            else:
                a, b, s = i.indices(n)
                assert s == 1
            lo.append(a)
            hi.append(b)
        f0 = sum(lo[d] * self.strides[d] for d in range(1, len(self.shape)))
        f1 = sum((hi[d] - 1) * self.strides[d] for d in range(1, len(self.shape))) + 1
        base = self.h.ap() if self.is_dram else self.h
        ap = base[tuple(idx)]
        if self.is_psum:
            f0, f1 = 0, 1 << 30
            lo[0], hi[0] = (lo[0] // 32) * 32, ((hi[0] + 31) // 32) * 32
        return V(self, ap, (lo[0], hi[0], f0, f1))

    def full(self):
        return self[tuple(slice(None) for _ in self.shape)]


class TA_(TT):
    def __init__(self, parent, off, shape, dtype=None, pbase=0):
        self.parent = parent
        self.shape = list(shape)
        self.is_dram = False
        self.off = off
        self.pbase = pbase
        self.ratio = 2 if dtype is not None else 1
        n = 1
        for s_ in shape[1:]:
            n *= s_
        nf = (n + self.ratio - 1) // self.ratio
        flat = parent.h[pbase:pbase + shape[0], off:off + nf]
        if dtype is not None:
            flat = flat.bitcast(dtype)
        names = " ".join("d%d" % i for i in range(1, len(shape)))
        kw = {"d%d" % i: shape[i] for i in range(1, len(shape))}
        self.base = flat.rearrange("p (%s) -> p %s" % (names, names), **kw) if len(shape) > 2 else flat
        st = [1] * len(shape)
        for i in range(len(shape) - 2, 0, -1):
            st[i] = st[i + 1] * shape[i + 1]
        st[0] = 0
        self.strides = st

    @property
    def recs(self):
        return self.parent.recs

    @recs.setter
    def recs(self, v):
        self.parent.recs = v

    def __getitem__(self, idx):
        if not isinstance(idx, tuple):
            idx = (idx,)
        idx = list(idx) + [slice(None)] * (len(self.shape) - len(idx))
        lo, hi = [], []
        for d, (i, n) in enumerate(zip(idx, self.shape)):
            if isinstance(i, int):
                a, b = i, i + 1
                if d == 0:
                    idx[d] = slice(i, i + 1)
            else:
                a, b, s = i.indices(n)
                assert s == 1
            lo.append(a)
            hi.append(b)
        f0 = sum(lo[d] * self.strides[d] for d in range(1, len(self.shape)))
        f1 = sum((hi[d] - 1) * self.strides[d] for d in range(1, len(self.shape))) + 1
        ap = self.base[tuple(idx)]
        r = self.ratio
        return V(self, ap, (self.pbase + lo[0], self.pbase + hi[0], self.off + f0 // r, self.off + (f1 + r - 1) // r))


def _overlap(a, b):
    return a[0] < b[1] and b[0] < a[1] and a[2] < b[3] and b[2] < a[3]


def _covers(a, b):
    return a[0] <= b[0] and a[1] >= b[1] and a[2] <= b[2] and a[3] >= b[3]


class FW:
    def __init__(self, nc, n_dma_sems=12):
        self.nc = nc
        self.ops = {e: [] for e in ("pe", "act", "dve", "pool", "sp")}
        self.tick = {e: 0 for e in COMPUTE}
        self.waited = {e: {} for e in self.ops}
        self.sems = {}
        self.n_dma_sems = n_dma_sems
        self.dma_cnt = {}
        self.dma_uses = {}
        self.stack = None
        self.n_ops = 0
        self.out_tokens = []

    def sbuf(self, name, shape, dtype=F32):
        h = self.nc.alloc_sbuf_tensor(name, list(shape), dtype)
        return TT(self, name, h, shape)

    def psum(self, name, shape, dtype=F32):
        h = self.nc.alloc_psum_tensor(name, list(shape), dtype)
        return TT(self, name, h, shape, is_psum=True)

    def dram(self, name, shape, dtype=F32, kind="Internal"):
        h = self.nc.dram_tensor(name, list(shape), dtype, kind=kind)
        return TT(self, name, h, shape, is_dram=True)

    def _sem(self, key):
        if key not in self.sems:
            self.sems[key] = self.nc.alloc_semaphore("s_%s_%s" % key)
        return self.sems[key]

    def _token_wait(self, eng, tok, force=False):
        kind = tok[0]
        if kind == "c":
            _, pe, n = tok
            if pe == "pe" and eng == "pe" and not force:
                return []
            if self.waited[eng].get(("c", pe), 0) >= n:
                return []
            self.waited[eng][("c", pe)] = n
            return [((pe, (n - 1) // SEM_CHUNK), (n - 1) % SEM_CHUNK + 1)]
        else:
            _, q, j, m = tok
            if self.waited[eng].get(("d", q, j), 0) >= m:
                return []
            self.waited[eng][("d", q, j)] = m
            return [(("dma" + q, j), 16 * m)]

    def op(self, eng, emit, reads=(), writes=(), dma=False, force=()):
        self.n_ops += 1
        if dma:
            q = eng
            i = self.dma_cnt.get(q, 0)
            self.dma_cnt[q] = i + 1
            j = i % self.n_dma_sems
            m = self.dma_uses.get((q, j), 0) + 1
            self.dma_uses[(q, j)] = m
            token = ("d", q, j, m)
            inc = (("dma" + q, j), 16)
        else:
            self.tick[eng] += 1
            n = self.tick[eng]
            token = ("c", eng, n)
            inc = ((eng, (n - 1) // SEM_CHUNK), 1)
        deps = []
        for v in reads:
            psum = getattr(v.tt, "is_psum", False)
            for r in v.tt.recs:
                if (r[3] or (psum and r[2] != eng)) and _overlap(r[0], v.box):
                    deps.append(r[1])
        for v in writes:
            for r in v.tt.recs:
                if _overlap(r[0], v.box):
                    deps.append(r[1])
        if dma and token[3] > 1:
            deps.append(("d", token[1], token[2], token[3] - 1))
        waits = []
        for t in deps:
            waits += self._token_wait(eng, t)
        for t in force:
            waits += self._token_wait(eng, t, force=True)
        for v in writes:
            v.tt.recs = [r for r in v.tt.recs if not _covers(v.box, r[0])]
            v.tt.recs.append((v.box, token, eng, True))
        for v in reads:
            recs = v.tt.recs
            for k, r in enumerate(recs):
                if (not r[3]) and r[2] == eng and r[0] == v.box and r[1][0] == token[0]:
                    recs[k] = (v.box, token, eng, False)
                    break
            else:
                recs.append((v.box, token, eng, False))
        self.ops[eng].append((waits, emit, inc))
        return token

    def wait_tokens(self, eng, tokens):
        waits = []
        for t in tokens:
            waits += self._token_wait(eng, t)
        self.ops[eng].append((waits, None, None))

    def emit(self):
        nc = self.nc
        for e in self.ops:
            for waits, _, inc in self.ops[e]:
                for k, _v in waits:
                    self._sem(k)
                if inc is not None:
                    self._sem(inc[0])
        engmap = {"pe": "tensor", "act": "scalar", "dve": "vector", "pool": "gpsimd", "sp": "sync"}
        with nc.Block() as block:
            for e, ops in self.ops.items():
                def body(engine, ops=ops):
                    for waits, emit, inc in ops:
                        for k, val in waits:
                            engine.wait_ge(self.sems[k], val)
                        if emit is not None:
                            inst = emit(engine)
                            inst.then_inc(self.sems[inc[0]], inc[1])
                getattr(block, engmap[e])(body)

    def dma(self, out, in_, q="sp", **kw):
        return self.op(q, lambda e: e.dma_start(out=out.ap, in_=in_.ap, **kw),
                       reads=[in_], writes=[out], dma=True)

    def _pe_cfg(self, out, lhsT, kind="M"):
        cfg = (kind, lhsT.box[0], lhsT.box[1], out.box[0], out.box[1])
        tt = out.tt
        force = ()
        last = getattr(tt, "pe_last", None)
        if last is not None and last[0] != cfg:
            force = (last[1],)
        dcls = str(lhsT.ap.dtype)
        gl = getattr(self, "pe_glast", None)
        if gl is not None and gl[0] != dcls:
            force = force + (gl[1],)
        self._pe_dcls = dcls
        return cfg, force

    def mm(self, out, lhsT, rhs, start=True, stop=True):
        cfg, force = self._pe_cfg(out, lhsT)
        tok = self.op("pe", lambda e: e.matmul(out.ap, lhsT.ap, rhs.ap, start=start, stop=stop),
                      reads=[lhsT, rhs], writes=[out], force=force)
        out.tt.pe_last = (cfg, tok)
        self.pe_glast = (self._pe_dcls, tok)
        return tok

    def tr(self, out, in_, ident):
        cfg, force = self._pe_cfg(out, in_, kind="T")
        tok = self.op("pe", lambda e: e.transpose(out.ap, in_.ap, ident.ap),
                      reads=[in_, ident], writes=[out], force=force)
        out.tt.pe_last = (cfg, tok)
        self.pe_glast = (self._pe_dcls, tok)
        return tok

    def act(self, out, in_, func, bias=None, scale=None, accum=None):
        reads = [in_]
        kw = {}
        if isinstance(bias, V):
            reads.append(bias)
            kw["bias"] = bias.ap
        elif bias is not None:
            kw["bias"] = bias
        if isinstance(scale, V):
            reads.append(scale)
            kw["scale"] = scale.ap
        elif scale is not None:
            kw["scale"] = scale
        writes = [out]
        if accum is not None:
            writes.append(accum)
            kw["accum_out"] = accum.ap
        return self.op("act", lambda e: e.activation(out.ap, in_.ap, func, **kw),
                       reads=reads, writes=writes)

    def tt(self, out, a, b, op, eng="dve"):
        return self.op(eng, lambda e: e.tensor_tensor(out.ap, a.ap, b.ap, op),
                       reads=[a, b], writes=[out])

    def ts(self, out, a, s1, op0, s2=None, op1=None, eng="dve", accum=None):
        reads = [a]
        x1 = s1.ap if isinstance(s1, V) else s1
        x2 = s2.ap if isinstance(s2, V) else s2
        if isinstance(s1, V):
            reads.append(s1)
        if isinstance(s2, V):
            reads.append(s2)
        writes = [out]
        kw = {}
        if accum is not None:
            writes.append(accum)
            kw["accum_out"] = accum.ap
        if op1 is None:
            return self.op(eng, lambda e: e.tensor_scalar(out.ap, a.ap, x1, None, op0, **kw),
                           reads=reads, writes=writes)
        return self.op(eng, lambda e: e.tensor_scalar(out.ap, a.ap, x1, x2, op0, op1, **kw),
                       reads=reads, writes=writes)

    def stt(self, out, a, s, b, op0, op1, eng="dve"):
        reads = [a, b]
        x = s.ap if isinstance(s, V) else s
        if isinstance(s, V):
            reads.append(s)
        return self.op(eng, lambda e: e.scalar_tensor_tensor(out.ap, a.ap, x, b.ap, op0, op1),
                       reads=reads, writes=[out])

    def copy(self, out, in_, eng="dve"):
        if eng == "act":
            return self.op("act", lambda e: e.copy(out.ap, in_.ap), reads=[in_], writes=[out])
        return self.op(eng, lambda e: e.tensor_copy(out.ap, in_.ap), reads=[in_], writes=[out])

    def memset(self, out, val, eng="dve"):
        return self.op(eng, lambda e: e.memset(out.ap, val), writes=[out])

    def scan(self, out, d0, d1, init, op0=ALU.mult, op1=ALU.add):
        reads = [d0, d1]
        x = init.ap if isinstance(init, V) else init
        if isinstance(init, V):
            reads.append(init)
        return self.op("dve", lambda e: e.tensor_tensor_scan(out.ap, d0.ap, d1.ap, x, op0, op1),
                       reads=reads, writes=[out])

    def recip(self, out, in_):
        return self.op("dve", lambda e: e.reciprocal(out.ap, in_.ap), reads=[in_], writes=[out])

TA = 2304
CTXN = 256
NPIECE = 9
PW_ = 256
L = 2
IN_OFF = [j * 128 if j < 12 else 1560 + (j - 12) * 128 for j in range(28)]
NEG = 30000.0


def _rev(ap, n):
    return bass.AP(tensor=ap.tensor, offset=ap.offset + (n - 1), ap=[list(ap.ap[0]), [-1, n]])


PV_SPEC = [("g_pre", 8), ("g_post", 8), ("g_fpre", 8), ("g_fpost", 8), ("ada_b", 48),
           ("gconv", 36), ("gnorm", 1), ("lconv", 8), ("lconvb", 2), ("lba", 4), ("lbx", 4),
           ("llam", 4), ("mu", 12), ("w0", 6), ("a0", 6), ("kk", 3), ("ka", 3), ("rk", 3),
           ("gnw", 3), ("gnb", 3)]
PV_OFF = {}
_o = 0
for _n, _w in PV_SPEC:
    PV_OFF[_n] = (_o, _w)
    _o += _w
NPV = _o
NCST = 13


def _cst():
    r = np.arange(128)[:, None]
    c = np.arange(128)[None, :]
    UI = (r <= c).astype(np.float32)
    LI = (r >= c).astype(np.float32)
    ident = np.eye(128, dtype=np.float32)
    ones = np.ones((128, 128), np.float32)
    bo = np.zeros((128, 128), np.float32)
    bo[:64, :64] = 1
    bo[64:, 64:] = 1
    mats = [ident, ones, bo, UI, LI, NEG * UI, NEG * LI, -NEG * UI, -NEG * LI,
            -NEG * (1 - UI), -NEG * (1 - LI), 1 - UI, 1 - LI]
    return np.ascontiguousarray(np.stack(mats, axis=1))


C_ID, C_ONES, C_BO, C_UI, C_LI, C_PUI, C_PLI, C_NUI, C_NLI, C_NSL, C_NSU, C_SL, C_SU = range(13)


def _kc(w):
    K = w.shape[0]
    return np.ascontiguousarray(w.reshape(K // 128, 128, w.shape[1]).transpose(1, 0, 2))


def _colchunks(w, offs):
    K = w.shape[0]
    return np.ascontiguousarray(
        np.stack([_kc(w[:, o:o + 128]).reshape(128, (K // 128) * 128) for o in offs]))


def _pcol(v, n):
    return np.ascontiguousarray(np.asarray(v).reshape(n, 128).T)


def host_shared(inp):
    d = {}
    d["cst"] = _cst()
    d["adaw"] = np.stack([_colchunks(inp["ada_w"][i], [j * 128 for j in range(48)]) for i in range(L)]).reshape(L * 48, 128, 1024)
    d["win"] = np.stack([_colchunks(inp["w_in"][i], IN_OFF) for i in range(L)]).reshape(L * 28, 128, 1024)
    d["wba"] = np.stack([_kc(inp["w_in"][i][:, 1536:1560]).reshape(128, 8 * 24) for i in range(L)])
    d["wout"] = np.stack([_colchunks(inp["w_out"][i], [j * 128 for j in range(8)]) for i in range(L)]).reshape(L * 8, 128, 1024)
    d["wup"] = np.stack([_colchunks(inp["ffn_up"][i], [j * 128 for j in range(32)]) for i in range(L)]).reshape(L * 32, 128, 1024)
    d["wdn"] = np.stack([_colchunks(inp["ffn_down"][i], [j * 128 for j in range(8)]) for i in range(L)]).reshape(L * 8, 128, 4096)
    pv = np.zeros((L, 128, NPV), np.float32)
    rowt = np.zeros((L, 128, 24), np.float32)
    lw = np.zeros((L, 128, 8, 128), np.float32)
    rup = np.zeros((L, 128, 3, 384), np.float32)
    for i in range(L):
        def put(name, arr):
            o, w = PV_OFF[name]
            pv[i, :, o:o + w] = arr
        put("g_pre", _pcol(inp["norm_mix_pre"][i], 8))
        put("g_post", _pcol(inp["norm_mix_post"][i], 8))
        put("g_fpre", _pcol(inp["norm_ffn_pre"][i], 8))
        put("g_fpost", _pcol(inp["norm_ffn_post"][i], 8))
        put("ada_b", _pcol(inp["ada_b"][i], 48))
        gc = inp["gdn_conv"][i]
        put("gconv", np.stack([_pcol(gc[k], 9) for k in range(4)], axis=2).reshape(128, 36))
        put("gnorm", np.tile(inp["gdn_norm"][i], 2)[:, None])
        lc = inp["lru_conv"][i]
        put("lconv", np.stack([_pcol(lc[k], 2) for k in range(4)], axis=2).reshape(128, 8))
        put("lconvb", _pcol(inp["lru_conv_b"][i], 2))
        put("lba", np.concatenate([_pcol(inp["lru_ba"][i][dd], 2) for dd in range(2)], axis=1))
        put("lbx", np.concatenate([_pcol(inp["lru_bx"][i][dd], 2) for dd in range(2)], axis=1))
        put("llam", np.concatenate([_pcol(inp["lru_lambda"][i][dd], 2) for dd in range(2)], axis=1))
        put("mu", _pcol(inp["rwkv_mu"][i], 12))
        put("w0", np.concatenate([_pcol(inp["rwkv_w0"][i][dd], 3) for dd in range(2)], axis=1))
        put("a0", np.concatenate([_pcol(inp["rwkv_a0"][i][dd], 3) for dd in range(2)], axis=1))
        put("kk", _pcol(inp["rwkv_k_k"][i], 3))
        put("ka", _pcol(inp["rwkv_k_a"][i], 3))
        put("rk", _pcol(inp["rwkv_r_k"][i].reshape(-1), 3))
        put("gnw", _pcol(inp["rwkv_gn_w"][i], 3))
        put("gnb", _pcol(inp["rwkv_gn_b"][i], 3))
        rowt[i, :, 0:12] = inp["gdn_a_log"][i].reshape(1, 12)
        rowt[i, :, 12:24] = inp["gdn_dt_bias"][i].reshape(1, 12)
        for ax, nm in enumerate(("lru_wa", "lru_wx")):
            for dd in range(2):
                for jc in range(2):
                    for bl in range(2):
                        lw[i, bl * 64:(bl + 1) * 64, ax * 4 + dd * 2 + jc, bl * 64:(bl + 1) * 64] = inp[nm][i][dd, 2 * jc + bl]
        rup[i, :, 0, :] = inp["rwkv_w_up"][i].reshape(128, 384)
        rup[i, :, 1, :] = inp["rwkv_a_up"][i].reshape(128, 384)
        rup[i, :, 2, :] = inp["rwkv_g_up"][i]
    d["pv"] = pv
    d["rowt"] = rowt
    d["lw"] = lw.reshape(L, 128, 1024)
    d["rup"] = rup.reshape(L, 128, 3 * 384)
    return d


def host_core(inp, b):
    xt = np.concatenate([inp["ctx"][b], inp["x"][b]], axis=0)
    xin = np.ascontiguousarray(xt.reshape(NPIECE, PW_, 8, 128).transpose(0, 3, 2, 1)).reshape(NPIECE, 128, 8 * PW_)
    cc = np.stack([inp["c"][b], inp["c_ctx"]], axis=1)
    call = np.ascontiguousarray(cc.reshape(8, 128, 2).transpose(1, 0, 2)).reshape(128, 16)
    return {"xin": xin, "call": call}


NROW = 10
GDN_ORDER = [list(range(18)), [1, 0] + list(range(17, 1, -1))]
RWKV_ORDER = [list(range(36)), [3, 2, 1, 0] + list(range(35, 3, -1))]
TOK_TILES = [(0, 256), (256, 768), (768, 1280), (1280, 1792), (1792, 2304)]
SEGS = [(0, CTXN), (CTXN, TA)]


def _bc(view, dims):
    return view.w(lambda ap: bass.AP(tensor=ap.tensor, offset=ap.offset, ap=[list(ap.ap[0])] + [list(d) for d in dims]))


class _Stop(Exception):
    pass


class Prog:
    def stop(self, tag):
        if self.stop_tag == tag:
            raise _Stop()

    def __init__(self, nc, dbg=(), nlayers=L, stop_after=None):
        self.nc = nc
        self.fw = FW(nc)
        self.dbg = set(dbg)
        self.dbg_out = {}
        self.final = []
        self.nlayers = nlayers
        self.stop_after = stop_after
        self.stop_tag = None
        self.skip = set()
        self.alloc()

    def alloc(self):
        fw = self.fw
        EI = "ExternalInput"
        self.xin = fw.dram("xin", [NPIECE, 128, 8 * PW_], F32, EI)
        self.call = fw.dram("call", [128, 16], F32, EI)
        self.cst_d = fw.dram("cst", [128, NCST, 128], F32, EI)
        self.adaw = fw.dram("adaw", [L * 48, 128, 1024], F32, EI)
        self.win = fw.dram("win", [L * 28, 128, 1024], F32, EI)
        self.wba_d = fw.dram("wba", [L, 128, 8 * 24], F32, EI)
        self.wout = fw.dram("wout", [L * 8, 128, 1024], F32, EI)
        self.wup = fw.dram("wup", [L * 32, 128, 1024], F32, EI)
        self.wdn = fw.dram("wdn", [L * 8, 128, 4096], F32, EI)
        self.pv_d = fw.dram("pv", [L, 128, NPV], F32, EI)
        self.rowt_d = fw.dram("rowt", [L, 128, 24], F32, EI)
        self.lw_d = fw.dram("lw", [L, 128, 1024], F32, EI)
        self.rup_d = fw.dram("rup", [L, 128, 3 * 384], F32, EI)
        self.out_d = fw.dram("out", [8, 128, 8 * PW_], F32, "ExternalOutput")
        self.xr = fw.dram("xr", [NPIECE, 128, 8 * PW_], F32)
        self.od = fw.dram("od", [8, 128, TA], BF16)

        self.cst = fw.sbuf("cst_s", [128, NCST, 128], F32)
        self.H = fw.sbuf("H", [128, 8, TA], BF16)
        self.WS = fw.sbuf("WS", [128, NROW * TA], F32)
        self.pv = fw.sbuf("pv_s", [128, NPV], F32)
        self.rowt = fw.sbuf("rowt_s", [128, 24], F32)
        self.sc = fw.sbuf("sc", [128, 8, 2], F32)
        self.mod = fw.sbuf("mod", [128, 48, 2], F32)
        self.gs = fw.sbuf("gs", [128, 4, 8, 2], F32)
        self.epsc = fw.sbuf("epsc", [128, 4], F32)
        self.wst = [fw.sbuf("wst%d" % i, [128, 8, 128], F32) for i in range(3)]
        self.wst_i = 0
        self.obf = fw.sbuf("obf", [128, TA], BF16)
        self.wbf = [fw.sbuf("wbf%d" % i, [128, 8, 128], BF16) for i in range(3)]
        self.wbf_i = 0
        self.wba = fw.sbuf("wba_s", [128, 8, 24], BF16)
        self.small = fw.sbuf("small", [128, 4096], F32)
        self.misc = fw.sbuf("misc", [128, 2048], F32)
        self.gt = fw.sbuf("gt", [128, 10, 18, 12], F32)
        self.lwt = fw.sbuf("lwt", [128, 8, 128], F32)
        self.rup = fw.sbuf("rup_s", [128, 3, 384], F32)
        self.identb = fw.sbuf("identb", [128, 128], BF16)
        self.pb = [fw.psum("pb%d" % i, [128, 512], F32) for i in range(8)]

    def row(self, r):
        return self.WS[:, r * TA:(r + 1) * TA]

    def rowc(self, r, c0, c1, p0=0, p1=128):
        return self.WS[p0:p1, r * TA + c0:r * TA + c1]

    def pvc(self, name, j=0, p0=0, p1=128):
        o, w = PV_OFF[name]
        return self.pv[p0:p1, o + j:o + j + 1]

    def C(self, idx, p0=0, p1=128, c0=0, c1=128):
        return self.cst[p0:p1, idx, c0:c1]

    def dump(self, name, view, shape):
        if name not in self.dbg:
            return
        o = self.fw.dram("dbg_" + name, list(shape), F32, kind="ExternalOutput")
        self.dbg_out[name] = o
        self.final.append(self.fw.dma(o.full(), view))

    def dump_row(self, name, j, nj, view):
        if name not in self.dbg:
            return
        if name not in self.dbg_out:
            self.dbg_out[name] = self.fw.dram("dbg_" + name, [nj, 128, TA], F32, kind="ExternalOutput")
        self.final.append(self.fw.dma(self.dbg_out[name][j], view))

    def next_wbf(self):
        t = self.wbf[self.wbf_i % 3]
        self.wbf_i += 1
        return t

    def load_w(self, dst, src, eng="pool"):
        st = self.wst[self.wst_i % 3]
        self.wst_i += 1
        self.fw.dma(st.full().w(lambda ap: ap.rearrange("p a b -> p (a b)")), src)
        self.fw.copy(dst, st.full(), eng=eng)

    def load_wc(self, dst, src, slot, first):
        r8 = lambda ap: ap.rearrange("p (a b) -> p a b", a=8)
        if first:
            self.load_w(dst, src, eng="dve")
            self.fw.dma(self.wcache[slot].w(r8), dst)
        else:
            self.fw.dma(dst, self.wcache[slot].w(r8))

    def build(self):
        fw = self.fw
        fw.dma(self.cst.full(), self.cst_d.full())
        fw.dma(self.sc.full().w(lambda ap: ap.rearrange("p a b -> p (a b)")), self.call.full())
        fw.act(self.sc.full(), self.sc.full(), AF.Silu)
        fw.copy(self.identb.full(), self.C(C_ID))
        fw.memset(self.epsc[:, 0:1], 1e-6)
        fw.memset(self.epsc[:, 1:2], 1.0)
        fw.memset(self.epsc[:, 2:3], 6.4e-4)
        fw.memset(self.epsc[:, 3:4], 0.0)
        try:
            for layer in range(self.nlayers):
                self.layer(layer)
                if self.stop_after is not None and self.stop_after[0] == layer:
                    break
        except _Stop:
            pass
        fw.wait_tokens("sp", self.final)
        fw.emit()

    def layer(self, layer):
        fw = self.fw
        last = (layer == L - 1)
        fw.dma(self.pv.full(), self.pv_d[layer])
        fw.dma(self.rowt.full(), self.rowt_d[layer])
        fw.dma(self.lwt.full().w(lambda ap: ap.rearrange("p a b -> p (a b)")), self.lw_d[layer])
        fw.dma(self.rup.full().w(lambda ap: ap.rearrange("p a b -> p (a b)")), self.rup_d[layer])
        self.modulation(layer)
        self.phase_a(layer)
        if self.stop_after == (layer, "a"):
            return
        st = self.wst[self.wst_i % 3]
        self.wst_i += 1
        stv = st.full().w(lambda ap: ap.rearrange("p a b -> p (a b)")[:, 0:192])
        self.fw.dma(stv, self.wba_d[layer])
        self.fw.copy(self.wba.full().w(lambda ap: ap.rearrange("p a b -> p (a b)")), stv, eng="pool")
        if "gdn" not in self.skip:
            self.gdn(layer)
        if self.stop_after == (layer, "gdn"):
            return
        if "lru" not in self.skip:
            self.lru(layer)
        if self.stop_after == (layer, "lru"):
            return
        if "rwkv" not in self.skip:
            self.rwkv(layer)
        if self.stop_after == (layer, "rwkv"):
            return
        self.phase_c(layer, last)

    def modulation(self, layer):
        fw = self.fw
        mp = self.pb[7]
        modp = mp[:, 0:96].w(lambda ap: ap.rearrange("p (a b) -> p a b", b=2))
        for j in range(48):
            wt = self.wst[self.wst_i % 3]
            self.wst_i += 1
            fw.dma(wt.full().w(lambda ap: ap.rearrange("p a b -> p (a b)")), self.adaw[layer * 48 + j])
            for kc in range(8):
                fw.mm(mp[:, 2 * j:2 * j + 2], wt[:, kc, :], self.sc[:, kc, :], start=(kc == 0), stop=(kc == 7))
        o, w = PV_OFF["ada_b"]
        bb = _bc(self.pv[:, o:o + 48], [[1, 48], [0, 2]])
        fw.tt(self.mod.full(), modp, bb, ALU.add)
        self.dump("mod%d" % layer, self.mod.full(), [128, 48, 2])

        def gcol(name):
            o, w = PV_OFF[name]
            return _bc(self.pv[:, o:o + 8], [[1, 8], [0, 2]])
        fw.stt(self.gs[:, 0, :, :], self.mod[:, 8:16, :], 1.0, gcol("g_pre"), ALU.add, ALU.mult)
        fw.tt(self.gs[:, 1, :, :], self.mod[:, 16:24, :], gcol("g_post"), ALU.mult)
        fw.stt(self.gs[:, 2, :, :], self.mod[:, 32:40, :], 1.0, gcol("g_fpre"), ALU.add, ALU.mult)
        fw.tt(self.gs[:, 3, :, :], self.mod[:, 40:48, :], gcol("g_fpost"), ALU.mult)

    def rms_rstd(self, src3, sq3, n, dst, ps):
        fw = self.fw
        fw.act(sq3, src3, AF.Square)
        for kc in range(8):
            fw.mm(ps, self.C(C_ONES), self._k(sq3, kc), start=(kc == 0), stop=(kc == 7))
        fw.act(dst, ps, AF.Ln, bias=self.epsc[:, 0:1], scale=1.0 / 1024.0)
        fw.act(dst, dst, AF.Exp, scale=-0.5)

    def _k(self, v3, kc):
        return v3.w(lambda ap: ap[:, kc, :])

    def phase_a(self, layer):
        fw = self.fw
        src = self.xin if layer == 0 else self.xr
        xt = self.small[:, 0:2048].w(lambda ap: ap.rearrange("p (a b) -> p a b", a=8))
        sq = self.small[:, 2048:4096].w(lambda ap: ap.rearrange("p (a b) -> p a b", a=8))
        for pc in range(NPIECE):
            which = 1 if pc == 0 else 0
            fw.dma(xt, src[pc].w(lambda ap: ap.rearrange("p (a b) -> p a b", a=8)))
            rstd = self.misc[:, 0:PW_]
            self.rms_rstd(xt, sq, PW_, rstd, self.pb[6][:, 0:PW_])
            for kc in range(8):
                tmp = self.misc[:, (1 + kc % 2) * PW_:(2 + kc % 2) * PW_]
                fw.tt(tmp, self._k(xt, kc), rstd, ALU.mult)
                fw.ts(self.H[:, kc, pc * PW_:(pc + 1) * PW_], tmp, self.gs[:, 0, kc, which:which + 1], ALU.mult,
                      self.mod[:, kc, which:which + 1], ALU.add, eng=("pool" if kc % 2 else "dve"))
        if ("h%d" % layer) in self.dbg:
            for kc in range(8):
                fw.copy(self.row(0), self.H[:, kc, :])
                self.dump_row("h%d" % layer, kc, 8, self.row(0))

    def inproj(self, layer, j, dst_row):
        fw = self.fw
        wt = self.next_wbf()
        self.load_w(wt.full(), self.win[layer * 28 + j])
        for ti, (c0, c1) in enumerate(TOK_TILES):
            ps = self.pb[ti % 2][:, 0:c1 - c0]
            for kc in range(8):
                fw.mm(ps, wt[:, kc, :], self.H[:, kc, c0:c1], start=(kc == 0), stop=(kc == 7))
            fw.copy(self.rowc(dst_row, c0, c1), ps, eng=("act" if ti % 2 == 0 else "dve"))

    def store_o(self, ch, row_view):
        self.fw.copy(self.obf.full(), row_view, eng="pool")
        self.fw.dma(self.od[ch], self.obf.full())

    def conv4(self, dst_row, src_row, wname, j, segs=SEGS):
        fw = self.fw
        o, w = PV_OFF[wname]
        wc = lambda k: self.pv[:, o + 4 * j + k:o + 4 * j + k + 1]
        fw.ts(self.row(dst_row), self.row(src_row), wc(2), ALU.mult)
        for (a, b) in segs:
            for k, sh in ((0, -2), (1, -1), (3, 1)):
                lo = max(a, a - sh)
                hi = min(b, b - sh)
                fw.stt(self.rowc(dst_row, lo, hi), self.rowc(src_row, lo + sh, hi + sh), wc(k),
                       self.rowc(dst_row, lo, hi), ALU.mult, ALU.add)

    def headstat(self, dst_row, src_row, eps_col, scale=1.0, square=True):
        fw = self.fw
        for ti, (c0, c1) in enumerate(TOK_TILES):
            n = c1 - c0
            sq = self.small[:, (ti % 2) * 512:(ti % 2) * 512 + n]
            fw.act(sq, self.rowc(src_row, c0, c1), AF.Square)
            ps = self.pb[2 + (ti % 2)][:, 0:n]
            fw.mm(ps, self.C(C_BO), sq)
            fw.act(self.rowc(dst_row, c0, c1), ps, AF.Ln, bias=self.epsc[:, eps_col:eps_col + 1], scale=scale)
        fw.act(self.row(dst_row), self.row(dst_row), AF.Exp, scale=-0.5)

    def neumann(self, P, nb, C, sign, nlev, XY, G, bf=False):
        fw = self.fw
        PX, PY, PG = self.pb[2], self.pb[3], self.pb[4]
        pv3 = lambda pbk: pbk[0:P, 0:nb * C].w(lambda ap: ap.rearrange("p (a b) -> p a b", b=C))
        identb = _bc(self.C(C_ID, 0, P, 0, C), [[0, nb], [1, C]])
        fw.tt(G[:, 0, :, :], identb, XY[:, 0, 1, :, :], ALU.add if sign > 0 else ALU.subtract)
        for p in range(1, nlev + 1):
            s, d = (p - 1) % 2, p % 2
            for u in range(nb):
                fw.mm(PX[0:P, u * C:(u + 1) * C], XY[:, s, 1, u, :], XY[:, s, 0, u, :])
            fw.copy(XY[:, d, 0, :, :], pv3(PX), eng="dve")
            if p < nlev:
                for u in range(nb):
                    fw.mm(PY[0:P, u * C:(u + 1) * C], XY[:, s, 0, u, :], XY[:, s, 1, u, :])
                fw.copy(XY[:, d, 1, :, :], pv3(PY), eng="act")
            for u in range(nb):
                fw.mm(PG[0:P, u * C:(u + 1) * C], XY[:, d, 0, u, :], G[:, s, u, :])
            fw.tt(G[:, d, :, :], pv3(PG), G[:, s, :, :], ALU.add)
        return nlev % 2

    def gdn_tables(self, layer):
        fw = self.fw
        gt = self.gt
        pba = self.pb[7][:, 0:432].w(lambda ap: ap.rearrange("p (a b) -> p a b", b=24))
        for i in range(18):
            for kc in range(8):
                fw.mm(self.pb[7][:, i * 24:(i + 1) * 24], self.H[:, kc, i * 128:(i + 1) * 128], self.wba[:, kc, :],
                      start=(kc == 0), stop=(kc == 7))
        braw = pba.w(lambda ap: ap[:, :, 0:12])
        araw = pba.w(lambda ap: ap[:, :, 12:24])
        T_G, T_LB, T_GC, T_GCL, T_NGC, T_B, T_BG, T_KD, T_EGL, T_TMP = range(10)
        one = self.epsc[:, 1:2]
        fw.act(gt[:, T_TMP], braw, AF.Exp, scale=-1.0)
        fw.act(gt[:, T_TMP], gt[:, T_TMP], AF.Ln, bias=one)
        fw.ts(gt[:, T_LB], gt[:, T_TMP], -1.0, ALU.mult)
        fw.tt(gt[:, T_TMP], araw, _bc(self.rowt[:, 12:24], [[0, 18], [1, 12]]), ALU.add)
        fw.act(gt[:, T_TMP], gt[:, T_TMP], AF.Exp)
        fw.act(gt[:, T_TMP], gt[:, T_TMP], AF.Ln, bias=one)
        na = self.misc[:, 0:12]
        fw.act(na, self.rowt[:, 0:12], AF.Exp)
        fw.stt(gt[:, T_G], gt[:, T_TMP], -1.0, _bc(na, [[0, 18], [1, 12]]), ALU.mult, ALU.mult)
        pg = self.pb[6][:, 0:216].w(lambda ap: ap.rearrange("p (a b) -> p a b", b=12))
        pl = self.pb[6][:, 256:472].w(lambda ap: ap.rearrange("p (a b) -> p a b", b=12))
        for i in range(18):
            for d in range(2):
                fw.mm(self.pb[6][:, i * 12 + d * 6:i * 12 + d * 6 + 6], self.C(C_UI if d == 0 else C_LI),
                      gt[:, T_G, i, d * 6:d * 6 + 6])
            fw.mm(self.pb[6][:, 256 + i * 12:256 + i * 12 + 12], self.C(C_ONES), gt[:, T_G, i, :])
        fw.copy(gt[:, T_GC], pg)
        fw.tt(gt[:, T_GCL], gt[:, T_GC], gt[:, T_LB], ALU.add)
        fw.ts(gt[:, T_NGC], gt[:, T_GC], -1.0, ALU.mult)
        fw.act(gt[:, T_B], gt[:, T_LB], AF.Exp)
        fw.act(gt[:, T_BG], gt[:, T_GCL], AF.Exp)
        fw.tt(gt[:, T_KD], pl, gt[:, T_GC], ALU.subtract)
        fw.act(gt[:, T_KD], gt[:, T_KD], AF.Exp)
        fw.act(gt[:, T_EGL], pl, AF.Exp)
        if ("gdn_g%d" % layer) in self.dbg:
            self.dump("gdn_g%d" % layer, gt[:, T_G], [128, 18, 12])
            self.dump("gdn_lb%d" % layer, gt[:, T_LB], [128, 18, 12])
            self.dump("gdn_gc%d" % layer, gt[:, T_GC], [128, 18, 12])

    def gdn(self, layer):
        fw = self.fw
        self.gdn_tables(layer)
        self.stop("gdn_t")
        T_G, T_LB, T_GC, T_GCL, T_NGC, T_B, T_BG, T_KD, T_EGL, T_TMP = range(10)
        gt = self.gt
        RQ, RK, RV, RT, RO0, RO1 = 0, 1, 2, 3, 4, 5
        base = 6 * TA
        WSt = self.WS
        gb = TA_(WSt, base, [128, 4, 128])
        lbb = TA_(WSt, base + 512, [128, 4, 128])
        dec = TA_(WSt, base + 1024, [128, 3, 4, 128])
        XY = TA_(WSt, base + 2560, [128, 2, 2, 4, 128])
        G = TA_(WSt, base + 4608, [128, 2, 4, 128])
        AT = TA_(WSt, base + 5632, [128, 2, 4, 128])
        TOK = TA_(WSt, base + 6656, [128, 2, 3, 4, 64])
        TOKB = TOK
        U = TA_(WSt, base + 8192, [128, 2, 4, 64])
        WT = TA_(WSt, base + 8704, [128, 2, 2, 128])
        QG = TA_(self.small, 0, [128, 2, 2, 128])
        EG = TA_(self.small, 512, [128, 4, 128])
        S = TA_(self.small, 1024, [128, 64])
        VN = TA_(self.small, 1088, [128, 2, 128])
        pb = self.pb
        for hp in range(3):
            for (j, dst) in ((hp, RQ), (3 + hp, RK), (6 + hp, RV)):
                self.inproj(layer, j, RT)
                self.conv4(dst, RT, "gconv", j)
                fw.act(self.row(dst), self.row(dst), AF.Silu)
            self.headstat(RT, RQ, 0)
            fw.stt(self.row(RQ), self.row(RQ), 0.125, self.row(RT), ALU.mult, ALU.mult)
            self.headstat(RT, RK, 0)
            fw.tt(self.row(RK), self.row(RK), self.row(RT), ALU.mult)
            if layer == 0 or True:
                self.dump_row("gdn_q%d" % layer, hp, 3, self.row(RQ))
                self.dump_row("gdn_k%d" % layer, hp, 3, self.row(RK))
                self.dump_row("gdn_v%d" % layer, hp, 3, self.row(RV))
            self.stop("gdn_p")
            for d in range(2):
                RO = RO0 + d
                fw.memset(S.full(), 0.0)
                order = GDN_ORDER[d]
                mincl = C_UI if d == 0 else C_LI
                m1 = C_PUI if d == 0 else C_PLI
                m2 = C_NLI if d == 0 else C_NUI
                m3 = C_NSL if d == 0 else C_NSU
                for b in range(9):
                    rr = b % 2
                    tiles = (order[2 * b], order[2 * b + 1])
                    units = [(hh, ts) for hh in range(2) for ts in range(2)]

                    def cidx(hh):
                        return d * 6 + 2 * hp + hh
                    for hh in range(2):
                        for ts in range(2):
                            i = tiles[ts]
                            u = hh * 2 + ts
                            ci = cidx(hh)
                            fw.copy(gb[:, u, :], _bc(gt[:, T_G, i, ci:ci + 1], [[0, 128]]), eng="pool")
                            fw.copy(lbb[:, u, :], _bc(gt[:, T_LB, i, ci:ci + 1], [[0, 128]]), eng="pool")
                    for u, (hh, ts) in enumerate(units):
                        sl = slice(u * 128, (u + 1) * 128)
                        fw.mm(pb[2][:, sl], gb[:, u, :], self.C(mincl), start=True, stop=False)
                        fw.mm(pb[2][:, sl], self.C(C_ID), self.C(m1), start=False, stop=True)
                        fw.mm(pb[3][:, sl], gb[:, u, :], self.C(mincl), start=True, stop=False)
                        fw.mm(pb[3][:, sl], lbb[:, u, :], self.C(C_ID), start=False, stop=False)
                        fw.mm(pb[3][:, sl], self.C(C_ID), self.C(m2), start=False, stop=True)
                        fw.mm(pb[4][:, sl], gb[:, u, :], self.C(mincl), start=True, stop=False)
                        fw.mm(pb[4][:, sl], self.C(C_ID), self.C(m3), start=False, stop=True)
                        fw.mm(pb[5][:, sl], gb[:, u, :], self.C(mincl), start=True, stop=True)
                    for u, (hh, ts) in enumerate(units):
                        i = tiles[ts]
                        sl = slice(u * 128, (u + 1) * 128)
                        ci = cidx(hh)
                        fw.act(dec[:, 0, u, :], pb[2][:, sl], AF.Exp, bias=gt[:, T_GCL, i, ci:ci + 1], scale=-1.0)
                        fw.act(dec[:, 1, u, :], pb[3][:, sl], AF.Exp, bias=gt[:, T_NGC, i, ci:ci + 1], scale=1.0)
                        fw.act(dec[:, 2, u, :], pb[4][:, sl], AF.Exp, bias=gt[:, T_NGC, i, ci:ci + 1], scale=1.0)
                    fw.act(EG.full(), pb[5][:, 0:512].w(lambda ap: ap.rearrange("p (a b) -> p a b", b=128)), AF.Exp)
                    for u, (hh, ts) in enumerate(units):
                        i = tiles[ts]
                        hb = hh * 64
                        cs = (i * 128, (i + 1) * 128)
                        kT = self.rowc(RK, cs[0], cs[1], hb, hb + 64)
                        qT = self.rowc(RQ, cs[0], cs[1], hb, hb + 64)
                        vT = self.rowc(RV, cs[0], cs[1], hb, hb + 64)
                        idn = self.C(C_ID, hb, hb + 64, hb, hb + 64)
                        fw.mm(pb[6 + hh][:, ts * 128:(ts + 1) * 128], kT, kT)
                        fw.mm(pb[6 + hh][:, 256 + ts * 128:256 + (ts + 1) * 128], kT, qT)
                        fw.tr(pb[hh][:, ts * 64:(ts + 1) * 64], kT, idn)
                        fw.tr(pb[hh][:, 128 + ts * 64:128 + (ts + 1) * 64], vT, idn)
                    for hh in range(2):
                        kk3 = pb[6 + hh][:, 0:256].w(lambda ap: ap.rearrange("p (a b) -> p a b", b=128))
                        kq3 = pb[6 + hh][:, 256:512].w(lambda ap: ap.rearrange("p (a b) -> p a b", b=128))
                        us = slice(hh * 2, hh * 2 + 2)
                        fw.tt(XY[:, 0, 0, us, :], kk3, dec[:, 0, us, :], ALU.mult)
                        fw.tt(XY[:, 0, 1, us, :], kk3, dec[:, 1, us, :], ALU.mult)
                        fw.tt(AT[:, rr, us, :], kq3, dec[:, 2, us, :], ALU.mult)
                    for u, (hh, ts) in enumerate(units):
                        i = tiles[ts]
                        ci = cidx(hh)
                        ktk = pb[hh][:, ts * 64:(ts + 1) * 64]
                        vtk = pb[hh][:, 128 + ts * 64:128 + (ts + 1) * 64]
                        fw.ts(TOKB[:, rr, 0, u, :], vtk, gt[:, T_B, i, ci:ci + 1], ALU.mult)
                        fw.ts(TOKB[:, rr, 1, u, :], ktk, gt[:, T_BG, i, ci:ci + 1], ALU.mult)
                        fw.ts(TOK[:, rr, 2, u, :], ktk, gt[:, T_KD, i, ci:ci + 1], ALU.mult)
                    for u, (hh, ts) in enumerate(units):
                        i = tiles[ts]
                        hb = hh * 64
                        fw.tt(QG[hb:hb + 64, rr, ts, :], self.rowc(RQ, i * 128, (i + 1) * 128, hb, hb + 64),
                              EG[hb:hb + 64, u, :], ALU.mult, eng="pool")
                    gi = self.neumann(128, 4, 128, -1, 6, XY, G)
                    for u, (hh, ts) in enumerate(units):
                        hb = hh * 64
                        fw.mm(pb[5][:, u * 64:(u + 1) * 64], G[:, gi, u, :], TOKB[:, rr, 0, u, :])
                        fw.mm(pb[6][hb:hb + 64, ts * 128:(ts + 1) * 128], TOKB[:, rr, 1, u, :], G[:, gi, u, :])
                    fw.copy(U[:, rr, :, :], pb[5][:, 0:256].w(lambda ap: ap.rearrange("p (a b) -> p a b", b=64)), eng="act")
                    fw.copy(WT[:, rr, :, :], pb[6][:, 0:256].w(lambda ap: ap.rearrange("p (a b) -> p a b", b=128)), eng="dve")
                    self.stop("gdn_b0")
                    for ts in range(2):
                        i = tiles[ts]
                        vr = ts
                        for hh in range(2):
                            hb = hh * 64
                            fw.mm(pb[hh][:, 0:64], WT[hb:hb + 64, rr, ts, :], S[hb:hb + 64, :])
                        for hh in range(2):
                            fw.tt(VN[:, vr, hh * 64:(hh + 1) * 64], U[:, rr, hh * 2 + ts, :], pb[hh][:, 0:64], ALU.subtract)
                        self.stop("gdn_ra")
                        for hh in range(2):
                            hb = hh * 64
                            u = hh * 2 + ts
                            fw.mm(pb[2][hb:hb + 64, 0:128], S[hb:hb + 64, :], QG[hb:hb + 64, rr, ts, :])
                            fw.mm(pb[3][hb:hb + 64, 0:128], VN[:, vr, hh * 64:(hh + 1) * 64], AT[:, rr, u, :])
                        fw.copy(self.rowc(RO, i * 128, (i + 1) * 128), pb[2][:, 0:128], eng="act")
                        fw.tt(self.rowc(RO, i * 128, (i + 1) * 128), self.rowc(RO, i * 128, (i + 1) * 128), pb[3][:, 0:128], ALU.add)
                        self.stop("gdn_rb")
                        for hh in range(2):
                            hb = hh * 64
                            u = hh * 2 + ts
                            fw.mm(pb[4][hb:hb + 64, 0:64], TOK[:, rr, 2, u, :], VN[:, vr, hh * 64:(hh + 1) * 64])
                        self.stop("gdn_rc")
                        for hh in range(2):
                            hb = hh * 64
                            ci = cidx(hh)
                            fw.stt(S[hb:hb + 64, :], S[hb:hb + 64, :], gt[hb:hb + 64, T_EGL, i, ci:ci + 1],
                                   pb[4][hb:hb + 64, 0:64], ALU.mult, ALU.add)
                    self.stop("gdn_r0")
                self.stop("gdn_d0")
            fw.tt(self.row(RO0), self.row(RO0), self.row(RO1), ALU.add)
            self.dump_row("gdn_o%d" % layer, hp, 3, self.row(RO0))
            self.headstat(RT, RO0, 0, scale=1.0 / 64.0)
            fw.stt(self.row(RO0), self.row(RO0), self.pvc("gnorm"), self.row(RT), ALU.mult, ALU.mult)
            self.inproj(layer, 9 + hp, RQ)
            fw.act(self.row(RQ), self.row(RQ), AF.Silu)
            fw.tt(self.row(RO0), self.row(RO0), self.row(RQ), ALU.mult)
            self.dump_row("mix%d" % layer, hp, 8, self.row(RO0))
            self.store_o(hp, self.row(RO0))

    def lru(self, layer):
        fw = self.fw
        RT, RP, RX, RA, RB, RH, RG, RI = 0, 1, 2, 3, 4, 5, 6, 7
        o, w = PV_OFF["llam"]
        c8 = self.misc[:, 1024:1028]
        c16 = self.misc[:, 1028:1032]
        fw.act(c8, self.pv[:, o:o + 4], AF.Exp, scale=-1.0)
        fw.act(c8, c8, AF.Ln, bias=self.epsc[:, 1:2])
        fw.ts(c16, c8, -16.0, ALU.mult)
        fw.ts(c8, c8, -8.0, ALU.mult)
        NL = TA - CTXN
        for jc in range(2):
            self.inproj(layer, 12 + jc, RT)
            fw.copy(self.rowc(RP, 0, CTXN), self.rowc(RT, 0, CTXN), eng="pool")
            fw.copy(self.rowc(RP, CTXN, TA), _bc(self.rowc(RT, CTXN, TA), [[1, 64], [64, 32]]), eng="pool")
            self.conv4(RX, RP, "lconv", jc)
            fw.ts(self.row(RX), self.row(RX), self.pvc("lconvb", jc), ALU.add)
            for d in range(2):
                for ti, (c0, c1) in enumerate(TOK_TILES):
                    n = c1 - c0
                    pa = self.pb[2 + ti % 2][:, 0:n]
                    px = self.pb[4 + ti % 2][:, 0:n]
                    fw.mm(pa, self.lwt[:, 0 * 4 + d * 2 + jc, :], self.rowc(RX, c0, c1))
                    fw.mm(px, self.lwt[:, 1 * 4 + d * 2 + jc, :], self.rowc(RX, c0, c1))
                    fw.act(self.rowc(RA, c0, c1), pa, AF.Sigmoid, bias=self.pvc("lba", d * 2 + jc))
                    fw.act(self.rowc(RI, c0, c1), px, AF.Sigmoid, bias=self.pvc("lbx", d * 2 + jc))
                k = d * 2 + jc
                fw.act(self.row(RB), self.row(RA), AF.Exp, scale=self.misc[:, 1028 + k:1029 + k])
                fw.ts(self.row(RB), self.row(RB), -1.0, ALU.mult, 1.0, ALU.add)
                fw.act(self.row(RB), self.row(RB), AF.Sqrt)
                fw.act(self.row(RA), self.row(RA), AF.Exp, scale=self.misc[:, 1024 + k:1025 + k])
                fw.tt(self.row(RI), self.row(RI), self.row(RX), ALU.mult, eng="pool")
                fw.tt(self.row(RB), self.row(RB), self.row(RI), ALU.mult)
                dst = RH if d == 0 else RG
                if d == 0:
                    fw.scan(self.row(dst), self.row(RA), self.row(RB), 0.0)
                else:
                    fw.scan(self.rowc(dst, 0, CTXN).w(lambda ap: _rev(ap, CTXN)),
                            self.rowc(RA, 0, CTXN).w(lambda ap: _rev(ap, CTXN)),
                            self.rowc(RB, 0, CTXN).w(lambda ap: _rev(ap, CTXN)), 0.0)
                    fw.scan(self.rowc(dst, CTXN, TA).w(lambda ap: _rev(ap, NL)),
                            self.rowc(RA, CTXN, TA).w(lambda ap: _rev(ap, NL)),
                            self.rowc(RB, CTXN, TA).w(lambda ap: _rev(ap, NL)), self.rowc(dst, 0, 1))
            fw.tt(self.row(RH), self.row(RH), self.row(RG), ALU.add)
            fw.copy(self.rowc(RP, 0, CTXN), self.rowc(RH, 0, CTXN), eng="pool")
            fw.copy(self.rowc(RP, CTXN, TA), _bc(self.rowc(RH, CTXN, TA), [[1, 32], [32, 64]]), eng="pool")
            self.inproj(layer, 14 + jc, RG)
            fw.act(self.row(RT), self.row(RG), AF.Square)
            fw.ts(self.row(RT), self.row(RT), 0.044715, ALU.mult, 1.0, ALU.add)
            fw.tt(self.row(RT), self.row(RT), self.row(RG), ALU.mult)
            fw.act(self.row(RT), self.row(RT), AF.Sigmoid, scale=1.5957691216057308)
            fw.tt(self.row(RT), self.row(RT), self.row(RG), ALU.mult)
            fw.tt(self.row(RP), self.row(RP), self.row(RT), ALU.mult)
            self.dump_row("mix%d" % layer, 3 + jc, 8, self.row(RP))
            self.store_o(3 + jc, self.row(RP))

    def phase_c(self, layer, last):
        fw = self.fw
        WSt = self.WS
        NS = 768
        hid = TA_(WSt, 0, [128, 32, NS], dtype=BF16)
        Y = TA_(WSt, 12288, [128, 8, NS])
        wdnb = [TA_(WSt, 18432 + 2048 * i, [128, 32, 128], dtype=BF16) for i in range(2)]
        xt = self.small[:, 0:2048].w(lambda ap: ap.rearrange("p (a b) -> p a b", a=8))
        sq = self.small[:, 2048:4096].w(lambda ap: ap.rearrange("p (a b) -> p a b", a=8))
        src = self.xin if layer == 0 else self.xr
        self.wcache = fw.dram("wcache%d" % layer, [72, 128, 1024], BF16)
        for st in range(3):
            first = (st == 0)
            pcs = [pc for pc in range(3 * st, 3 * st + 3) if not (last and pc == 0)]
            base = pcs[0] * PW_
            ncol = len(pcs) * PW_
            nt = []
            c = pcs[0] * PW_
            end = (pcs[-1] + 1) * PW_
            while c < end:
                n = min(512, end - c)
                if c < CTXN:
                    n = min(n, CTXN - c)
                nt.append((c - base, c - base + n))
                c += n
            Ot = self.H[:, :, 0:ncol]
            fw.dma(Ot, self.od[:, :, base:base + ncol].w(lambda ap: ap.rearrange("k p t -> p k t")))
            for m in range(8):
                wt = self.next_wbf()
                self.load_wc(wt.full(), self.wout[layer * 8 + m], m, first)
                for ti, (a, b) in enumerate(nt):
                    ps = self.pb[ti % 2][:, 0:b - a]
                    for kc in range(8):
                        fw.mm(ps, wt[:, kc, :], self.H[:, kc, a:b], start=(kc == 0), stop=(kc == 7))
                    fw.copy(Y[:, m, a:b], ps, eng=("act" if ti % 2 == 0 else "dve"))
            for pi, pc in enumerate(pcs):
                which = 1 if pc == 0 else 0
                lo = pi * PW_
                fw.dma(xt, src[pc].w(lambda ap: ap.rearrange("p (a b) -> p a b", a=8)))
                rstd = self.misc[:, 0:PW_]
                self.rms_rstd(Y[:, :, lo:lo + PW_], sq, PW_, rstd, self.pb[6][:, 0:PW_])
                for kc in range(8):
                    tmp = self.misc[:, (1 + kc % 2) * PW_:(2 + kc % 2) * PW_]
                    fw.tt(tmp, Y[:, kc, lo:lo + PW_], rstd, ALU.mult)
                    fw.stt(self._k(xt, kc), tmp, self.gs[:, 1, kc, which:which + 1], self._k(xt, kc), ALU.mult, ALU.add)
                if ("xmid%d" % layer) in self.dbg:
                    if ("xmid%d" % layer) not in self.dbg_out:
                        self.dbg_out["xmid%d" % layer] = fw.dram("dbg_xmid%d" % layer, [NPIECE, 128, 8 * PW_], F32, kind="ExternalOutput")
                    self.final.append(fw.dma(self.dbg_out["xmid%d" % layer][pc].w(lambda ap: ap.rearrange("p (a b) -> p a b", a=8)), xt))
                fw.dma(self.xr[pc].w(lambda ap: ap.rearrange("p (a b) -> p a b", a=8)), xt)
                rstd2 = self.misc[:, 3 * PW_:4 * PW_]
                self.rms_rstd(xt, sq, PW_, rstd2, self.pb[6][:, 0:PW_])
                for kc in range(8):
                    tmp = self.misc[:, (1 + kc % 2) * PW_:(2 + kc % 2) * PW_]
                    fw.tt(tmp, self._k(xt, kc), rstd2, ALU.mult)
                    fw.ts(self.H[:, kc, NS + lo:NS + lo + PW_], tmp, self.gs[:, 2, kc, which:which + 1], ALU.mult,
                          self.mod[:, 24 + kc, which:which + 1], ALU.add, eng=("pool" if kc % 2 else "dve"))
            for m in range(32):
                wt = self.next_wbf()
                self.load_wc(wt.full(), self.wup[layer * 32 + m], 8 + m, first)
                for ti, (a, b) in enumerate(nt):
                    ps = self.pb[ti % 2][:, 0:b - a]
                    for kc in range(8):
                        fw.mm(ps, wt[:, kc, :], self.H[:, kc, NS + a:NS + b], start=(kc == 0), stop=(kc == 7))
                    rl = self.small[:, 0:b - a] if ti % 2 == 0 else self.small[:, 512:512 + b - a]
                    fw.act(rl, ps, AF.Relu)
                    fw.tt(hid[:, m, a:b], rl, rl, ALU.mult, eng=("pool" if ti % 2 == 0 else "dve"))
            for m in range(8):
                wt = wdnb[m % 2]
                for kq in range(4):
                    self.load_wc(wt[:, 8 * kq:8 * kq + 8, :], self.wdn[layer * 8 + m, :, 1024 * kq:1024 * kq + 1024], 40 + 4 * m + kq, first)
                for ti, (a, b) in enumerate(nt):
                    ps = self.pb[ti % 2][:, 0:b - a]
                    for kc in range(32):
                        fw.mm(ps, wt[:, kc, :], hid[:, kc, a:b], start=(kc == 0), stop=(kc == 31))
                    fw.copy(Y[:, m, a:b], ps, eng=("act" if ti % 2 == 0 else "dve"))
            for pi, pc in enumerate(pcs):
                which = 1 if pc == 0 else 0
                lo = pi * PW_
                fw.dma(xt, self.xr[pc].w(lambda ap: ap.rearrange("p (a b) -> p a b", a=8)))
                rstd = self.misc[:, 0:PW_]
                self.rms_rstd(Y[:, :, lo:lo + PW_], sq, PW_, rstd, self.pb[6][:, 0:PW_])
                for kc in range(8):
                    tmp = self.misc[:, (1 + kc % 2) * PW_:(2 + kc % 2) * PW_]
                    fw.tt(tmp, Y[:, kc, lo:lo + PW_], rstd, ALU.mult)
                    fw.stt(self._k(xt, kc), tmp, self.gs[:, 3, kc, which:which + 1], self._k(xt, kc), ALU.mult, ALU.add)
                if last:
                    self.final.append(fw.dma(self.out_d[pc - 1].w(lambda ap: ap.rearrange("p (a b) -> p a b", a=8)), xt))
                else:
                    fw.dma(self.xr[pc].w(lambda ap: ap.rearrange("p (a b) -> p a b", a=8)), xt)

    def lerp(self, dst_row, raw_row, muidx):
        fw = self.fw
        for (a, b) in SEGS:
            fw.tt(self.rowc(dst_row, a + 1, b - 1), self.rowc(raw_row, a, b - 2), self.rowc(raw_row, a + 2, b), ALU.add)
            fw.copy(self.rowc(dst_row, a, a + 1), self.rowc(raw_row, a + 1, a + 2), eng="pool")
            fw.copy(self.rowc(dst_row, b - 1, b), self.rowc(raw_row, b - 2, b - 1), eng="pool")
        fw.ts(self.row(dst_row), self.row(dst_row), self.misc[:, 1040 + 12 + muidx:1040 + 13 + muidx], ALU.mult)
        fw.stt(self.row(dst_row), self.row(raw_row), self.misc[:, 1040 + muidx:1040 + muidx + 1], self.row(dst_row), ALU.mult, ALU.add)

    def rwkv(self, layer):
        fw = self.fw
        pb = self.pb
        RR, RV, RKK, RKA, RCW, RKD, RY, RBON, T1, T2 = range(10)
        o, w = PV_OFF["mu"]
        fw.ts(self.misc[:, 1040 + 0:1040 + 12], self.pv[:, o:o + 12], -1.0, ALU.mult, 1.0, ALU.add)
        fw.ts(self.misc[:, 1040 + 12:1040 + 24], self.pv[:, o:o + 12], 0.5, ALU.mult)
        o2, w2 = PV_OFF["ka"]
        fw.ts(self.misc[:, 1040 + 24:1040 + 27], self.pv[:, o2:o2 + 3], -1.0, ALU.mult, 1.0, ALU.add)
        spill = self.fw.dram("spill%d" % layer, [4, 128, TA], F32)
        WSt = self.WS
        bT = T1 * TA
        DER = TA_(WSt, bT, [128, 6, 4, 64])
        E1 = TA_(WSt, bT + 1536, [128, 4, 64])
        E2 = TA_(WSt, bT + 1792, [128, 4, 64])
        XY = TA_(WSt, bT + 2048, [64, 2, 2, 8, 64], dtype=BF16)
        ATKB = TA_(WSt, bT + 3072, [64, 8, 64], dtype=BF16)
        W1B = TA_(WSt, bT + 3328, [64, 8, 64], dtype=BF16)
        PC = TA_(WSt, bT + 4096, [128, 4])
        M = TA_(WSt, bT + 4104, [128, 64])
        US = TA_(WSt, bT + 4168, [64, 128])
        AHT = TA_(WSt, bT + 4296, [128, 4, 64])
        G = TA_(self.small, 0, [64, 2, 8, 64], dtype=BF16)
        MK = TA_(self.small, 1024, [64, 3, 8, 64])
        TK = TA_(self.small, 2560, [64, 3, 8, 64])
        VT = TA_(self.misc, 0, [64, 8, 64])
        W1 = TA_(self.misc, 512, [64, 8, 64])
        UV = TA_(self.misc, 1200, [64, 8, 64])
        p3 = lambda pbk: pbk[0:64, 0:512].w(lambda ap: ap.rearrange("p (a b) -> p a b", b=64))
        self.inproj(layer, 26, T1)
        self.lerp(T2, T1, 10)
        fw.dma(spill[1], self.row(T2))
        self.inproj(layer, 25, T1)
        self.lerp(T2, T1, 9)
        fw.act(self.row(T2), self.row(T2), AF.Tanh)
        fw.dma(spill[2], self.row(T2))
        self.inproj(layer, 27, T1)
        self.lerp(T2, T1, 11)
        fw.act(self.row(T2), self.row(T2), AF.Sigmoid)
        fw.dma(spill[3], self.row(T2))
        for hp in range(3):
            self.inproj(layer, 16 + hp, T1)
            self.lerp(RR, T1, hp)
            self.inproj(layer, 22 + hp, T1)
            self.lerp(RV, T1, 6 + hp)
            self.inproj(layer, 19 + hp, T1)
            self.lerp(T2, T1, 3 + hp)
            fw.dma(spill[0], self.row(T2))
            fw.ts(self.row(T1), self.row(T2), self.pvc("kk", hp), ALU.mult)
            self.headstat(RKK, T1, 0)
            fw.tt(self.row(RKK), self.row(RKK), self.row(T1), ALU.mult)
            for d in range(2):
                db = d * 64
                fw.dma(self.row(T2), spill[1])
                for ti, (c0, c1) in enumerate(TOK_TILES):
                    ps = pb[2 + ti % 2][:, 0:c1 - c0]
                    fw.mm(ps, self.rup[db:db + 64, 1, hp * 128:(hp + 1) * 128], self.rowc(T2, c0, c1, db, db + 64))
                    fw.act(self.rowc(RKA, c0, c1), ps, AF.Sigmoid, bias=self.pvc("a0", d * 3 + hp))
                fw.dma(self.row(T2), spill[0])
                fw.ts(self.row(T1), self.row(RKA), self.pvc("ka", hp), ALU.mult, self.misc[:, 1040 + 24 + hp:1040 + 25 + hp], ALU.add)
                fw.tt(self.row(RKD), self.row(T2), self.row(T1), ALU.mult)
                fw.tt(self.row(RKA), self.row(RKA), self.row(RKK), ALU.mult)
                if d == 0:
                    fw.stt(self.row(RBON), self.row(RR), self.pvc("rk", hp), self.row(RKD), ALU.mult, ALU.mult)
                else:
                    fw.stt(self.row(T1), self.row(RR), self.pvc("rk", hp), self.row(RKD), ALU.mult, ALU.mult)
                    fw.tt(self.row(RBON), self.row(RBON), self.row(T1), ALU.add)
                fw.dma(self.row(T2), spill[2])
                for ti, (c0, c1) in enumerate(TOK_TILES):
                    ps = pb[2 + ti % 2][:, 0:c1 - c0]
                    fw.mm(ps, self.rup[db:db + 64, 0, hp * 128:(hp + 1) * 128], self.rowc(T2, c0, c1, db, db + 64))
                    fw.act(self.rowc(T1, c0, c1), ps, AF.Sigmoid, bias=self.pvc("w0", d * 3 + hp))
                fw.ts(self.row(T1), self.row(T1), -0.6065306597126334, ALU.mult)
                onesb = _bc(self.epsc[:, 1:2], [[0, TA]])
                r3 = lambda ap: ap.rearrange("p (a b) -> p a b", b=64)
                if d == 0:
                    fw.scan(self.row(T2), onesb, self.row(T1), 0.0)
                    fw.tt(self.rowc(RCW, 64, TA).w(r3), self.rowc(T2, 64, TA).w(r3),
                          _bc(self.rowc(T2, 63, 64), [[64, 35], [0, 64]]), ALU.subtract)
                    fw.copy(self.rowc(RCW, 0, 64), self.rowc(T2, 0, 64), eng="pool")
                else:
                    fw.scan(self.row(T2).w(lambda ap: _rev(ap, TA)), onesb, self.row(T1).w(lambda ap: _rev(ap, TA)), 0.0)
                    fw.tt(self.rowc(RCW, 0, TA - 64).w(r3), self.rowc(T2, 0, TA - 64).w(r3),
                          _bc(self.rowc(T2, 64, 65), [[64, 35], [0, 64]]), ALU.subtract)
                    fw.copy(self.rowc(RCW, TA - 64, TA), self.rowc(T2, TA - 64, TA), eng="pool")
                if d == 0 and hp == 0:
                    self.dump_row("rw_cw%d" % layer, 0, 1, self.row(RCW))
                    self.dump_row("rw_kd%d" % layer, 0, 1, self.row(RKD))
                    self.dump_row("rw_ka%d" % layer, 0, 1, self.row(RKA))
                fw.memset(M.full(), 0.0)
                sgn_strict_ts = C_SL if d == 0 else C_SU
                sgn_strict_st = C_SU if d == 0 else C_SL
                incl_st = C_UI if d == 0 else C_LI
                for b in range(9):
                    if d == 0:
                        w0c = 256 * b
                    else:
                        w0c = 0 if b == 0 else TA - 256 * b
                    win = lambda r: self.rowc(r, w0c, w0c + 256).w(lambda ap: ap.rearrange("p (a b) -> p a b", b=64))
                    lastp = 63 if d == 0 else 0
                    cwl = _bc(self.rowc(RCW, w0c + lastp, w0c + lastp + 1), [[64, 4], [0, 64]])
                    fw.act(E1.full(), win(RCW), AF.Exp)
                    fw.act(E2.full(), win(RCW), AF.Exp, scale=-1.0)
                    fw.tt(DER[:, 3], win(RR), E1.full(), ALU.mult)
                    fw.stt(DER[:, 1], win(RKA), -1.0, E2.full(), ALU.mult, ALU.mult)
                    fw.tt(DER[:, 2], win(RKD), E2.full(), ALU.mult, eng="pool")
                    if d == 0:
                        fw.tt(DER[:, 0, :, 1:64], self.rowc(RKK, w0c, w0c + 256).w(lambda ap: ap.rearrange("p (a b) -> p a b", b=64)[:, :, 1:64]),
                              E1[:, :, 0:63], ALU.mult)
                        fw.copy(DER[:, 0, :, 0:1], self.rowc(RKK, w0c, w0c + 256).w(lambda ap: ap.rearrange("p (a b) -> p a b", b=64)[:, :, 0:1]), eng="pool")
                    else:
                        fw.tt(DER[:, 0, :, 0:63], self.rowc(RKK, w0c, w0c + 256).w(lambda ap: ap.rearrange("p (a b) -> p a b", b=64)[:, :, 0:63]),
                              E1[:, :, 1:64], ALU.mult)
                        fw.copy(DER[:, 0, :, 63:64], self.rowc(RKK, w0c, w0c + 256).w(lambda ap: ap.rearrange("p (a b) -> p a b", b=64)[:, :, 63:64]), eng="pool")
                    fw.tt(E2.full(), cwl, win(RCW), ALU.subtract)
                    fw.act(E2.full(), E2.full(), AF.Exp)
                    fw.tt(DER[:, 4], win(RKD), E2.full(), ALU.mult, eng="pool")
                    fw.stt(DER[:, 5], win(RKA), -1.0, E2.full(), ALU.mult, ALU.mult)
                    fw.act(PC.full(), _bc(self.rowc(RCW, w0c + lastp, w0c + lastp + 1), [[64, 4]]), AF.Exp)
                    wcs = [cs if d == 0 else 3 - cs for cs in range(4)]
                    units = [(cs, hh) for cs in range(4) for hh in range(2)]
                    fm = lambda q, wc, hb: DER[hb:hb + 64, q, wc, :]
                    for u, (cs, hh) in sorted(enumerate(units), key=lambda t: (t[1][1], t[1][0])):
                        wc, hb = wcs[cs], hh * 64
                        sl = slice(u * 64, (u + 1) * 64)
                        fw.mm(pb[5][0:64, sl], fm(0, wc, hb), fm(1, wc, hb))
                        fw.mm(pb[6][0:64, sl], fm(1, wc, hb), fm(0, wc, hb))
                        fw.mm(pb[7][0:64, sl], fm(2, wc, hb), fm(0, wc, hb))
                    mk = lambda ci: _bc(self.C(ci, 0, 64, 0, 64), [[0, 8], [1, 64]])
                    fw.tt(XY[:, 0, 0, :, :], p3(pb[5]), mk(sgn_strict_ts), ALU.mult)
                    fw.tt(XY[:, 0, 1, :, :], p3(pb[6]), mk(sgn_strict_st), ALU.mult)
                    fw.tt(MK[:, 0, :, :], p3(pb[7]), mk(sgn_strict_st), ALU.mult)
                    for u, (cs, hh) in sorted(enumerate(units), key=lambda t: (t[1][1], t[1][0])):
                        wc, hb = wcs[cs], hh * 64
                        sl = slice(u * 64, (u + 1) * 64)
                        fw.mm(pb[5][0:64, sl], fm(1, wc, hb), fm(3, wc, hb))
                        fw.mm(pb[6][0:64, sl], fm(2, wc, hb), fm(3, wc, hb))
                    fw.tt(MK[:, 1, :, :], p3(pb[5]), mk(incl_st), ALU.mult)
                    fw.tt(MK[:, 2, :, :], p3(pb[6]), mk(incl_st), ALU.mult)
                    for u, (cs, hh) in sorted(enumerate(units), key=lambda t: (t[1][1], t[1][0])):
                        wc, hb = wcs[cs], hh * 64
                        sl = slice(u * 64, (u + 1) * 64)
                        idn = self.C(C_ID, hb, hb + 64, hb, hb + 64)
                        fw.tr(pb[0][0:64, sl], fm(0, wc, hb), idn)
                        fw.tr(pb[1][0:64, sl], fm(4, wc, hb), idn)
                        fw.tr(pb[7][0:64, sl], fm(5, wc, hb), idn)
                        c0 = w0c + wc * 64
                        fw.tr(pb[5][0:64, sl], self.rowc(RV, c0, c0 + 64, hb, hb + 64), idn)
                    fw.copy(ATKB.full(), p3(pb[0]), eng="act")
                    fw.copy(TK[:, 1, :, :], p3(pb[1]), eng="dve")
                    fw.copy(TK[:, 2, :, :], p3(pb[7]), eng="act")
                    fw.copy(VT.full(), p3(pb[5]), eng="dve")
                    gi = self.neumann(64, 8, 64, +1, 5, XY, G, bf=True)
                    for u in range(8):
                        fw.mm(pb[5][0:64, u * 64:(u + 1) * 64], MK[:, 0, u, :], VT[:, u, :])
                    fw.copy(W1B.full(), p3(pb[5]), eng="act")
                    for u, (cs, hh) in sorted(enumerate(units), key=lambda t: (t[1][1], t[1][0])):
                        hb = hh * 64
                        fw.mm(pb[6][0:64, u * 64:(u + 1) * 64], G[:, gi, u, :], W1B[:, u, :])
                        fw.mm(pb[7][hb:hb + 64, cs * 64:(cs + 1) * 64], ATKB[:, u, :], G[:, gi, u, :])
                    fw.copy(UV.full(), p3(pb[6]), eng="dve")
                    fw.copy(AHT.full(), pb[7][:, 0:256].w(lambda ap: ap.rearrange("p (a b) -> p a b", b=64)), eng="act")
                    for cs in range(4):
                        wc = wcs[cs]
                        c0 = w0c + wc * 64
                        for hh in range(2):
                            hb = hh * 64
                            fw.mm(pb[0][0:64, hh * 64:(hh + 1) * 64], AHT[hb:hb + 64, cs, :], M[hb:hb + 64, :])
                        fw.tt(US.full(), UV[:, 2 * cs:2 * cs + 2, :].w(lambda ap: ap.rearrange("p a b -> p (a b)")), pb[0][0:64, 0:128], ALU.add)
                        for hh in range(2):
                            hb = hh * 64
                            u = cs * 2 + hh
                            fw.mm(pb[1][hb:hb + 64, 0:64], M[hb:hb + 64, :], fm(3, wc, hb), start=True, stop=False)
                            fw.mm(pb[1][hb:hb + 64, 0:64], US[:, hh * 64:(hh + 1) * 64], MK[:, 1, u, :], start=False, stop=False)
                            fw.mm(pb[1][hb:hb + 64, 0:64], VT[:, u, :], MK[:, 2, u, :], start=False, stop=True)
                        if d == 0:
                            fw.copy(self.rowc(RY, c0, c0 + 64), pb[1][:, 0:64], eng="act")
                        else:
                            fw.tt(self.rowc(RY, c0, c0 + 64), self.rowc(RY, c0, c0 + 64), pb[1][:, 0:64], ALU.add)
                        for hh in range(2):
                            hb = hh * 64
                            u = cs * 2 + hh
                            fw.mm(pb[0][hb:hb + 64, 256:320], TK[:, 1, u, :], VT[:, u, :], start=True, stop=False)
                            fw.mm(pb[0][hb:hb + 64, 256:320], TK[:, 2, u, :], US[:, hh * 64:(hh + 1) * 64], start=False, stop=True)
                        fw.stt(M.full(), M.full(), PC[:, wc:wc + 1], pb[0][:, 256:320], ALU.mult, ALU.add)
            self.dump_row("rw_y%d" % layer, hp, 3, self.row(RY))
            for ti, (c0, c1) in enumerate(TOK_TILES):
                n = c1 - c0
                ps = pb[2 + ti % 2][:, 0:n]
                fw.mm(ps, self.C(C_BO), self.rowc(RY, c0, c1))
                fw.stt(self.rowc(T1, c0, c1), ps, -1.0 / 64.0, self.rowc(RY, c0, c1), ALU.mult, ALU.add)
            self.headstat(T2, T1, 2, scale=1.0 / 64.0)
            fw.tt(self.row(T1), self.row(T1), self.row(T2), ALU.mult)
            fw.ts(self.row(T1), self.row(T1), self.pvc("gnw", hp), ALU.mult, self.pvc("gnb", hp), ALU.add)
            for ti, (c0, c1) in enumerate(TOK_TILES):
                n = c1 - c0
                ps = pb[2 + ti % 2][:, 0:n]
                fw.mm(ps, self.C(C_BO), self.rowc(RBON, c0, c1))
                fw.tt(self.rowc(T2, c0, c1), ps, self.rowc(RV, c0, c1), ALU.mult)
            fw.tt(self.row(T1), self.row(T1), self.row(T2), ALU.add)
            fw.dma(self.row(T2), spill[3])
            for ti, (c0, c1) in enumerate(TOK_TILES):
                n = c1 - c0
                ps = pb[2 + ti % 2][:, 0:n]
                fw.mm(ps, self.rup[:, 2, hp * 128:(hp + 1) * 128], self.rowc(T2, c0, c1))
                fw.tt(self.rowc(T1, c0, c1), self.rowc(T1, c0, c1), ps, ALU.mult)
            self.dump_row("mix%d" % layer, 5 + hp, 8, self.row(T1))
            self.store_o(5 + hp, self.row(T1))


def kernel(**inputs):
    inp = {k: np.asarray(v) for k, v in inputs.items()}
    shared = host_shared(inp)
    nc = bass.Bass("TRN2", target_bir_lowering=False)
    prog = Prog(nc)
    prog.build()
    in_maps = []
    for b in range(8):
        m = dict(shared)
        m.update(host_core(inp, b))
        in_maps.append(m)
    res = run_bass_kernel_spmd(nc, in_maps, core_ids=list(range(8)))
    out = np.empty((8, 2048, 1024), np.float32)
    for b in range(8):
        o = np.asarray(res.results[b]["out"]).reshape(8, 128, 8, PW_)
        out[b] = o.transpose(0, 3, 2, 1).reshape(2048, 1024)
    return out
```

```python
import numpy as np
import concourse.bass as bass
import concourse.mybir as mybir
from concourse.bass_utils import run_bass_kernel_spmd

F32 = mybir.dt.float32
BF16 = mybir.dt.bfloat16
AF = mybir.ActivationFunctionType
ALU = mybir.AluOpType

SEM_CHUNK = 30000
COMPUTE = ("pe", "act", "dve", "pool")


class V:
    def __init__(self, tt, ap, box):
        self.tt, self.ap, self.box = tt, ap, box

    def w(self, fn):
        return V(self.tt, fn(self.ap), self.box)


class TT:
    def __init__(self, fw, name, handle, shape, is_dram=False, is_psum=False):
        self.fw, self.name, self.h, self.shape = fw, name, handle, list(shape)
        self.is_dram = is_dram
        self.is_psum = is_psum
        self.recs = []
        st = [1] * len(shape)
        for i in range(len(shape) - 2, 0, -1):
            st[i] = st[i + 1] * shape[i + 1]
        st[0] = 0
        self.strides = st

    def __getitem__(self, idx):
        if not isinstance(idx, tuple):
            idx = (idx,)
        idx = list(idx) + [slice(None)] * (len(self.shape) - len(idx))
        lo, hi = [], []
        for d, (i, n) in enumerate(zip(idx, self.shape)):
            if isinstance(i, int):
                a, b = i, i + 1
                idx[d] = slice(i, i + 1) if (d == 0 and not self.is_dram) else i
            else:
                a, b, s = i.indices(n)
                assert s == 1
            lo.append(a)
            hi.append(b)
        f0 = sum(lo[d] * self.strides[d] for d in range(1, len(self.shape)))
        f1 = sum((hi[d] - 1) * self.strides[d] for d in range(1, len(self.shape))) + 1
        base = self.h.ap() if self.is_dram else self.h
        ap = base[tuple(idx)]
        if self.is_psum:
            f0, f1 = 0, 1 << 30
            lo[0], hi[0] = (lo[0] // 32) * 32, ((hi[0] + 31) // 32) * 32
        return V(self, ap, (lo[0], hi[0], f0, f1))

    def full(self):
        return self[tuple(slice(None) for _ in self.shape)]


class TA_(TT):
    def __init__(self, parent, off, shape, dtype=None, pbase=0):
        self.parent = parent
        self.shape = list(shape)
        self.is_dram = False
        self.off = off
        self.pbase = pbase
        self.ratio = 2 if dtype is not None else 1
        n = 1
        for s_ in shape[1:]:
            n *= s_
        nf = (n + self.ratio - 1) // self.ratio
        flat = parent.h[pbase:pbase + shape[0], off:off + nf]
        if dtype is not None:
            flat = flat.bitcast(dtype)
        names = " ".join("d%d" % i for i in range(1, len(shape)))
        kw = {"d%d" % i: shape[i] for i in range(1, len(shape))}
        self.base = flat.rearrange("p (%s) -> p %s" % (names, names), **kw) if len(shape) > 2 else flat
        st = [1] * len(shape)
        for i in range(len(shape) - 2, 0, -1):
            st[i] = st[i + 1] * shape[i + 1]
        st[0] = 0
        self.strides = st

    @property
    def recs(self):
        return self.parent.recs

    @recs.setter
    def recs(self, v):
        self.parent.recs = v

    def __getitem__(self, idx):
        if not isinstance(idx, tuple):
            idx = (idx,)
        idx = list(idx) + [slice(None)] * (len(self.shape) - len(idx))
        lo, hi = [], []
        for d, (i, n) in enumerate(zip(idx, self.shape)):
            if isinstance(i, int):
                a, b = i, i + 1
                if d == 0:
                    idx[d] = slice(i, i + 1)
            else:
                a, b, s = i.indices(n)
                assert s == 1
            lo.append(a)
            hi.append(b)
        f0 = sum(lo[d] * self.strides[d] for d in range(1, len(self.shape)))
        f1 = sum((hi[d] - 1) * self.strides[d] for d in range(1, len(self.shape))) + 1
        ap = self.base[tuple(idx)]
        r = self.ratio
        return V(self, ap, (self.pbase + lo[0], self.pbase + hi[0], self.off + f0 // r, self.off + (f1 + r - 1) // r))


def _overlap(a, b):
    return a[0] < b[1] and b[0] < a[1] and a[2] < b[3] and b[2] < a[3]


def _covers(a, b):
    return a[0] <= b[0] and a[1] >= b[1] and a[2] <= b[2] and a[3] >= b[3]


class FW:
    def __init__(self, nc, n_dma_sems=12):
        self.nc = nc
        self.ops = {e: [] for e in ("pe", "act", "dve", "pool", "sp")}
        self.tick = {e: 0 for e in COMPUTE}
        self.waited = {e: {} for e in self.ops}
        self.sems = {}
        self.n_dma_sems = n_dma_sems
        self.dma_cnt = {}
        self.dma_uses = {}
        self.stack = None
        self.n_ops = 0
        self.out_tokens = []

    def sbuf(self, name, shape, dtype=F32):
        h = self.nc.alloc_sbuf_tensor(name, list(shape), dtype)
        return TT(self, name, h, shape)

    def psum(self, name, shape, dtype=F32):
        h = self.nc.alloc_psum_tensor(name, list(shape), dtype)
        return TT(self, name, h, shape, is_psum=True)

    def dram(self, name, shape, dtype=F32, kind="Internal"):
        h = self.nc.dram_tensor(name, list(shape), dtype, kind=kind)
        return TT(self, name, h, shape, is_dram=True)

    def _sem(self, key):
        if key not in self.sems:
            self.sems[key] = self.nc.alloc_semaphore("s_%s_%s" % key)
        return self.sems[key]

    def _token_wait(self, eng, tok, force=False):
        kind = tok[0]
        if kind == "c":
            _, pe, n = tok
            if pe == "pe" and eng == "pe" and not force:
                return []
            if self.waited[eng].get(("c", pe), 0) >= n:
                return []
            self.waited[eng][("c", pe)] = n
            return [((pe, (n - 1) // SEM_CHUNK), (n - 1) % SEM_CHUNK + 1)]
        else:
            _, q, j, m = tok
            if self.waited[eng].get(("d", q, j), 0) >= m:
                return []
            self.waited[eng][("d", q, j)] = m
            return [(("dma" + q, j), 16 * m)]

    def op(self, eng, emit, reads=(), writes=(), dma=False, force=()):
        self.n_ops += 1
        if dma:
            q = eng
            i = self.dma_cnt.get(q, 0)
            self.dma_cnt[q] = i + 1
            j = i % self.n_dma_sems
            m = self.dma_uses.get((q, j), 0) + 1
            self.dma_uses[(q, j)] = m
            token = ("d", q, j, m)
            inc = (("dma" + q, j), 16)
        else:
            self.tick[eng] += 1
            n = self.tick[eng]
            token = ("c", eng, n)
            inc = ((eng, (n - 1) // SEM_CHUNK), 1)
        deps = []
        for v in reads:
            psum = getattr(v.tt, "is_psum", False)
            for r in v.tt.recs:
                if (r[3] or (psum and r[2] != eng)) and _overlap(r[0], v.box):
                    deps.append(r[1])
        for v in writes:
            for r in v.tt.recs:
                if _overlap(r[0], v.box):
                    deps.append(r[1])
        if dma and token[3] > 1:
            deps.append(("d", token[1], token[2], token[3] - 1))
        waits = []
        for t in deps:
            waits += self._token_wait(eng, t)
        for t in force:
            waits += self._token_wait(eng, t, force=True)
        for v in writes:
            v.tt.recs = [r for r in v.tt.recs if not _covers(v.box, r[0])]
            v.tt.recs.append((v.box, token, eng, True))
        for v in reads:
            recs = v.tt.recs
            for k, r in enumerate(recs):
                if (not r[3]) and r[2] == eng and r[0] == v.box and r[1][0] == token[0]:
                    recs[k] = (v.box, token, eng, False)
                    break
            else:
                recs.append((v.box, token, eng, False))
        self.ops[eng].append((waits, emit, inc))
        return token

    def wait_tokens(self, eng, tokens):
        waits = []
        for t in tokens:
            waits += self._token_wait(eng, t)
        self.ops[eng].append((waits, None, None))

    def emit(self):
        nc = self.nc
        for e in self.ops:
            for waits, _, inc in self.ops[e]:
                for k, _v in waits:
                    self._sem(k)
                if inc is not None:
                    self._sem(inc[0])
        engmap = {"pe": "tensor", "act": "scalar", "dve": "vector", "pool": "gpsimd", "sp": "sync"}
        with nc.Block() as block:
            for e, ops in self.ops.items():
                def body(engine, ops=ops):
                    for waits, emit, inc in ops:
                        for k, val in waits:
                            engine.wait_ge(self.sems[k], val)
                        if emit is not None:
                            inst = emit(engine)
                            inst.then_inc(self.sems[inc[0]], inc[1])
                getattr(block, engmap[e])(body)

    def dma(self, out, in_, q="sp", **kw):
        return self.op(q, lambda e: e.dma_start(out=out.ap, in_=in_.ap, **kw),
                       reads=[in_], writes=[out], dma=True)

    def _pe_cfg(self, out, lhsT, kind="M"):
        cfg = (kind, lhsT.box[0], lhsT.box[1], out.box[0], out.box[1])
        tt = out.tt
        force = ()
        last = getattr(tt, "pe_last", None)
        if last is not None and last[0] != cfg:
            force = (last[1],)
        dcls = str(lhsT.ap.dtype)
        gl = getattr(self, "pe_glast", None)
        if gl is not None and gl[0] != dcls:
            force = force + (gl[1],)
        self._pe_dcls = dcls
        return cfg, force

    def mm(self, out, lhsT, rhs, start=True, stop=True):
        cfg, force = self._pe_cfg(out, lhsT)
        tok = self.op("pe", lambda e: e.matmul(out.ap, lhsT.ap, rhs.ap, start=start, stop=stop),
                      reads=[lhsT, rhs], writes=[out], force=force)
        out.tt.pe_last = (cfg, tok)
        self.pe_glast = (self._pe_dcls, tok)
        return tok

    def tr(self, out, in_, ident):
        cfg, force = self._pe_cfg(out, in_, kind="T")
        tok = self.op("pe", lambda e: e.transpose(out.ap, in_.ap, ident.ap),
                      reads=[in_, ident], writes=[out], force=force)
        out.tt.pe_last = (cfg, tok)
        self.pe_glast = (self._pe_dcls, tok)
        return tok

    def act(self, out, in_, func, bias=None, scale=None, accum=None):
        reads = [in_]
        kw = {}
        if isinstance(bias, V):
            reads.append(bias)
            kw["bias"] = bias.ap
        elif bias is not None:
            kw["bias"] = bias
        if isinstance(scale, V):
            reads.append(scale)
            kw["scale"] = scale.ap
        elif scale is not None:
            kw["scale"] = scale
        writes = [out]
        if accum is not None:
            writes.append(accum)
            kw["accum_out"] = accum.ap
        return self.op("act", lambda e: e.activation(out.ap, in_.ap, func, **kw),
                       reads=reads, writes=writes)

    def tt(self, out, a, b, op, eng="dve"):
        return self.op(eng, lambda e: e.tensor_tensor(out.ap, a.ap, b.ap, op),
                       reads=[a, b], writes=[out])

    def ts(self, out, a, s1, op0, s2=None, op1=None, eng="dve", accum=None):
        reads = [a]
        x1 = s1.ap if isinstance(s1, V) else s1
        x2 = s2.ap if isinstance(s2, V) else s2
        if isinstance(s1, V):
            reads.append(s1)
        if isinstance(s2, V):
            reads.append(s2)
        writes = [out]
        kw = {}
        if accum is not None:
            writes.append(accum)
            kw["accum_out"] = accum.ap
        if op1 is None:
            return self.op(eng, lambda e: e.tensor_scalar(out.ap, a.ap, x1, None, op0, **kw),
                           reads=reads, writes=writes)
        return self.op(eng, lambda e: e.tensor_scalar(out.ap, a.ap, x1, x2, op0, op1, **kw),
                       reads=reads, writes=writes)

    def stt(self, out, a, s, b, op0, op1, eng="dve"):
        reads = [a, b]
        x = s.ap if isinstance(s, V) else s
        if isinstance(s, V):
            reads.append(s)
        return self.op(eng, lambda e: e.scalar_tensor_tensor(out.ap, a.ap, x, b.ap, op0, op1),
                       reads=reads, writes=[out])

    def copy(self, out, in_, eng="dve"):
        if eng == "act":
            return self.op("act", lambda e: e.copy(out.ap, in_.ap), reads=[in_], writes=[out])
        return self.op(eng, lambda e: e.tensor_copy(out.ap, in_.ap), reads=[in_], writes=[out])

    def memset(self, out, val, eng="dve"):
        return self.op(eng, lambda e: e.memset(out.ap, val), writes=[out])

    def scan(self, out, d0, d1, init, op0=ALU.mult, op1=ALU.add):
        reads = [d0, d1]
        x = init.ap if isinstance(init, V) else init
        if isinstance(init, V):
            reads.append(init)
        return self.op("dve", lambda e: e.tensor_tensor_scan(out.ap, d0.ap, d1.ap, x, op0, op1),
                       reads=reads, writes=[out])

    def recip(self, out, in_):
        return self.op("dve", lambda e: e.reciprocal(out.ap, in_.ap), reads=[in_], writes=[out])

TA = 2304
CTXN = 256
NPIECE = 9
PW_ = 256
L = 2
IN_OFF = [j * 128 if j < 12 else 1560 + (j - 12) * 128 for j in range(28)]
NEG = 30000.0


def _rev(ap, n):
    return bass.AP(tensor=ap.tensor, offset=ap.offset + (n - 1), ap=[list(ap.ap[0]), [-1, n]])


PV_SPEC = [("g_pre", 8), ("g_post", 8), ("g_fpre", 8), ("g_fpost", 8), ("ada_b", 48),
           ("gconv", 36), ("gnorm", 1), ("lconv", 8), ("lconvb", 2), ("lba", 4), ("lbx", 4),
           ("llam", 4), ("mu", 12), ("w0", 6), ("a0", 6), ("kk", 3), ("ka", 3), ("rk", 3),
           ("gnw", 3), ("gnb", 3)]
PV_OFF = {}
_o = 0
for _n, _w in PV_SPEC:
    PV_OFF[_n] = (_o, _w)
    _o += _w
NPV = _o
NCST = 13


def _cst():
    r = np.arange(128)[:, None]
    c = np.arange(128)[None, :]
    UI = (r <= c).astype(np.float32)
    LI = (r >= c).astype(np.float32)
    ident = np.eye(128, dtype=np.float32)
    ones = np.ones((128, 128), np.float32)
    bo = np.zeros((128, 128), np.float32)
    bo[:64, :64] = 1
    bo[64:, 64:] = 1
    mats = [ident, ones, bo, UI, LI, NEG * UI, NEG * LI, -NEG * UI, -NEG * LI,
            -NEG * (1 - UI), -NEG * (1 - LI), 1 - UI, 1 - LI]
    return np.ascontiguousarray(np.stack(mats, axis=1))


C_ID, C_ONES, C_BO, C_UI, C_LI, C_PUI, C_PLI, C_NUI, C_NLI, C_NSL, C_NSU, C_SL, C_SU = range(13)


def _kc(w):
    K = w.shape[0]
    return np.ascontiguousarray(w.reshape(K // 128, 128, w.shape[1]).transpose(1, 0, 2))


def _colchunks(w, offs):
    K = w.shape[0]
    return np.ascontiguousarray(
        np.stack([_kc(w[:, o:o + 128]).reshape(128, (K // 128) * 128) for o in offs]))


def _pcol(v, n):
    return np.ascontiguousarray(np.asarray(v).reshape(n, 128).T)


def host_shared(inp):
    d = {}
    d["cst"] = _cst()
    d["adaw"] = np.stack([_colchunks(inp["ada_w"][i], [j * 128 for j in range(48)]) for i in range(L)]).reshape(L * 48, 128, 1024)
    d["win"] = np.stack([_colchunks(inp["w_in"][i], IN_OFF) for i in range(L)]).reshape(L * 28, 128, 1024)
    d["wba"] = np.stack([_kc(inp["w_in"][i][:, 1536:1560]).reshape(128, 8 * 24) for i in range(L)])
    d["wout"] = np.stack([_colchunks(inp["w_out"][i], [j * 128 for j in range(8)]) for i in range(L)]).reshape(L * 8, 128, 1024)
    d["wup"] = np.stack([_colchunks(inp["ffn_up"][i], [j * 128 for j in range(32)]) for i in range(L)]).reshape(L * 32, 128, 1024)
    d["wdn"] = np.stack([_colchunks(inp["ffn_down"][i], [j * 128 for j in range(8)]) for i in range(L)]).reshape(L * 8, 128, 4096)
    pv = np.zeros((L, 128, NPV), np.float32)
    rowt = np.zeros((L, 128, 24), np.float32)
    lw = np.zeros((L, 128, 8, 128), np.float32)
    rup = np.zeros((L, 128, 3, 384), np.float32)
    for i in range(L):
        def put(name, arr):
            o, w = PV_OFF[name]
            pv[i, :, o:o + w] = arr
        put("g_pre", _pcol(inp["norm_mix_pre"][i], 8))
        put("g_post", _pcol(inp["norm_mix_post"][i], 8))
        put("g_fpre", _pcol(inp["norm_ffn_pre"][i], 8))
        put("g_fpost", _pcol(inp["norm_ffn_post"][i], 8))
        put("ada_b", _pcol(inp["ada_b"][i], 48))
        gc = inp["gdn_conv"][i]
        put("gconv", np.stack([_pcol(gc[k], 9) for k in range(4)], axis=2).reshape(128, 36))
        put("gnorm", np.tile(inp["gdn_norm"][i], 2)[:, None])
        lc = inp["lru_conv"][i]
        put("lconv", np.stack([_pcol(lc[k], 2) for k in range(4)], axis=2).reshape(128, 8))
        put("lconvb", _pcol(inp["lru_conv_b"][i], 2))
        put("lba", np.concatenate([_pcol(inp["lru_ba"][i][dd], 2) for dd in range(2)], axis=1))
        put("lbx", np.concatenate([_pcol(inp["lru_bx"][i][dd], 2) for dd in range(2)], axis=1))
        put("llam", np.concatenate([_pcol(inp["lru_lambda"][i][dd], 2) for dd in range(2)], axis=1))
        put("mu", _pcol(inp["rwkv_mu"][i], 12))
        put("w0", np.concatenate([_pcol(inp["rwkv_w0"][i][dd], 3) for dd in range(2)], axis=1))
        put("a0", np.concatenate([_pcol(inp["rwkv_a0"][i][dd], 3) for dd in range(2)], axis=1))
        put("kk", _pcol(inp["rwkv_k_k"][i], 3))
        put("ka", _pcol(inp["rwkv_k_a"][i], 3))
        put("rk", _pcol(inp["rwkv_r_k"][i].reshape(-1), 3))
        put("gnw", _pcol(inp["rwkv_gn_w"][i], 3))
        put("gnb", _pcol(inp["rwkv_gn_b"][i], 3))
        rowt[i, :, 0:12] = inp["gdn_a_log"][i].reshape(1, 12)
        rowt[i, :, 12:24] = inp["gdn_dt_bias"][i].reshape(1, 12)
        for ax, nm in enumerate(("lru_wa", "lru_wx")):
            for dd in range(2):
                for jc in range(2):
                    for bl in range(2):
                        lw[i, bl * 64:(bl + 1) * 64, ax * 4 + dd * 2 + jc, bl * 64:(bl + 1) * 64] = inp[nm][i][dd, 2 * jc + bl]
        rup[i, :, 0, :] = inp["rwkv_w_up"][i].reshape(128, 384)
        rup[i, :, 1, :] = inp["rwkv_a_up"][i].reshape(128, 384)
        rup[i, :, 2, :] = inp["rwkv_g_up"][i]
    d["pv"] = pv
    d["rowt"] = rowt
    d["lw"] = lw.reshape(L, 128, 1024)
    d["rup"] = rup.reshape(L, 128, 3 * 384)
    return d


def host_core(inp, b):
    xt = np.concatenate([inp["ctx"][b], inp["x"][b]], axis=0)
    xin = np.ascontiguousarray(xt.reshape(NPIECE, PW_, 8, 128).transpose(0, 3, 2, 1)).reshape(NPIECE, 128, 8 * PW_)
    cc = np.stack([inp["c"][b], inp["c_ctx"]], axis=1)
    call = np.ascontiguousarray(cc.reshape(8, 128, 2).transpose(1, 0, 2)).reshape(128, 16)
    return {"xin": xin, "call": call}


NROW = 10
GDN_ORDER = [list(range(18)), [1, 0] + list(range(17, 1, -1))]
RWKV_ORDER = [list(range(36)), [3, 2, 1, 0] + list(range(35, 3, -1))]
TOK_TILES = [(0, 256), (256, 768), (768, 1280), (1280, 1792), (1792, 2304)]
SEGS = [(0, CTXN), (CTXN, TA)]


def _bc(view, dims):
    return view.w(lambda ap: bass.AP(tensor=ap.tensor, offset=ap.offset, ap=[list(ap.ap[0])] + [list(d) for d in dims]))


class _Stop(Exception):
    pass


class Prog:
    def stop(self, tag):
        if self.stop_tag == tag:
            raise _Stop()

    def __init__(self, nc, dbg=(), nlayers=L, stop_after=None):
        self.nc = nc
        self.fw = FW(nc)
        self.dbg = set(dbg)
        self.dbg_out = {}
        self.final = []
        self.nlayers = nlayers
        self.stop_after = stop_after
        self.stop_tag = None
        self.skip = set()
        self.alloc()

    def alloc(self):
        fw = self.fw
        EI = "ExternalInput"
        self.xin = fw.dram("xin", [NPIECE, 128, 8 * PW_], F32, EI)
        self.call = fw.dram("call", [128, 16], F32, EI)
        self.cst_d = fw.dram("cst", [128, NCST, 128], F32, EI)
        self.adaw = fw.dram("adaw", [L * 48, 128, 1024], F32, EI)
        self.win = fw.dram("win", [L * 28, 128, 1024], F32, EI)
        self.wba_d = fw.dram("wba", [L, 128, 8 * 24], F32, EI)
        self.wout = fw.dram("wout", [L * 8, 128, 1024], F32, EI)
        self.wup = fw.dram("wup", [L * 32, 128, 1024], F32, EI)
        self.wdn = fw.dram("wdn", [L * 8, 128, 4096], F32, EI)
        self.pv_d = fw.dram("pv", [L, 128, NPV], F32, EI)
        self.rowt_d = fw.dram("rowt", [L, 128, 24], F32, EI)
        self.lw_d = fw.dram("lw", [L, 128, 1024], F32, EI)
        self.rup_d = fw.dram("rup", [L, 128, 3 * 384], F32, EI)
        self.out_d = fw.dram("out", [8, 128, 8 * PW_], F32, "ExternalOutput")
        self.xr = fw.dram("xr", [NPIECE, 128, 8 * PW_], F32)
        self.od = fw.dram("od", [8, 128, TA], BF16)

        self.cst = fw.sbuf("cst_s", [128, NCST, 128], F32)
        self.H = fw.sbuf("H", [128, 8, TA], BF16)
        self.WS = fw.sbuf("WS", [128, NROW * TA], F32)
        self.pv = fw.sbuf("pv_s", [128, NPV], F32)
        self.rowt = fw.sbuf("rowt_s", [128, 24], F32)
        self.sc = fw.sbuf("sc", [128, 8, 2], F32)
        self.mod = fw.sbuf("mod", [128, 48, 2], F32)
        self.gs = fw.sbuf("gs", [128, 4, 8, 2], F32)
        self.epsc = fw.sbuf("epsc", [128, 4], F32)
        self.wst = [fw.sbuf("wst%d" % i, [128, 8, 128], F32) for i in range(3)]
        self.wst_i = 0
        self.obf = fw.sbuf("obf", [128, TA], BF16)
        self.wbf = [fw.sbuf("wbf%d" % i, [128, 8, 128], BF16) for i in range(3)]
        self.wbf_i = 0
        self.wba = fw.sbuf("wba_s", [128, 8, 24], BF16)
        self.small = fw.sbuf("small", [128, 4096], F32)
        self.misc = fw.sbuf("misc", [128, 2048], F32)
        self.gt = fw.sbuf("gt", [128, 10, 18, 12], F32)
        self.lwt = fw.sbuf("lwt", [128, 8, 128], F32)
        self.rup = fw.sbuf("rup_s", [128, 3, 384], F32)
        self.identb = fw.sbuf("identb", [128, 128], BF16)
        self.pb = [fw.psum("pb%d" % i, [128, 512], F32) for i in range(8)]

    def row(self, r):
        return self.WS[:, r * TA:(r + 1) * TA]

    def rowc(self, r, c0, c1, p0=0, p1=128):
        return self.WS[p0:p1, r * TA + c0:r * TA + c1]

    def pvc(self, name, j=0, p0=0, p1=128):
        o, w = PV_OFF[name]
        return self.pv[p0:p1, o + j:o + j + 1]

    def C(self, idx, p0=0, p1=128, c0=0, c1=128):
        return self.cst[p0:p1, idx, c0:c1]

    def dump(self, name, view, shape):
        if name not in self.dbg:
            return
        o = self.fw.dram("dbg_" + name, list(shape), F32, kind="ExternalOutput")
        self.dbg_out[name] = o
        self.final.append(self.fw.dma(o.full(), view))

    def dump_row(self, name, j, nj, view):
        if name not in self.dbg:
            return
        if name not in self.dbg_out:
            self.dbg_out[name] = self.fw.dram("dbg_" + name, [nj, 128, TA], F32, kind="ExternalOutput")
        self.final.append(self.fw.dma(self.dbg_out[name][j], view))

    def next_wbf(self):
        t = self.wbf[self.wbf_i % 3]
        self.wbf_i += 1
        return t

    def load_w(self, dst, src, eng="pool"):
        st = self.wst[self.wst_i % 3]
        self.wst_i += 1
        self.fw.dma(st.full().w(lambda ap: ap.rearrange("p a b -> p (a b)")), src)
        self.fw.copy(dst, st.full(), eng=eng)

    def load_wc(self, dst, src, slot, first):
        r8 = lambda ap: ap.rearrange("p (a b) -> p a b", a=8)
        if first:
            self.load_w(dst, src, eng="dve")
            self.pending.append((dst, slot))
            while len(self.pending) > 1:
                v, s = self.pending.pop(0)
                self.fw.dma(self.wcache[s].w(r8), v)
        else:
            self.fw.dma(dst, self.wcache[slot].w(r8))

    def build(self):
        fw = self.fw
        fw.dma(self.cst.full(), self.cst_d.full())
        fw.dma(self.sc.full().w(lambda ap: ap.rearrange("p a b -> p (a b)")), self.call.full())
        fw.act(self.sc.full(), self.sc.full(), AF.Silu)
        fw.copy(self.identb.full(), self.C(C_ID))
        fw.memset(self.epsc[:, 0:1], 1e-6)
        fw.memset(self.epsc[:, 1:2], 1.0)
        fw.memset(self.epsc[:, 2:3], 6.4e-4)
        fw.memset(self.epsc[:, 3:4], 0.0)
        try:
            for layer in range(self.nlayers):
                self.layer(layer)
                if self.stop_after is not None and self.stop_after[0] == layer:
                    break
        except _Stop:
            pass
        fw.wait_tokens("sp", self.final)
        fw.emit()

    def layer(self, layer):
        fw = self.fw
        last = (layer == L - 1)
        fw.dma(self.pv.full(), self.pv_d[layer])
        fw.dma(self.rowt.full(), self.rowt_d[layer])
        fw.dma(self.lwt.full().w(lambda ap: ap.rearrange("p a b -> p (a b)")), self.lw_d[layer])
        fw.dma(self.rup.full().w(lambda ap: ap.rearrange("p a b -> p (a b)")), self.rup_d[layer])
        self.modulation(layer)
        self.phase_a(layer)
        if self.stop_after == (layer, "a"):
            return
        st = self.wst[self.wst_i % 3]
        self.wst_i += 1
        stv = st.full().w(lambda ap: ap.rearrange("p a b -> p (a b)")[:, 0:192])
        self.fw.dma(stv, self.wba_d[layer])
        self.fw.copy(self.wba.full().w(lambda ap: ap.rearrange("p a b -> p (a b)")), stv, eng="pool")
        if "gdn" not in self.skip:
            self.gdn(layer)
        if self.stop_after == (layer, "gdn"):
            return
        if "lru" not in self.skip:
            self.lru(layer)
        if self.stop_after == (layer, "lru"):
            return
        if "rwkv" not in self.skip:
            self.rwkv(layer)
        if self.stop_after == (layer, "rwkv"):
            return
        self.phase_c(layer, last)

    def modulation(self, layer):
        fw = self.fw
        mp = self.pb[7]
        modp = mp[:, 0:96].w(lambda ap: ap.rearrange("p (a b) -> p a b", b=2))
        for j in range(48):
            wt = self.wst[self.wst_i % 3]
            self.wst_i += 1
            fw.dma(wt.full().w(lambda ap: ap.rearrange("p a b -> p (a b)")), self.adaw[layer * 48 + j])
            for kc in range(8):
                fw.mm(mp[:, 2 * j:2 * j + 2], wt[:, kc, :], self.sc[:, kc, :], start=(kc == 0), stop=(kc == 7))
        o, w = PV_OFF["ada_b"]
        bb = _bc(self.pv[:, o:o + 48], [[1, 48], [0, 2]])
        fw.tt(self.mod.full(), modp, bb, ALU.add)
        self.dump("mod%d" % layer, self.mod.full(), [128, 48, 2])

        def gcol(name):
            o, w = PV_OFF[name]
            return _bc(self.pv[:, o:o + 8], [[1, 8], [0, 2]])
        fw.stt(self.gs[:, 0, :, :], self.mod[:, 8:16, :], 1.0, gcol("g_pre"), ALU.add, ALU.mult)
        fw.tt(self.gs[:, 1, :, :], self.mod[:, 16:24, :], gcol("g_post"), ALU.mult)
        fw.stt(self.gs[:, 2, :, :], self.mod[:, 32:40, :], 1.0, gcol("g_fpre"), ALU.add, ALU.mult)
        fw.tt(self.gs[:, 3, :, :], self.mod[:, 40:48, :], gcol("g_fpost"), ALU.mult)

    def rms_rstd(self, src3, sq3, n, dst, ps):
        fw = self.fw
        fw.act(sq3, src3, AF.Square)
        for kc in range(8):
            fw.mm(ps, self.C(C_ONES), self._k(sq3, kc), start=(kc == 0), stop=(kc == 7))
        fw.act(dst, ps, AF.Ln, bias=self.epsc[:, 0:1], scale=1.0 / 1024.0)
        fw.act(dst, dst, AF.Exp, scale=-0.5)

    def _k(self, v3, kc):
        return v3.w(lambda ap: ap[:, kc, :])

    def phase_a(self, layer):
        fw = self.fw
        src = self.xin if layer == 0 else self.xr
        xt = self.small[:, 0:2048].w(lambda ap: ap.rearrange("p (a b) -> p a b", a=8))
        sq = self.small[:, 2048:4096].w(lambda ap: ap.rearrange("p (a b) -> p a b", a=8))
        for pc in range(NPIECE):
            which = 1 if pc == 0 else 0
            fw.dma(xt, src[pc].w(lambda ap: ap.rearrange("p (a b) -> p a b", a=8)))
            rstd = self.misc[:, 0:PW_]
            self.rms_rstd(xt, sq, PW_, rstd, self.pb[6][:, 0:PW_])
            for kc in range(8):
                tmp = self.misc[:, (1 + kc % 2) * PW_:(2 + kc % 2) * PW_]
                fw.tt(tmp, self._k(xt, kc), rstd, ALU.mult)
                fw.ts(self.H[:, kc, pc * PW_:(pc + 1) * PW_], tmp, self.gs[:, 0, kc, which:which + 1], ALU.mult,
                      self.mod[:, kc, which:which + 1], ALU.add, eng=("pool" if kc % 2 else "dve"))
        if ("h%d" % layer) in self.dbg:
            for kc in range(8):
                fw.copy(self.row(0), self.H[:, kc, :])
                self.dump_row("h%d" % layer, kc, 8, self.row(0))

    def inproj(self, layer, j, dst_row):
        fw = self.fw
        wt = self.next_wbf()
        self.load_w(wt.full(), self.win[layer * 28 + j])
        for ti, (c0, c1) in enumerate(TOK_TILES):
            ps = self.pb[ti % 2][:, 0:c1 - c0]
            for kc in range(8):
                fw.mm(ps, wt[:, kc, :], self.H[:, kc, c0:c1], start=(kc == 0), stop=(kc == 7))
            fw.copy(self.rowc(dst_row, c0, c1), ps, eng=("act" if ti % 2 == 0 else "dve"))

    def store_o(self, ch, row_view):
        self.fw.copy(self.obf.full(), row_view, eng="pool")
        self.fw.dma(self.od[ch], self.obf.full())

    def conv4(self, dst_row, src_row, wname, j, segs=SEGS):
        fw = self.fw
        o, w = PV_OFF[wname]
        wc = lambda k: self.pv[:, o + 4 * j + k:o + 4 * j + k + 1]
        fw.ts(self.row(dst_row), self.row(src_row), wc(2), ALU.mult)
        for (a, b) in segs:
            for k, sh in ((0, -2), (1, -1), (3, 1)):
                lo = max(a, a - sh)
                hi = min(b, b - sh)
                fw.stt(self.rowc(dst_row, lo, hi), self.rowc(src_row, lo + sh, hi + sh), wc(k),
                       self.rowc(dst_row, lo, hi), ALU.mult, ALU.add)

    def headstat(self, dst_row, src_row, eps_col, scale=1.0, square=True):
        fw = self.fw
        for ti, (c0, c1) in enumerate(TOK_TILES):
            n = c1 - c0
            sq = self.small[:, (ti % 2) * 512:(ti % 2) * 512 + n]
            fw.act(sq, self.rowc(src_row, c0, c1), AF.Square)
            ps = self.pb[2 + (ti % 2)][:, 0:n]
            fw.mm(ps, self.C(C_BO), sq)
            fw.act(self.rowc(dst_row, c0, c1), ps, AF.Ln, bias=self.epsc[:, eps_col:eps_col + 1], scale=scale)
        fw.act(self.row(dst_row), self.row(dst_row), AF.Exp, scale=-0.5)

    def neumann(self, P, nb, C, sign, nlev, XY, G, bf=False):
        fw = self.fw
        PX, PY, PG = self.pb[2], self.pb[3], self.pb[4]
        pv3 = lambda pbk: pbk[0:P, 0:nb * C].w(lambda ap: ap.rearrange("p (a b) -> p a b", b=C))
        identb = _bc(self.C(C_ID, 0, P, 0, C), [[0, nb], [1, C]])
        fw.tt(G[:, 0, :, :], identb, XY[:, 0, 1, :, :], ALU.add if sign > 0 else ALU.subtract)
        for p in range(1, nlev + 1):
            s, d = (p - 1) % 2, p % 2
            for u in range(nb):
                fw.mm(PX[0:P, u * C:(u + 1) * C], XY[:, s, 1, u, :], XY[:, s, 0, u, :])
            fw.copy(XY[:, d, 0, :, :], pv3(PX), eng="dve")
            if p < nlev:
                for u in range(nb):
                    fw.mm(PY[0:P, u * C:(u + 1) * C], XY[:, s, 0, u, :], XY[:, s, 1, u, :])
                fw.copy(XY[:, d, 1, :, :], pv3(PY), eng="act")
            for u in range(nb):
                fw.mm(PG[0:P, u * C:(u + 1) * C], XY[:, d, 0, u, :], G[:, s, u, :])
            fw.tt(G[:, d, :, :], pv3(PG), G[:, s, :, :], ALU.add)
        return nlev % 2

    def gdn_tables(self, layer):
        fw = self.fw
        gt = self.gt
        pba = self.pb[7][:, 0:432].w(lambda ap: ap.rearrange("p (a b) -> p a b", b=24))
        for i in range(18):
            for kc in range(8):
                fw.mm(self.pb[7][:, i * 24:(i + 1) * 24], self.H[:, kc, i * 128:(i + 1) * 128], self.wba[:, kc, :],
                      start=(kc == 0), stop=(kc == 7))
        braw = pba.w(lambda ap: ap[:, :, 0:12])
        araw = pba.w(lambda ap: ap[:, :, 12:24])
        T_G, T_LB, T_GC, T_GCL, T_NGC, T_B, T_BG, T_KD, T_EGL, T_TMP = range(10)
        one = self.epsc[:, 1:2]
        fw.act(gt[:, T_TMP], braw, AF.Exp, scale=-1.0)
        fw.act(gt[:, T_TMP], gt[:, T_TMP], AF.Ln, bias=one)
        fw.ts(gt[:, T_LB], gt[:, T_TMP], -1.0, ALU.mult)
        fw.tt(gt[:, T_TMP], araw, _bc(self.rowt[:, 12:24], [[0, 18], [1, 12]]), ALU.add)
        fw.act(gt[:, T_TMP], gt[:, T_TMP], AF.Exp)
        fw.act(gt[:, T_TMP], gt[:, T_TMP], AF.Ln, bias=one)
        na = self.misc[:, 0:12]
        fw.act(na, self.rowt[:, 0:12], AF.Exp)
        fw.stt(gt[:, T_G], gt[:, T_TMP], -1.0, _bc(na, [[0, 18], [1, 12]]), ALU.mult, ALU.mult)
        pg = self.pb[6][:, 0:216].w(lambda ap: ap.rearrange("p (a b) -> p a b", b=12))
        pl = self.pb[6][:, 256:472].w(lambda ap: ap.rearrange("p (a b) -> p a b", b=12))
        for i in range(18):
            for d in range(2):
                fw.mm(self.pb[6][:, i * 12 + d * 6:i * 12 + d * 6 + 6], self.C(C_UI if d == 0 else C_LI),
                      gt[:, T_G, i, d * 6:d * 6 + 6])
            fw.mm(self.pb[6][:, 256 + i * 12:256 + i * 12 + 12], self.C(C_ONES), gt[:, T_G, i, :])
        fw.copy(gt[:, T_GC], pg)
        fw.tt(gt[:, T_GCL], gt[:, T_GC], gt[:, T_LB], ALU.add)
        fw.ts(gt[:, T_NGC], gt[:, T_GC], -1.0, ALU.mult)
        fw.act(gt[:, T_B], gt[:, T_LB], AF.Exp)
        fw.act(gt[:, T_BG], gt[:, T_GCL], AF.Exp)
        fw.tt(gt[:, T_KD], pl, gt[:, T_GC], ALU.subtract)
        fw.act(gt[:, T_KD], gt[:, T_KD], AF.Exp)
        fw.act(gt[:, T_EGL], pl, AF.Exp)
        if ("gdn_g%d" % layer) in self.dbg:
            self.dump("gdn_g%d" % layer, gt[:, T_G], [128, 18, 12])
            self.dump("gdn_lb%d" % layer, gt[:, T_LB], [128, 18, 12])
            self.dump("gdn_gc%d" % layer, gt[:, T_GC], [128, 18, 12])

    def gdn(self, layer):
        fw = self.fw
        self.gdn_tables(layer)
        self.stop("gdn_t")
        T_G, T_LB, T_GC, T_GCL, T_NGC, T_B, T_BG, T_KD, T_EGL, T_TMP = range(10)
        gt = self.gt
        RQ, RK, RV, RT, RO0, RO1 = 0, 1, 2, 3, 4, 5
        base = 6 * TA
        WSt = self.WS
        gb = TA_(WSt, base, [128, 4, 128])
        lbb = TA_(WSt, base + 512, [128, 4, 128])
        dec = TA_(WSt, base + 1024, [128, 3, 4, 128])
        XY = TA_(WSt, base + 2560, [128, 2, 2, 4, 128])
        G = TA_(WSt, base + 4608, [128, 2, 4, 128])
        AT = TA_(WSt, base + 5632, [128, 2, 4, 128])
        TOK = TA_(WSt, base + 6656, [128, 2, 3, 4, 64])
        TOKB = TOK
        U = TA_(WSt, base + 8192, [128, 2, 4, 64])
        WT = TA_(WSt, base + 8704, [128, 2, 2, 128])
        QG = TA_(self.small, 0, [128, 2, 2, 128])
        EG = TA_(self.small, 512, [128, 4, 128])
        S = TA_(self.small, 1024, [128, 64])
        VN = TA_(self.small, 1088, [128, 2, 128])
        pb = self.pb
        for hp in range(3):
            for (j, dst) in ((hp, RQ), (3 + hp, RK), (6 + hp, RV)):
                self.inproj(layer, j, RT)
                self.conv4(dst, RT, "gconv", j)
                fw.act(self.row(dst), self.row(dst), AF.Silu)
            self.headstat(RT, RQ, 0)
            fw.stt(self.row(RQ), self.row(RQ), 0.125, self.row(RT), ALU.mult, ALU.mult)
            self.headstat(RT, RK, 0)
            fw.tt(self.row(RK), self.row(RK), self.row(RT), ALU.mult)
            if layer == 0 or True:
                self.dump_row("gdn_q%d" % layer, hp, 3, self.row(RQ))
                self.dump_row("gdn_k%d" % layer, hp, 3, self.row(RK))
                self.dump_row("gdn_v%d" % layer, hp, 3, self.row(RV))
            self.stop("gdn_p")
            for d in range(2):
                RO = RO0 + d
                fw.memset(S.full(), 0.0)
                order = GDN_ORDER[d]
                mincl = C_UI if d == 0 else C_LI
                m1 = C_PUI if d == 0 else C_PLI
                m2 = C_NLI if d == 0 else C_NUI
                m3 = C_NSL if d == 0 else C_NSU
                for b in range(9):
                    rr = b % 2
                    tiles = (order[2 * b], order[2 * b + 1])
                    units = [(hh, ts) for hh in range(2) for ts in range(2)]

                    def cidx(hh):
                        return d * 6 + 2 * hp + hh
                    for hh in range(2):
                        for ts in range(2):
                            i = tiles[ts]
                            u = hh * 2 + ts
                            ci = cidx(hh)
                            fw.copy(gb[:, u, :], _bc(gt[:, T_G, i, ci:ci + 1], [[0, 128]]), eng="pool")
                            fw.copy(lbb[:, u, :], _bc(gt[:, T_LB, i, ci:ci + 1], [[0, 128]]), eng="pool")
                    for u, (hh, ts) in enumerate(units):
                        sl = slice(u * 128, (u + 1) * 128)
                        fw.mm(pb[2][:, sl], gb[:, u, :], self.C(mincl), start=True, stop=False)
                        fw.mm(pb[2][:, sl], self.C(C_ID), self.C(m1), start=False, stop=True)
                        fw.mm(pb[3][:, sl], gb[:, u, :], self.C(mincl), start=True, stop=False)
                        fw.mm(pb[3][:, sl], lbb[:, u, :], self.C(C_ID), start=False, stop=False)
                        fw.mm(pb[3][:, sl], self.C(C_ID), self.C(m2), start=False, stop=True)
                        fw.mm(pb[4][:, sl], gb[:, u, :], self.C(mincl), start=True, stop=False)
                        fw.mm(pb[4][:, sl], self.C(C_ID), self.C(m3), start=False, stop=True)
                        fw.mm(pb[5][:, sl], gb[:, u, :], self.C(mincl), start=True, stop=True)
                    for u, (hh, ts) in enumerate(units):
                        i = tiles[ts]
                        sl = slice(u * 128, (u + 1) * 128)
                        ci = cidx(hh)
                        fw.act(dec[:, 0, u, :], pb[2][:, sl], AF.Exp, bias=gt[:, T_GCL, i, ci:ci + 1], scale=-1.0)
                        fw.act(dec[:, 1, u, :], pb[3][:, sl], AF.Exp, bias=gt[:, T_NGC, i, ci:ci + 1], scale=1.0)
                        fw.act(dec[:, 2, u, :], pb[4][:, sl], AF.Exp, bias=gt[:, T_NGC, i, ci:ci + 1], scale=1.0)
                    fw.act(EG.full(), pb[5][:, 0:512].w(lambda ap: ap.rearrange("p (a b) -> p a b", b=128)), AF.Exp)
                    for u, (hh, ts) in enumerate(units):
                        i = tiles[ts]
                        hb = hh * 64
                        cs = (i * 128, (i + 1) * 128)
                        kT = self.rowc(RK, cs[0], cs[1], hb, hb + 64)
                        qT = self.rowc(RQ, cs[0], cs[1], hb, hb + 64)
                        vT = self.rowc(RV, cs[0], cs[1], hb, hb + 64)
                        idn = self.C(C_ID, hb, hb + 64, hb, hb + 64)
                        fw.mm(pb[6 + hh][:, ts * 128:(ts + 1) * 128], kT, kT)
                        fw.mm(pb[6 + hh][:, 256 + ts * 128:256 + (ts + 1) * 128], kT, qT)
                        fw.tr(pb[hh][:, ts * 64:(ts + 1) * 64], kT, idn)
                        fw.tr(pb[hh][:, 128 + ts * 64:128 + (ts + 1) * 64], vT, idn)
                    for hh in range(2):
                        kk3 = pb[6 + hh][:, 0:256].w(lambda ap: ap.rearrange("p (a b) -> p a b", b=128))
                        kq3 = pb[6 + hh][:, 256:512].w(lambda ap: ap.rearrange("p (a b) -> p a b", b=128))
                        us = slice(hh * 2, hh * 2 + 2)
                        fw.tt(XY[:, 0, 0, us, :], kk3, dec[:, 0, us, :], ALU.mult)
                        fw.tt(XY[:, 0, 1, us, :], kk3, dec[:, 1, us, :], ALU.mult)
                        fw.tt(AT[:, rr, us, :], kq3, dec[:, 2, us, :], ALU.mult)
                    for u, (hh, ts) in enumerate(units):
                        i = tiles[ts]
                        ci = cidx(hh)
                        ktk = pb[hh][:, ts * 64:(ts + 1) * 64]
                        vtk = pb[hh][:, 128 + ts * 64:128 + (ts + 1) * 64]
                        fw.ts(TOKB[:, rr, 0, u, :], vtk, gt[:, T_B, i, ci:ci + 1], ALU.mult)
                        fw.ts(TOKB[:, rr, 1, u, :], ktk, gt[:, T_BG, i, ci:ci + 1], ALU.mult)
                        fw.ts(TOK[:, rr, 2, u, :], ktk, gt[:, T_KD, i, ci:ci + 1], ALU.mult)
                    for u, (hh, ts) in enumerate(units):
                        i = tiles[ts]
                        hb = hh * 64
                        fw.tt(QG[hb:hb + 64, rr, ts, :], self.rowc(RQ, i * 128, (i + 1) * 128, hb, hb + 64),
                              EG[hb:hb + 64, u, :], ALU.mult, eng="pool")
                    gi = self.neumann(128, 4, 128, -1, 6, XY, G)
                    for u, (hh, ts) in enumerate(units):
                        hb = hh * 64
                        fw.mm(pb[5][:, u * 64:(u + 1) * 64], G[:, gi, u, :], TOKB[:, rr, 0, u, :])
                        fw.mm(pb[6][hb:hb + 64, ts * 128:(ts + 1) * 128], TOKB[:, rr, 1, u, :], G[:, gi, u, :])
                    fw.copy(U[:, rr, :, :], pb[5][:, 0:256].w(lambda ap: ap.rearrange("p (a b) -> p a b", b=64)), eng="act")
                    fw.copy(WT[:, rr, :, :], pb[6][:, 0:256].w(lambda ap: ap.rearrange("p (a b) -> p a b", b=128)), eng="dve")
                    self.stop("gdn_b0")
                    for ts in range(2):
                        i = tiles[ts]
                        vr = ts
                        for hh in range(2):
                            hb = hh * 64
                            fw.mm(pb[hh][:, 0:64], WT[hb:hb + 64, rr, ts, :], S[hb:hb + 64, :])
                        for hh in range(2):
                            fw.tt(VN[:, vr, hh * 64:(hh + 1) * 64], U[:, rr, hh * 2 + ts, :], pb[hh][:, 0:64], ALU.subtract)
                        self.stop("gdn_ra")
                        for hh in range(2):
                            hb = hh * 64
                            u = hh * 2 + ts
                            fw.mm(pb[2][hb:hb + 64, 0:128], S[hb:hb + 64, :], QG[hb:hb + 64, rr, ts, :])
                            fw.mm(pb[3][hb:hb + 64, 0:128], VN[:, vr, hh * 64:(hh + 1) * 64], AT[:, rr, u, :])
                        fw.copy(self.rowc(RO, i * 128, (i + 1) * 128), pb[2][:, 0:128], eng="act")
                        fw.tt(self.rowc(RO, i * 128, (i + 1) * 128), self.rowc(RO, i * 128, (i + 1) * 128), pb[3][:, 0:128], ALU.add)
                        self.stop("gdn_rb")
                        for hh in range(2):
                            hb = hh * 64
                            u = hh * 2 + ts
                            fw.mm(pb[4][hb:hb + 64, 0:64], TOK[:, rr, 2, u, :], VN[:, vr, hh * 64:(hh + 1) * 64])
                        self.stop("gdn_rc")
                        for hh in range(2):
                            hb = hh * 64
                            ci = cidx(hh)
                            fw.stt(S[hb:hb + 64, :], S[hb:hb + 64, :], gt[hb:hb + 64, T_EGL, i, ci:ci + 1],
                                   pb[4][hb:hb + 64, 0:64], ALU.mult, ALU.add)
                    self.stop("gdn_r0")
                self.stop("gdn_d0")
            fw.tt(self.row(RO0), self.row(RO0), self.row(RO1), ALU.add)
            self.dump_row("gdn_o%d" % layer, hp, 3, self.row(RO0))
            self.headstat(RT, RO0, 0, scale=1.0 / 64.0)
            fw.stt(self.row(RO0), self.row(RO0), self.pvc("gnorm"), self.row(RT), ALU.mult, ALU.mult)
            self.inproj(layer, 9 + hp, RQ)
            fw.act(self.row(RQ), self.row(RQ), AF.Silu)
            fw.tt(self.row(RO0), self.row(RO0), self.row(RQ), ALU.mult)
            self.dump_row("mix%d" % layer, hp, 8, self.row(RO0))
            self.store_o(hp, self.row(RO0))

    def lru(self, layer):
        fw = self.fw
        RT, RP, RX, RA, RB, RH, RG, RI = 0, 1, 2, 3, 4, 5, 6, 7
        o, w = PV_OFF["llam"]
        c8 = self.misc[:, 1024:1028]
        c16 = self.misc[:, 1028:1032]
        fw.act(c8, self.pv[:, o:o + 4], AF.Exp, scale=-1.0)
        fw.act(c8, c8, AF.Ln, bias=self.epsc[:, 1:2])
        fw.ts(c16, c8, -16.0, ALU.mult)
        fw.ts(c8, c8, -8.0, ALU.mult)
        NL = TA - CTXN
        for jc in range(2):
            self.inproj(layer, 12 + jc, RT)
            fw.copy(self.rowc(RP, 0, CTXN), self.rowc(RT, 0, CTXN), eng="pool")
            fw.copy(self.rowc(RP, CTXN, TA), _bc(self.rowc(RT, CTXN, TA), [[1, 64], [64, 32]]), eng="pool")
            self.conv4(RX, RP, "lconv", jc)
            fw.ts(self.row(RX), self.row(RX), self.pvc("lconvb", jc), ALU.add)
            for d in range(2):
                for ti, (c0, c1) in enumerate(TOK_TILES):
                    n = c1 - c0
                    pa = self.pb[2 + ti % 2][:, 0:n]
                    px = self.pb[4 + ti % 2][:, 0:n]
                    fw.mm(pa, self.lwt[:, 0 * 4 + d * 2 + jc, :], self.rowc(RX, c0, c1))
                    fw.mm(px, self.lwt[:, 1 * 4 + d * 2 + jc, :], self.rowc(RX, c0, c1))
                    fw.act(self.rowc(RA, c0, c1), pa, AF.Sigmoid, bias=self.pvc("lba", d * 2 + jc))
                    fw.act(self.rowc(RI, c0, c1), px, AF.Sigmoid, bias=self.pvc("lbx", d * 2 + jc))
                k = d * 2 + jc
                fw.act(self.row(RB), self.row(RA), AF.Exp, scale=self.misc[:, 1028 + k:1029 + k])
                fw.ts(self.row(RB), self.row(RB), -1.0, ALU.mult, 1.0, ALU.add)
                fw.act(self.row(RB), self.row(RB), AF.Sqrt)
                fw.act(self.row(RA), self.row(RA), AF.Exp, scale=self.misc[:, 1024 + k:1025 + k])
                fw.tt(self.row(RI), self.row(RI), self.row(RX), ALU.mult, eng="pool")
                fw.tt(self.row(RB), self.row(RB), self.row(RI), ALU.mult)
                dst = RH if d == 0 else RG
                if d == 0:
                    fw.scan(self.row(dst), self.row(RA), self.row(RB), 0.0)
                else:
                    fw.scan(self.rowc(dst, 0, CTXN).w(lambda ap: _rev(ap, CTXN)),
                            self.rowc(RA, 0, CTXN).w(lambda ap: _rev(ap, CTXN)),
                            self.rowc(RB, 0, CTXN).w(lambda ap: _rev(ap, CTXN)), 0.0)
                    fw.scan(self.rowc(dst, CTXN, TA).w(lambda ap: _rev(ap, NL)),
                            self.rowc(RA, CTXN, TA).w(lambda ap: _rev(ap, NL)),
                            self.rowc(RB, CTXN, TA).w(lambda ap: _rev(ap, NL)), self.rowc(dst, 0, 1))
            fw.tt(self.row(RH), self.row(RH), self.row(RG), ALU.add)
            fw.copy(self.rowc(RP, 0, CTXN), self.rowc(RH, 0, CTXN), eng="pool")
            fw.copy(self.rowc(RP, CTXN, TA), _bc(self.rowc(RH, CTXN, TA), [[1, 32], [32, 64]]), eng="pool")
            self.inproj(layer, 14 + jc, RG)
            fw.act(self.row(RT), self.row(RG), AF.Square)
            fw.ts(self.row(RT), self.row(RT), 0.044715, ALU.mult, 1.0, ALU.add)
            fw.tt(self.row(RT), self.row(RT), self.row(RG), ALU.mult)
            fw.act(self.row(RT), self.row(RT), AF.Sigmoid, scale=1.5957691216057308)
            fw.tt(self.row(RT), self.row(RT), self.row(RG), ALU.mult)
            fw.tt(self.row(RP), self.row(RP), self.row(RT), ALU.mult)
            self.dump_row("mix%d" % layer, 3 + jc, 8, self.row(RP))
            self.store_o(3 + jc, self.row(RP))

    def phase_c(self, layer, last):
        fw = self.fw
        WSt = self.WS
        NS = 768
        hid = TA_(WSt, 0, [128, 32, NS], dtype=BF16)
        Y = TA_(WSt, 12288, [128, 8, NS])
        wdnb = [TA_(WSt, 18432 + 2048 * i, [128, 32, 128], dtype=BF16) for i in range(2)]
        xt = self.small[:, 0:2048].w(lambda ap: ap.rearrange("p (a b) -> p a b", a=8))
        sq = self.small[:, 2048:4096].w(lambda ap: ap.rearrange("p (a b) -> p a b", a=8))
        src = self.xin if layer == 0 else self.xr
        self.wcache = fw.dram("wcache%d" % layer, [72, 128, 1024], BF16)
        self.pending = []
        for st in range(3):
            first = (st == 0)
            pcs = [pc for pc in range(3 * st, 3 * st + 3) if not (last and pc == 0)]
            base = pcs[0] * PW_
            ncol = len(pcs) * PW_
            nt = []
            c = pcs[0] * PW_
            end = (pcs[-1] + 1) * PW_
            while c < end:
                n = min(512, end - c)
                if c < CTXN:
                    n = min(n, CTXN - c)
                nt.append((c - base, c - base + n))
                c += n
            Ot = self.H[:, :, 0:ncol]
            fw.dma(Ot, self.od[:, :, base:base + ncol].w(lambda ap: ap.rearrange("k p t -> p k t")))
            for m in range(8):
                wt = self.next_wbf()
                self.load_wc(wt.full(), self.wout[layer * 8 + m], m, first)
                for ti, (a, b) in enumerate(nt):
                    ps = self.pb[ti % 2][:, 0:b - a]
                    for kc in range(8):
                        fw.mm(ps, wt[:, kc, :], self.H[:, kc, a:b], start=(kc == 0), stop=(kc == 7))
                    fw.copy(Y[:, m, a:b], ps, eng=("act" if ti % 2 == 0 else "dve"))
            for pi, pc in enumerate(pcs):
                which = 1 if pc == 0 else 0
                lo = pi * PW_
                fw.dma(xt, src[pc].w(lambda ap: ap.rearrange("p (a b) -> p a b", a=8)))
                rstd = self.misc[:, 0:PW_]
                self.rms_rstd(Y[:, :, lo:lo + PW_], sq, PW_, rstd, self.pb[6][:, 0:PW_])
                for kc in range(8):
                    tmp = self.misc[:, (1 + kc % 2) * PW_:(2 + kc % 2) * PW_]
                    fw.tt(tmp, Y[:, kc, lo:lo + PW_], rstd, ALU.mult)
                    fw.stt(self._k(xt, kc), tmp, self.gs[:, 1, kc, which:which + 1], self._k(xt, kc), ALU.mult, ALU.add)
                if ("xmid%d" % layer) in self.dbg:
                    if ("xmid%d" % layer) not in self.dbg_out:
                        self.dbg_out["xmid%d" % layer] = fw.dram("dbg_xmid%d" % layer, [NPIECE, 128, 8 * PW_], F32, kind="ExternalOutput")
                    self.final.append(fw.dma(self.dbg_out["xmid%d" % layer][pc].w(lambda ap: ap.rearrange("p (a b) -> p a b", a=8)), xt))
                fw.dma(self.xr[pc].w(lambda ap: ap.rearrange("p (a b) -> p a b", a=8)), xt)
                rstd2 = self.misc[:, 3 * PW_:4 * PW_]
                self.rms_rstd(xt, sq, PW_, rstd2, self.pb[6][:, 0:PW_])
                for kc in range(8):
                    tmp = self.misc[:, (1 + kc % 2) * PW_:(2 + kc % 2) * PW_]
                    fw.tt(tmp, self._k(xt, kc), rstd2, ALU.mult)
                    fw.ts(self.H[:, kc, NS + lo:NS + lo + PW_], tmp, self.gs[:, 2, kc, which:which + 1], ALU.mult,
                          self.mod[:, 24 + kc, which:which + 1], ALU.add, eng=("pool" if kc % 2 else "dve"))
            for m in range(32):
                wt = self.next_wbf()
                self.load_wc(wt.full(), self.wup[layer * 32 + m], 8 + m, first)
                for ti, (a, b) in enumerate(nt):
                    ps = self.pb[ti % 2][:, 0:b - a]
                    for kc in range(8):
                        fw.mm(ps, wt[:, kc, :], self.H[:, kc, NS + a:NS + b], start=(kc == 0), stop=(kc == 7))
                    rl = self.small[:, 0:b - a] if ti % 2 == 0 else self.small[:, 512:512 + b - a]
                    fw.act(rl, ps, AF.Relu)
                    fw.tt(hid[:, m, a:b], rl, rl, ALU.mult, eng=("pool" if ti % 2 == 0 else "dve"))
            for m in range(8):
                wt = wdnb[m % 2]
                for kq in range(4):
                    self.load_wc(wt[:, 8 * kq:8 * kq + 8, :], self.wdn[layer * 8 + m, :, 1024 * kq:1024 * kq + 1024], 40 + 4 * m + kq, first)
                for ti, (a, b) in enumerate(nt):
                    ps = self.pb[ti % 2][:, 0:b - a]
                    for kc in range(32):
                        fw.mm(ps, wt[:, kc, :], hid[:, kc, a:b], start=(kc == 0), stop=(kc == 31))
                    fw.copy(Y[:, m, a:b], ps, eng=("act" if ti % 2 == 0 else "dve"))
            while self.pending:
                v, s = self.pending.pop(0)
                fw.dma(self.wcache[s].w(lambda ap: ap.rearrange("p (a b) -> p a b", a=8)), v)
            for pi, pc in enumerate(pcs):
                which = 1 if pc == 0 else 0
                lo = pi * PW_
                fw.dma(xt, self.xr[pc].w(lambda ap: ap.rearrange("p (a b) -> p a b", a=8)))
                rstd = self.misc[:, 0:PW_]
                self.rms_rstd(Y[:, :, lo:lo + PW_], sq, PW_, rstd, self.pb[6][:, 0:PW_])
                for kc in range(8):
                    tmp = self.misc[:, (1 + kc % 2) * PW_:(2 + kc % 2) * PW_]
                    fw.tt(tmp, Y[:, kc, lo:lo + PW_], rstd, ALU.mult)
                    fw.stt(self._k(xt, kc), tmp, self.gs[:, 3, kc, which:which + 1], self._k(xt, kc), ALU.mult, ALU.add)
                if last:
                    self.final.append(fw.dma(self.out_d[pc - 1].w(lambda ap: ap.rearrange("p (a b) -> p a b", a=8)), xt))
                else:
                    fw.dma(self.xr[pc].w(lambda ap: ap.rearrange("p (a b) -> p a b", a=8)), xt)

    def lerp(self, dst_row, raw_row, muidx):
        fw = self.fw
        for (a, b) in SEGS:
            fw.tt(self.rowc(dst_row, a + 1, b - 1), self.rowc(raw_row, a, b - 2), self.rowc(raw_row, a + 2, b), ALU.add)
            fw.copy(self.rowc(dst_row, a, a + 1), self.rowc(raw_row, a + 1, a + 2), eng="pool")
            fw.copy(self.rowc(dst_row, b - 1, b), self.rowc(raw_row, b - 2, b - 1), eng="pool")
        fw.ts(self.row(dst_row), self.row(dst_row), self.misc[:, 1040 + 12 + muidx:1040 + 13 + muidx], ALU.mult)
        fw.stt(self.row(dst_row), self.row(raw_row), self.misc[:, 1040 + muidx:1040 + muidx + 1], self.row(dst_row), ALU.mult, ALU.add)

    def rwkv(self, layer):
        fw = self.fw
        pb = self.pb
        RR, RV, RKK, RKA, RCW, RKD, RY, RBON, T1, T2 = range(10)
        o, w = PV_OFF["mu"]
        fw.ts(self.misc[:, 1040 + 0:1040 + 12], self.pv[:, o:o + 12], -1.0, ALU.mult, 1.0, ALU.add)
        fw.ts(self.misc[:, 1040 + 12:1040 + 24], self.pv[:, o:o + 12], 0.5, ALU.mult)
        o2, w2 = PV_OFF["ka"]
        fw.ts(self.misc[:, 1040 + 24:1040 + 27], self.pv[:, o2:o2 + 3], -1.0, ALU.mult, 1.0, ALU.add)
        spill = self.fw.dram("spill%d" % layer, [4, 128, TA], F32)
        WSt = self.WS
        bT = T1 * TA
        DER = TA_(WSt, bT, [128, 6, 4, 64])
        E1 = TA_(WSt, bT + 1536, [128, 4, 64])
        E2 = TA_(WSt, bT + 1792, [128, 4, 64])
        XY = TA_(WSt, bT + 2048, [64, 2, 2, 8, 64], dtype=BF16)
        ATKB = TA_(WSt, bT + 3072, [64, 8, 64], dtype=BF16)
        W1B = TA_(WSt, bT + 3328, [64, 8, 64], dtype=BF16)
        PC = TA_(WSt, bT + 4096, [128, 4])
        M = TA_(WSt, bT + 4104, [128, 64])
        US = TA_(WSt, bT + 4168, [64, 128])
        AHT = TA_(WSt, bT + 4296, [128, 4, 64])
        G = TA_(self.small, 0, [64, 2, 8, 64], dtype=BF16)
        MK = TA_(self.small, 1024, [64, 3, 8, 64])
        TK = TA_(self.small, 2560, [64, 3, 8, 64])
        VT = TA_(self.misc, 0, [64, 8, 64])
        W1 = TA_(self.misc, 512, [64, 8, 64])
        UV = TA_(self.misc, 1200, [64, 8, 64])
        p3 = lambda pbk: pbk[0:64, 0:512].w(lambda ap: ap.rearrange("p (a b) -> p a b", b=64))
        self.inproj(layer, 26, T1)
        self.lerp(T2, T1, 10)
        fw.dma(spill[1], self.row(T2))
        self.inproj(layer, 25, T1)
        self.lerp(T2, T1, 9)
        fw.act(self.row(T2), self.row(T2), AF.Tanh)
        fw.dma(spill[2], self.row(T2))
        self.inproj(layer, 27, T1)
        self.lerp(T2, T1, 11)
        fw.act(self.row(T2), self.row(T2), AF.Sigmoid)
        fw.dma(spill[3], self.row(T2))
        for hp in range(3):
            self.inproj(layer, 16 + hp, T1)
            self.lerp(RR, T1, hp)
            self.inproj(layer, 22 + hp, T1)
            self.lerp(RV, T1, 6 + hp)
            self.inproj(layer, 19 + hp, T1)
            self.lerp(T2, T1, 3 + hp)
            fw.dma(spill[0], self.row(T2))
            fw.ts(self.row(T1), self.row(T2), self.pvc("kk", hp), ALU.mult)
            self.headstat(RKK, T1, 0)
            fw.tt(self.row(RKK), self.row(RKK), self.row(T1), ALU.mult)
            for d in range(2):
                db = d * 64
                fw.dma(self.row(T2), spill[1])
                for ti, (c0, c1) in enumerate(TOK_TILES):
                    ps = pb[2 + ti % 2][:, 0:c1 - c0]
                    fw.mm(ps, self.rup[db:db + 64, 1, hp * 128:(hp + 1) * 128], self.rowc(T2, c0, c1, db, db + 64))
                    fw.act(self.rowc(RKA, c0, c1), ps, AF.Sigmoid, bias=self.pvc("a0", d * 3 + hp))
                fw.dma(self.row(T2), spill[0])
                fw.ts(self.row(T1), self.row(RKA), self.pvc("ka", hp), ALU.mult, self.misc[:, 1040 + 24 + hp:1040 + 25 + hp], ALU.add)
                fw.tt(self.row(RKD), self.row(T2), self.row(T1), ALU.mult)
                fw.tt(self.row(RKA), self.row(RKA), self.row(RKK), ALU.mult)
                if d == 0:
                    fw.stt(self.row(RBON), self.row(RR), self.pvc("rk", hp), self.row(RKD), ALU.mult, ALU.mult)
                else:
                    fw.stt(self.row(T1), self.row(RR), self.pvc("rk", hp), self.row(RKD), ALU.mult, ALU.mult)
                    fw.tt(self.row(RBON), self.row(RBON), self.row(T1), ALU.add)
                fw.dma(self.row(T2), spill[2])
                for ti, (c0, c1) in enumerate(TOK_TILES):
                    ps = pb[2 + ti % 2][:, 0:c1 - c0]
                    fw.mm(ps, self.rup[db:db + 64, 0, hp * 128:(hp + 1) * 128], self.rowc(T2, c0, c1, db, db + 64))
                    fw.act(self.rowc(T1, c0, c1), ps, AF.Sigmoid, bias=self.pvc("w0", d * 3 + hp))
                fw.ts(self.row(T1), self.row(T1), -0.6065306597126334, ALU.mult)
                onesb = _bc(self.epsc[:, 1:2], [[0, TA]])
                r3 = lambda ap: ap.rearrange("p (a b) -> p a b", b=64)
                if d == 0:
                    fw.scan(self.row(T2), onesb, self.row(T1), 0.0)
                    fw.tt(self.rowc(RCW, 64, TA).w(r3), self.rowc(T2, 64, TA).w(r3),
                          _bc(self.rowc(T2, 63, 64), [[64, 35], [0, 64]]), ALU.subtract)
                    fw.copy(self.rowc(RCW, 0, 64), self.rowc(T2, 0, 64), eng="pool")
                else:
                    fw.scan(self.row(T2).w(lambda ap: _rev(ap, TA)), onesb, self.row(T1).w(lambda ap: _rev(ap, TA)), 0.0)
                    fw.tt(self.rowc(RCW, 0, TA - 64).w(r3), self.rowc(T2, 0, TA - 64).w(r3),
                          _bc(self.rowc(T2, 64, 65), [[64, 35], [0, 64]]), ALU.subtract)
                    fw.copy(self.rowc(RCW, TA - 64, TA), self.rowc(T2, TA - 64, TA), eng="pool")
                if d == 0 and hp == 0:
                    self.dump_row("rw_cw%d" % layer, 0, 1, self.row(RCW))
                    self.dump_row("rw_kd%d" % layer, 0, 1, self.row(RKD))
                    self.dump_row("rw_ka%d" % layer, 0, 1, self.row(RKA))
                fw.memset(M.full(), 0.0)
                sgn_strict_ts = C_SL if d == 0 else C_SU
                sgn_strict_st = C_SU if d == 0 else C_SL
                incl_st = C_UI if d == 0 else C_LI
                for b in range(9):
                    if d == 0:
                        w0c = 256 * b
                    else:
                        w0c = 0 if b == 0 else TA - 256 * b
                    win = lambda r: self.rowc(r, w0c, w0c + 256).w(lambda ap: ap.rearrange("p (a b) -> p a b", b=64))
                    lastp = 63 if d == 0 else 0
                    cwl = _bc(self.rowc(RCW, w0c + lastp, w0c + lastp + 1), [[64, 4], [0, 64]])
                    fw.act(E1.full(), win(RCW), AF.Exp)
                    fw.act(E2.full(), win(RCW), AF.Exp, scale=-1.0)
                    fw.tt(DER[:, 3], win(RR), E1.full(), ALU.mult)
                    fw.stt(DER[:, 1], win(RKA), -1.0, E2.full(), ALU.mult, ALU.mult)
                    fw.tt(DER[:, 2], win(RKD), E2.full(), ALU.mult, eng="pool")
                    if d == 0:
                        fw.tt(DER[:, 0, :, 1:64], self.rowc(RKK, w0c, w0c + 256).w(lambda ap: ap.rearrange("p (a b) -> p a b", b=64)[:, :, 1:64]),
                              E1[:, :, 0:63], ALU.mult)
                        fw.copy(DER[:, 0, :, 0:1], self.rowc(RKK, w0c, w0c + 256).w(lambda ap: ap.rearrange("p (a b) -> p a b", b=64)[:, :, 0:1]), eng="pool")
                    else:
                        fw.tt(DER[:, 0, :, 0:63], self.rowc(RKK, w0c, w0c + 256).w(lambda ap: ap.rearrange("p (a b) -> p a b", b=64)[:, :, 0:63]),
                              E1[:, :, 1:64], ALU.mult)
                        fw.copy(DER[:, 0, :, 63:64], self.rowc(RKK, w0c, w0c + 256).w(lambda ap: ap.rearrange("p (a b) -> p a b", b=64)[:, :, 63:64]), eng="pool")
                    fw.tt(E2.full(), cwl, win(RCW), ALU.subtract)
                    fw.act(E2.full(), E2.full(), AF.Exp)
                    fw.tt(DER[:, 4], win(RKD), E2.full(), ALU.mult, eng="pool")
                    fw.stt(DER[:, 5], win(RKA), -1.0, E2.full(), ALU.mult, ALU.mult)
                    fw.act(PC.full(), _bc(self.rowc(RCW, w0c + lastp, w0c + lastp + 1), [[64, 4]]), AF.Exp)
                    wcs = [cs if d == 0 else 3 - cs for cs in range(4)]
                    units = [(cs, hh) for cs in range(4) for hh in range(2)]
                    fm = lambda q, wc, hb: DER[hb:hb + 64, q, wc, :]
                    for u, (cs, hh) in sorted(enumerate(units), key=lambda t: (t[1][1], t[1][0])):
                        wc, hb = wcs[cs], hh * 64
                        sl = slice(u * 64, (u + 1) * 64)
                        fw.mm(pb[5][0:64, sl], fm(0, wc, hb), fm(1, wc, hb))
                        fw.mm(pb[6][0:64, sl], fm(1, wc, hb), fm(0, wc, hb))
                        fw.mm(pb[7][0:64, sl], fm(2, wc, hb), fm(0, wc, hb))
                    mk = lambda ci: _bc(self.C(ci, 0, 64, 0, 64), [[0, 8], [1, 64]])
                    fw.tt(XY[:, 0, 0, :, :], p3(pb[5]), mk(sgn_strict_ts), ALU.mult)
                    fw.tt(XY[:, 0, 1, :, :], p3(pb[6]), mk(sgn_strict_st), ALU.mult)
                    fw.tt(MK[:, 0, :, :], p3(pb[7]), mk(sgn_strict_st), ALU.mult)
                    for u, (cs, hh) in sorted(enumerate(units), key=lambda t: (t[1][1], t[1][0])):
                        wc, hb = wcs[cs], hh * 64
                        sl = slice(u * 64, (u + 1) * 64)
                        fw.mm(pb[5][0:64, sl], fm(1, wc, hb), fm(3, wc, hb))
                        fw.mm(pb[6][0:64, sl], fm(2, wc, hb), fm(3, wc, hb))
                    fw.tt(MK[:, 1, :, :], p3(pb[5]), mk(incl_st), ALU.mult)
                    fw.tt(MK[:, 2, :, :], p3(pb[6]), mk(incl_st), ALU.mult)
                    for u, (cs, hh) in sorted(enumerate(units), key=lambda t: (t[1][1], t[1][0])):
                        wc, hb = wcs[cs], hh * 64
                        sl = slice(u * 64, (u + 1) * 64)
                        idn = self.C(C_ID, hb, hb + 64, hb, hb + 64)
                        fw.tr(pb[0][0:64, sl], fm(0, wc, hb), idn)
                        fw.tr(pb[1][0:64, sl], fm(4, wc, hb), idn)
                        fw.tr(pb[7][0:64, sl], fm(5, wc, hb), idn)
                        c0 = w0c + wc * 64
                        fw.tr(pb[5][0:64, sl], self.rowc(RV, c0, c0 + 64, hb, hb + 64), idn)
                    fw.copy(ATKB.full(), p3(pb[0]), eng="act")
                    fw.copy(TK[:, 1, :, :], p3(pb[1]), eng="dve")
                    fw.copy(TK[:, 2, :, :], p3(pb[7]), eng="act")
                    fw.copy(VT.full(), p3(pb[5]), eng="dve")
                    gi = self.neumann(64, 8, 64, +1, 5, XY, G, bf=True)
                    for u in range(8):
                        fw.mm(pb[5][0:64, u * 64:(u + 1) * 64], MK[:, 0, u, :], VT[:, u, :])
                    fw.copy(W1B.full(), p3(pb[5]), eng="act")
                    for u, (cs, hh) in sorted(enumerate(units), key=lambda t: (t[1][1], t[1][0])):
                        hb = hh * 64
                        fw.mm(pb[6][0:64, u * 64:(u + 1) * 64], G[:, gi, u, :], W1B[:, u, :])
                        fw.mm(pb[7][hb:hb + 64, cs * 64:(cs + 1) * 64], ATKB[:, u, :], G[:, gi, u, :])
                    fw.copy(UV.full(), p3(pb[6]), eng="dve")
                    fw.copy(AHT.full(), pb[7][:, 0:256].w(lambda ap: ap.rearrange("p (a b) -> p a b", b=64)), eng="act")
                    for cs in range(4):
                        wc = wcs[cs]
                        c0 = w0c + wc * 64
                        for hh in range(2):
                            hb = hh * 64
                            fw.mm(pb[0][0:64, hh * 64:(hh + 1) * 64], AHT[hb:hb + 64, cs, :], M[hb:hb + 64, :])
                        fw.tt(US.full(), UV[:, 2 * cs:2 * cs + 2, :].w(lambda ap: ap.rearrange("p a b -> p (a b)")), pb[0][0:64, 0:128], ALU.add)
                        for hh in range(2):
                            hb = hh * 64
                            u = cs * 2 + hh
                            fw.mm(pb[1][hb:hb + 64, 0:64], M[hb:hb + 64, :], fm(3, wc, hb), start=True, stop=False)
                            fw.mm(pb[1][hb:hb + 64, 0:64], US[:, hh * 64:(hh + 1) * 64], MK[:, 1, u, :], start=False, stop=False)
                            fw.mm(pb[1][hb:hb + 64, 0:64], VT[:, u, :], MK[:, 2, u, :], start=False, stop=True)
                        if d == 0:
                            fw.copy(self.rowc(RY, c0, c0 + 64), pb[1][:, 0:64], eng="act")
                        else:
                            fw.tt(self.rowc(RY, c0, c0 + 64), self.rowc(RY, c0, c0 + 64), pb[1][:, 0:64], ALU.add)
                        for hh in range(2):
                            hb = hh * 64
                            u = cs * 2 + hh
                            fw.mm(pb[0][hb:hb + 64, 256:320], TK[:, 1, u, :], VT[:, u, :], start=True, stop=False)
                            fw.mm(pb[0][hb:hb + 64, 256:320], TK[:, 2, u, :], US[:, hh * 64:(hh + 1) * 64], start=False, stop=True)
                        fw.stt(M.full(), M.full(), PC[:, wc:wc + 1], pb[0][:, 256:320], ALU.mult, ALU.add)
            self.dump_row("rw_y%d" % layer, hp, 3, self.row(RY))
            for ti, (c0, c1) in enumerate(TOK_TILES):
                n = c1 - c0
                ps = pb[2 + ti % 2][:, 0:n]
                fw.mm(ps, self.C(C_BO), self.rowc(RY, c0, c1))
                fw.stt(self.rowc(T1, c0, c1), ps, -1.0 / 64.0, self.rowc(RY, c0, c1), ALU.mult, ALU.add)
            self.headstat(T2, T1, 2, scale=1.0 / 64.0)
            fw.tt(self.row(T1), self.row(T1), self.row(T2), ALU.mult)
            fw.ts(self.row(T1), self.row(T1), self.pvc("gnw", hp), ALU.mult, self.pvc("gnb", hp), ALU.add)
            for ti, (c0, c1) in enumerate(TOK_TILES):
                n = c1 - c0
                ps = pb[2 + ti % 2][:, 0:n]
                fw.mm(ps, self.C(C_BO), self.rowc(RBON, c0, c1))
                fw.tt(self.rowc(T2, c0, c1), ps, self.rowc(RV, c0, c1), ALU.mult)
            fw.tt(self.row(T1), self.row(T1), self.row(T2), ALU.add)
            fw.dma(self.row(T2), spill[3])
            for ti, (c0, c1) in enumerate(TOK_TILES):
                n = c1 - c0
                ps = pb[2 + ti % 2][:, 0:n]
                fw.mm(ps, self.rup[:, 2, hp * 128:(hp + 1) * 128], self.rowc(T2, c0, c1))
                fw.tt(self.rowc(T1, c0, c1), self.rowc(T1, c0, c1), ps, ALU.mult)
            self.dump_row("mix%d" % layer, 5 + hp, 8, self.row(T1))
            self.store_o(5 + hp, self.row(T1))


def kernel(**inputs):
    inp = {k: np.asarray(v) for k, v in inputs.items()}
    shared = host_shared(inp)
    nc = bass.Bass("TRN2", target_bir_lowering=False)
    prog = Prog(nc)
    prog.build()
    in_maps = []
    for b in range(8):
        m = dict(shared)
        m.update(host_core(inp, b))
        in_maps.append(m)
    res = run_bass_kernel_spmd(nc, in_maps, core_ids=list(range(8)))
    out = np.empty((8, 2048, 1024), np.float32)
    for b in range(8):
        o = np.asarray(res.results[b]["out"]).reshape(8, 128, 8, PW_)
        out[b] = o.transpose(0, 3, 2, 1).reshape(2048, 1024)
    return out
```

```python
import numpy as np
import concourse.bass as bass
import concourse.mybir as mybir
from concourse.bass_utils import run_bass_kernel_spmd

F32 = mybir.dt.float32
BF16 = mybir.dt.bfloat16
AF = mybir.ActivationFunctionType
ALU = mybir.AluOpType

SEM_CHUNK = 30000
COMPUTE = ("pe", "act", "dve", "pool")


class V:
    def __init__(self, tt, ap, box):
        self.tt, self.ap, self.box = tt, ap, box

    def w(self, fn):
        return V(self.tt, fn(self.ap), self.box)


class TT:
    def __init__(self, fw, name, handle, shape, is_dram=False, is_psum=False):
        self.fw, self.name, self.h, self.shape = fw, name, handle, list(shape)
        self.is_dram = is_dram
        self.is_psum = is_psum
        self.recs = []
        st = [1] * len(shape)
        for i in range(len(shape) - 2, 0, -1):
            st[i] = st[i + 1] * shape[i + 1]
        st[0] = 0
        self.strides = st

    def __getitem__(self, idx):
        if not isinstance(idx, tuple):
            idx = (idx,)
        idx = list(idx) + [slice(None)] * (len(self.shape) - len(idx))
        lo, hi = [], []
        for d, (i, n) in enumerate(zip(idx, self.shape)):
            if isinstance(i, int):
                a, b = i, i + 1
                idx[d] = slice(i, i + 1) if (d == 0 and not self.is_dram) else i
            else:
                a, b, s = i.indices(n)
                assert s == 1
            lo.append(a)
            hi.append(b)
        f0 = sum(lo[d] * self.strides[d] for d in range(1, len(self.shape)))
        f1 = sum((hi[d] - 1) * self.strides[d] for d in range(1, len(self.shape))) + 1
        base = self.h.ap() if self.is_dram else self.h
        ap = base[tuple(idx)]
        if self.is_psum:
            f0, f1 = 0, 1 << 30
            lo[0], hi[0] = (lo[0] // 32) * 32, ((hi[0] + 31) // 32) * 32
        return V(self, ap, (lo[0], hi[0], f0, f1))

    def full(self):
        return self[tuple(slice(None) for _ in self.shape)]


class TA_(TT):
    def __init__(self, parent, off, shape, dtype=None, pbase=0):
        self.parent = parent
        self.shape = list(shape)
        self.is_dram = False
        self.off = off
        self.pbase = pbase
        self.ratio = 2 if dtype is not None else 1
        n = 1
        for s_ in shape[1:]:
            n *= s_
        nf = (n + self.ratio - 1) // self.ratio
        flat = parent.h[pbase:pbase + shape[0], off:off + nf]
        if dtype is not None:
            flat = flat.bitcast(dtype)
        names = " ".join("d%d" % i for i in range(1, len(shape)))
        kw = {"d%d" % i: shape[i] for i in range(1, len(shape))}
        self.base = flat.rearrange("p (%s) -> p %s" % (names, names), **kw) if len(shape) > 2 else flat
        st = [1] * len(shape)
        for i in range(len(shape) - 2, 0, -1):
            st[i] = st[i + 1] * shape[i + 1]
        st[0] = 0
        self.strides = st

    @property
    def recs(self):
        return self.parent.recs

    @recs.setter
    def recs(self, v):
        self.parent.recs = v

    def __getitem__(self, idx):
        if not isinstance(idx, tuple):
            idx = (idx,)
        idx = list(idx) + [slice(None)] * (len(self.shape) - len(idx))
        lo, hi = [], []
        for d, (i, n) in enumerate(zip(idx, self.shape)):
            if isinstance(i, int):
                a, b = i, i + 1
                if d == 0:
                    idx[d] = slice(i, i + 1)
            else:
                a, b, s = i.indices(n)
                assert s == 1
            lo.append(a)
            hi.append(b)
        f0 = sum(lo[d] * self.strides[d] for d in range(1, len(self.shape)))
        f1 = sum((hi[d] - 1) * self.strides[d] for d in range(1, len(self.shape))) + 1
        ap = self.base[tuple(idx)]
        r = self.ratio
        return V(self, ap, (self.pbase + lo[0], self.pbase + hi[0], self.off + f0 // r, self.off + (f1 + r - 1) // r))


def _overlap(a, b):
    return a[0] < b[1] and b[0] < a[1] and a[2] < b[3] and b[2] < a[3]


def _covers(a, b):
    return a[0] <= b[0] and a[1] >= b[1] and a[2] <= b[2] and a[3] >= b[3]


class FW:
    def __init__(self, nc, n_dma_sems=12):
        self.nc = nc
        self.ops = {e: [] for e in ("pe", "act", "dve", "pool", "sp")}
        self.tick = {e: 0 for e in COMPUTE}
        self.waited = {e: {} for e in self.ops}
        self.sems = {}
        self.n_dma_sems = n_dma_sems
        self.dma_cnt = {}
        self.dma_uses = {}
        self.stack = None
        self.n_ops = 0
        self.out_tokens = []

    def sbuf(self, name, shape, dtype=F32):
        h = self.nc.alloc_sbuf_tensor(name, list(shape), dtype)
        return TT(self, name, h, shape)

    def psum(self, name, shape, dtype=F32):
        h = self.nc.alloc_psum_tensor(name, list(shape), dtype)
        return TT(self, name, h, shape, is_psum=True)

    def dram(self, name, shape, dtype=F32, kind="Internal"):
        h = self.nc.dram_tensor(name, list(shape), dtype, kind=kind)
        return TT(self, name, h, shape, is_dram=True)

    def _sem(self, key):
        if key not in self.sems:
            self.sems[key] = self.nc.alloc_semaphore("s_%s_%s" % key)
        return self.sems[key]

    def _token_wait(self, eng, tok, force=False):
        kind = tok[0]
        if kind == "c":
            _, pe, n = tok
            if pe == "pe" and eng == "pe" and not force:
                return []
            if self.waited[eng].get(("c", pe), 0) >= n:
                return []
            self.waited[eng][("c", pe)] = n
            return [((pe, (n - 1) // SEM_CHUNK), (n - 1) % SEM_CHUNK + 1)]
        else:
            _, q, j, m = tok
            if self.waited[eng].get(("d", q, j), 0) >= m:
                return []
            self.waited[eng][("d", q, j)] = m
            return [(("dma" + q, j), 16 * m)]

    def op(self, eng, emit, reads=(), writes=(), dma=False, force=()):
        self.n_ops += 1
        if dma:
            q = eng
            i = self.dma_cnt.get(q, 0)
            self.dma_cnt[q] = i + 1
            j = i % self.n_dma_sems
            m = self.dma_uses.get((q, j), 0) + 1
            self.dma_uses[(q, j)] = m
            token = ("d", q, j, m)
            inc = (("dma" + q, j), 16)
        else:
            self.tick[eng] += 1
            n = self.tick[eng]
            token = ("c", eng, n)
            inc = ((eng, (n - 1) // SEM_CHUNK), 1)
        deps = []
        for v in reads:
            psum = getattr(v.tt, "is_psum", False)
            for r in v.tt.recs:
                if (r[3] or (psum and r[2] != eng)) and _overlap(r[0], v.box):
                    deps.append(r[1])
        for v in writes:
            for r in v.tt.recs:
                if _overlap(r[0], v.box):
                    deps.append(r[1])
        if dma and token[3] > 1:
            deps.append(("d", token[1], token[2], token[3] - 1))
        waits = []
        for t in deps:
            waits += self._token_wait(eng, t)
        for t in force:
            waits += self._token_wait(eng, t, force=True)
        for v in writes:
            v.tt.recs = [r for r in v.tt.recs if not _covers(v.box, r[0])]
            v.tt.recs.append((v.box, token, eng, True))
        for v in reads:
            recs = v.tt.recs
            for k, r in enumerate(recs):
                if (not r[3]) and r[2] == eng and r[0] == v.box and r[1][0] == token[0]:
                    recs[k] = (v.box, token, eng, False)
                    break
            else:
                recs.append((v.box, token, eng, False))
        self.ops[eng].append((waits, emit, inc))
        return token

    def wait_tokens(self, eng, tokens):
        waits = []
        for t in tokens:
            waits += self._token_wait(eng, t)
        self.ops[eng].append((waits, None, None))

    def emit(self):
        nc = self.nc
        for e in self.ops:
            for waits, _, inc in self.ops[e]:
                for k, _v in waits:
                    self._sem(k)
                if inc is not None:
                    self._sem(inc[0])
        engmap = {"pe": "tensor", "act": "scalar", "dve": "vector", "pool": "gpsimd", "sp": "sync"}
        with nc.Block() as block:
            for e, ops in self.ops.items():
                def body(engine, ops=ops):
                    for waits, emit, inc in ops:
                        for k, val in waits:
                            engine.wait_ge(self.sems[k], val)
                        if emit is not None:
                            inst = emit(engine)
                            inst.then_inc(self.sems[inc[0]], inc[1])
                getattr(block, engmap[e])(body)

    def dma(self, out, in_, q="sp", **kw):
        return self.op(q, lambda e: e.dma_start(out=out.ap, in_=in_.ap, **kw),
                       reads=[in_], writes=[out], dma=True)

    def _pe_cfg(self, out, lhsT, kind="M"):
        cfg = (kind, lhsT.box[0], lhsT.box[1], out.box[0], out.box[1])
        tt = out.tt
        force = ()
        last = getattr(tt, "pe_last", None)
        if last is not None and last[0] != cfg:
            force = (last[1],)
        dcls = str(lhsT.ap.dtype)
        gl = getattr(self, "pe_glast", None)
        if gl is not None and gl[0] != dcls:
            force = force + (gl[1],)
        self._pe_dcls = dcls
        return cfg, force

    def mm(self, out, lhsT, rhs, start=True, stop=True):
        cfg, force = self._pe_cfg(out, lhsT)
        tok = self.op("pe", lambda e: e.matmul(out.ap, lhsT.ap, rhs.ap, start=start, stop=stop),
                      reads=[lhsT, rhs], writes=[out], force=force)
        out.tt.pe_last = (cfg, tok)
        self.pe_glast = (self._pe_dcls, tok)
        return tok

    def tr(self, out, in_, ident):
        cfg, force = self._pe_cfg(out, in_, kind="T")
        tok = self.op("pe", lambda e: e.transpose(out.ap, in_.ap, ident.ap),
                      reads=[in_, ident], writes=[out], force=force)
        out.tt.pe_last = (cfg, tok)
        self.pe_glast = (self._pe_dcls, tok)
        return tok

    def act(self, out, in_, func, bias=None, scale=None, accum=None):
        reads = [in_]
        kw = {}
        if isinstance(bias, V):
            reads.append(bias)
            kw["bias"] = bias.ap
        elif bias is not None:
            kw["bias"] = bias
        if isinstance(scale, V):
            reads.append(scale)
            kw["scale"] = scale.ap
        elif scale is not None:
            kw["scale"] = scale
        writes = [out]
        if accum is not None:
            writes.append(accum)
            kw["accum_out"] = accum.ap
        return self.op("act", lambda e: e.activation(out.ap, in_.ap, func, **kw),
                       reads=reads, writes=writes)

    def tt(self, out, a, b, op, eng="dve"):
        return self.op(eng, lambda e: e.tensor_tensor(out.ap, a.ap, b.ap, op),
                       reads=[a, b], writes=[out])

    def ts(self, out, a, s1, op0, s2=None, op1=None, eng="dve", accum=None):
        reads = [a]
        x1 = s1.ap if isinstance(s1, V) else s1
        x2 = s2.ap if isinstance(s2, V) else s2
        if isinstance(s1, V):
            reads.append(s1)
        if isinstance(s2, V):
            reads.append(s2)
        writes = [out]
        kw = {}
        if accum is not None:
            writes.append(accum)
            kw["accum_out"] = accum.ap
        if op1 is None:
            return self.op(eng, lambda e: e.tensor_scalar(out.ap, a.ap, x1, None, op0, **kw),
                           reads=reads, writes=writes)
        return self.op(eng, lambda e: e.tensor_scalar(out.ap, a.ap, x1, x2, op0, op1, **kw),
                       reads=reads, writes=writes)

    def stt(self, out, a, s, b, op0, op1, eng="dve"):
        reads = [a, b]
        x = s.ap if isinstance(s, V) else s
        if isinstance(s, V):
            reads.append(s)
        return self.op(eng, lambda e: e.scalar_tensor_tensor(out.ap, a.ap, x, b.ap, op0, op1),
                       reads=reads, writes=[out])

    def copy(self, out, in_, eng="dve"):
        if eng == "act":
            return self.op("act", lambda e: e.copy(out.ap, in_.ap), reads=[in_], writes=[out])
        return self.op(eng, lambda e: e.tensor_copy(out.ap, in_.ap), reads=[in_], writes=[out])

    def memset(self, out, val, eng="dve"):
        return self.op(eng, lambda e: e.memset(out.ap, val), writes=[out])

    def scan(self, out, d0, d1, init, op0=ALU.mult, op1=ALU.add):
        reads = [d0, d1]
        x = init.ap if isinstance(init, V) else init
        if isinstance(init, V):
            reads.append(init)
        return self.op("dve", lambda e: e.tensor_tensor_scan(out.ap, d0.ap, d1.ap, x, op0, op1),
                       reads=reads, writes=[out])

    def recip(self, out, in_):
        return self.op("dve", lambda e: e.reciprocal(out.ap, in_.ap), reads=[in_], writes=[out])

TA = 2304
CTXN = 256
NPIECE = 9
PW_ = 256
L = 2
IN_OFF = [j * 128 if j < 12 else 1560 + (j - 12) * 128 for j in range(28)]
NEG = 30000.0


def _rev(ap, n):
    return bass.AP(tensor=ap.tensor, offset=ap.offset + (n - 1), ap=[list(ap.ap[0]), [-1, n]])


PV_SPEC = [("g_pre", 8), ("g_post", 8), ("g_fpre", 8), ("g_fpost", 8), ("ada_b", 48),
           ("gconv", 36), ("gnorm", 1), ("lconv", 8), ("lconvb", 2), ("lba", 4), ("lbx", 4),
           ("llam", 4), ("mu", 12), ("w0", 6), ("a0", 6), ("kk", 3), ("ka", 3), ("rk", 3),
           ("gnw", 3), ("gnb", 3)]
PV_OFF = {}
_o = 0
for _n, _w in PV_SPEC:
    PV_OFF[_n] = (_o, _w)
    _o += _w
NPV = _o
NCST = 13


def _cst():
    r = np.arange(128)[:, None]
    c = np.arange(128)[None, :]
    UI = (r <= c).astype(np.float32)
    LI = (r >= c).astype(np.float32)
    ident = np.eye(128, dtype=np.float32)
    ones = np.ones((128, 128), np.float32)
    bo = np.zeros((128, 128), np.float32)
    bo[:64, :64] = 1
    bo[64:, 64:] = 1
    mats = [ident, ones, bo, UI, LI, NEG * UI, NEG * LI, -NEG * UI, -NEG * LI,
            -NEG * (1 - UI), -NEG * (1 - LI), 1 - UI, 1 - LI]
    return np.ascontiguousarray(np.stack(mats, axis=1))


C_ID, C_ONES, C_BO, C_UI, C_LI, C_PUI, C_PLI, C_NUI, C_NLI, C_NSL, C_NSU, C_SL, C_SU = range(13)


def _kc(w):
    K = w.shape[0]
    return np.ascontiguousarray(w.reshape(K // 128, 128, w.shape[1]).transpose(1, 0, 2))


def _colchunks(w, offs):
    K = w.shape[0]
    return np.ascontiguousarray(
        np.stack([_kc(w[:, o:o + 128]).reshape(128, (K // 128) * 128) for o in offs]))


def _pcol(v, n):
    return np.ascontiguousarray(np.asarray(v).reshape(n, 128).T)


def host_shared(inp):
    d = {}
    d["cst"] = _cst()
    d["adaw"] = np.stack([_colchunks(inp["ada_w"][i], [j * 128 for j in range(48)]) for i in range(L)]).reshape(L * 48, 128, 1024)
    d["win"] = np.stack([_colchunks(inp["w_in"][i], IN_OFF) for i in range(L)]).reshape(L * 28, 128, 1024)
    d["wba"] = np.stack([_kc(inp["w_in"][i][:, 1536:1560]).reshape(128, 8 * 24) for i in range(L)])
    d["wout"] = np.stack([_colchunks(inp["w_out"][i], [j * 128 for j in range(8)]) for i in range(L)]).reshape(L * 8, 128, 1024)
    d["wup"] = np.stack([_colchunks(inp["ffn_up"][i], [j * 128 for j in range(32)]) for i in range(L)]).reshape(L * 32, 128, 1024)
    d["wdn"] = np.stack([_colchunks(inp["ffn_down"][i], [j * 128 for j in range(8)]) for i in range(L)]).reshape(L * 8, 128, 4096)
    pv = np.zeros((L, 128, NPV), np.float32)
    rowt = np.zeros((L, 128, 24), np.float32)
    lw = np.zeros((L, 128, 8, 128), np.float32)
    rup = np.zeros((L, 128, 3, 384), np.float32)
    for i in range(L):
        def put(name, arr):
            o, w = PV_OFF[name]
            pv[i, :, o:o + w] = arr
        put("g_pre", _pcol(inp["norm_mix_pre"][i], 8))
        put("g_post", _pcol(inp["norm_mix_post"][i], 8))
        put("g_fpre", _pcol(inp["norm_ffn_pre"][i], 8))
        put("g_fpost", _pcol(inp["norm_ffn_post"][i], 8))
        put("ada_b", _pcol(inp["ada_b"][i], 48))
        gc = inp["gdn_conv"][i]
        put("gconv", np.stack([_pcol(gc[k], 9) for k in range(4)], axis=2).reshape(128, 36))
        put("gnorm", np.tile(inp["gdn_norm"][i], 2)[:, None])
        lc = inp["lru_conv"][i]
        put("lconv", np.stack([_pcol(lc[k], 2) for k in range(4)], axis=2).reshape(128, 8))
        put("lconvb", _pcol(inp["lru_conv_b"][i], 2))
        put("lba", np.concatenate([_pcol(inp["lru_ba"][i][dd], 2) for dd in range(2)], axis=1))
        put("lbx", np.concatenate([_pcol(inp["lru_bx"][i][dd], 2) for dd in range(2)], axis=1))
        put("llam", np.concatenate([_pcol(inp["lru_lambda"][i][dd], 2) for dd in range(2)], axis=1))
        put("mu", _pcol(inp["rwkv_mu"][i], 12))
        put("w0", np.concatenate([_pcol(inp["rwkv_w0"][i][dd], 3) for dd in range(2)], axis=1))
        put("a0", np.concatenate([_pcol(inp["rwkv_a0"][i][dd], 3) for dd in range(2)], axis=1))
        put("kk", _pcol(inp["rwkv_k_k"][i], 3))
        put("ka", _pcol(inp["rwkv_k_a"][i], 3))
        put("rk", _pcol(inp["rwkv_r_k"][i].reshape(-1), 3))
        put("gnw", _pcol(inp["rwkv_gn_w"][i], 3))
        put("gnb", _pcol(inp["rwkv_gn_b"][i], 3))
        rowt[i, :, 0:12] = inp["gdn_a_log"][i].reshape(1, 12)
        rowt[i, :, 12:24] = inp["gdn_dt_bias"][i].reshape(1, 12)
        for ax, nm in enumerate(("lru_wa", "lru_wx")):
            for dd in range(2):
                for jc in range(2):
                    for bl in range(2):
                        lw[i, bl * 64:(bl + 1) * 64, ax * 4 + dd * 2 + jc, bl * 64:(bl + 1) * 64] = inp[nm][i][dd, 2 * jc + bl]
        rup[i, :, 0, :] = inp["rwkv_w_up"][i].reshape(128, 384)
        rup[i, :, 1, :] = inp["rwkv_a_up"][i].reshape(128, 384)
        rup[i, :, 2, :] = inp["rwkv_g_up"][i]
    d["pv"] = pv
    d["rowt"] = rowt
    d["lw"] = lw.reshape(L, 128, 1024)
    d["rup"] = rup.reshape(L, 128, 3 * 384)
    return d


def host_core(inp, b):
    xt = np.concatenate([inp["ctx"][b], inp["x"][b]], axis=0)
    xin = np.ascontiguousarray(xt.reshape(NPIECE, PW_, 8, 128).transpose(0, 3, 2, 1)).reshape(NPIECE, 128, 8 * PW_)
    cc = np.stack([inp["c"][b], inp["c_ctx"]], axis=1)
    call = np.ascontiguousarray(cc.reshape(8, 128, 2).transpose(1, 0, 2)).reshape(128, 16)
    return {"xin": xin, "call": call}


NROW = 10
GDN_ORDER = [list(range(18)), [1, 0] + list(range(17, 1, -1))]
RWKV_ORDER = [list(range(36)), [3, 2, 1, 0] + list(range(35, 3, -1))]
TOK_TILES = [(0, 256), (256, 768), (768, 1280), (1280, 1792), (1792, 2304)]
SEGS = [(0, CTXN), (CTXN, TA)]


def _bc(view, dims):
    return view.w(lambda ap: bass.AP(tensor=ap.tensor, offset=ap.offset, ap=[list(ap.ap[0])] + [list(d) for d in dims]))


class _Stop(Exception):
    pass


class Prog:
    def stop(self, tag):
        if self.stop_tag == tag:
            raise _Stop()

    def __init__(self, nc, dbg=(), nlayers=L, stop_after=None):
        self.nc = nc
        self.fw = FW(nc)
        self.dbg = set(dbg)
        self.dbg_out = {}
        self.final = []
        self.nlayers = nlayers
        self.stop_after = stop_after
        self.stop_tag = None
        self.skip = set()
        self.alloc()

    def alloc(self):
        fw = self.fw
        EI = "ExternalInput"
        self.xin = fw.dram("xin", [NPIECE, 128, 8 * PW_], F32, EI)
        self.call = fw.dram("call", [128, 16], F32, EI)
        self.cst_d = fw.dram("cst", [128, NCST, 128], F32, EI)
        self.adaw = fw.dram("adaw", [L * 48, 128, 1024], F32, EI)
        self.win = fw.dram("win", [L * 28, 128, 1024], F32, EI)
        self.wba_d = fw.dram("wba", [L, 128, 8 * 24], F32, EI)
        self.wout = fw.dram("wout", [L * 8, 128, 1024], F32, EI)
        self.wup = fw.dram("wup", [L * 32, 128, 1024], F32, EI)
        self.wdn = fw.dram("wdn", [L * 8, 128, 4096], F32, EI)
        self.pv_d = fw.dram("pv", [L, 128, NPV], F32, EI)
        self.rowt_d = fw.dram("rowt", [L, 128, 24], F32, EI)
        self.lw_d = fw.dram("lw", [L, 128, 1024], F32, EI)
        self.rup_d = fw.dram("rup", [L, 128, 3 * 384], F32, EI)
        self.out_d = fw.dram("out", [8, 128, 8 * PW_], F32, "ExternalOutput")
        self.xr = fw.dram("xr", [NPIECE, 128, 8 * PW_], F32)
        self.od = fw.dram("od", [8, 128, TA], BF16)

        self.cst = fw.sbuf("cst_s", [128, NCST, 128], F32)
        self.H = fw.sbuf("H", [128, 8, TA], BF16)
        self.WS = fw.sbuf("WS", [128, NROW * TA], F32)
        self.pv = fw.sbuf("pv_s", [128, NPV], F32)
        self.rowt = fw.sbuf("rowt_s", [128, 24], F32)
        self.sc = fw.sbuf("sc", [128, 8, 2], F32)
        self.mod = fw.sbuf("mod", [128, 48, 2], F32)
        self.gs = fw.sbuf("gs", [128, 4, 8, 2], F32)
        self.epsc = fw.sbuf("epsc", [128, 4], F32)
        self.wst = [fw.sbuf("wst%d" % i, [128, 8, 128], F32) for i in range(3)]
        self.wst_i = 0
        self.obf = fw.sbuf("obf", [128, TA], BF16)
        self.wbf = [fw.sbuf("wbf%d" % i, [128, 8, 128], BF16) for i in range(3)]
        self.wbf_i = 0
        self.wba = fw.sbuf("wba_s", [128, 8, 24], BF16)
        self.small = fw.sbuf("small", [128, 4096], F32)
        self.misc = fw.sbuf("misc", [128, 2048], F32)
        self.gt = fw.sbuf("gt", [128, 10, 18, 12], F32)
        self.lwt = fw.sbuf("lwt", [128, 8, 128], F32)
        self.rup = fw.sbuf("rup_s", [128, 3, 384], F32)
        self.identb = fw.sbuf("identb", [128, 128], BF16)
        self.pb = [fw.psum("pb%d" % i, [128, 512], F32) for i in range(8)]

    def row(self, r):
        return self.WS[:, r * TA:(r + 1) * TA]

    def rowc(self, r, c0, c1, p0=0, p1=128):
        return self.WS[p0:p1, r * TA + c0:r * TA + c1]

    def pvc(self, name, j=0, p0=0, p1=128):
        o, w = PV_OFF[name]
        return self.pv[p0:p1, o + j:o + j + 1]

    def C(self, idx, p0=0, p1=128, c0=0, c1=128):
        return self.cst[p0:p1, idx, c0:c1]

    def dump(self, name, view, shape):
        if name not in self.dbg:
            return
        o = self.fw.dram("dbg_" + name, list(shape), F32, kind="ExternalOutput")
        self.dbg_out[name] = o
        self.final.append(self.fw.dma(o.full(), view))

    def dump_row(self, name, j, nj, view):
        if name not in self.dbg:
            return
        if name not in self.dbg_out:
            self.dbg_out[name] = self.fw.dram("dbg_" + name, [nj, 128, TA], F32, kind="ExternalOutput")
        self.final.append(self.fw.dma(self.dbg_out[name][j], view))

    def next_wbf(self):
        t = self.wbf[self.wbf_i % 3]
        self.wbf_i += 1
        return t

    def load_w(self, dst, src, eng="pool"):
        st = self.wst[self.wst_i % 3]
        self.wst_i += 1
        self.fw.dma(st.full().w(lambda ap: ap.rearrange("p a b -> p (a b)")), src)
        self.fw.copy(dst, st.full(), eng=eng)

    def load_wc(self, dst, src, slot, first):
        r8 = lambda ap: ap.rearrange("p (a b) -> p a b", a=8)
        if first:
            self.load_w(dst, src, eng="dve")
            self.pending.append((dst, slot))
            while len(self.pending) > 1:
                v, s = self.pending.pop(0)
                self.fw.dma(self.wcache[s].w(r8), v)
        else:
            self.fw.dma(dst, self.wcache[slot].w(r8))

    def build(self):
        fw = self.fw
        fw.dma(self.cst.full(), self.cst_d.full())
        fw.dma(self.sc.full().w(lambda ap: ap.rearrange("p a b -> p (a b)")), self.call.full())
        fw.act(self.sc.full(), self.sc.full(), AF.Silu)
        fw.copy(self.identb.full(), self.C(C_ID))
        fw.memset(self.epsc[:, 0:1], 1e-6)
        fw.memset(self.epsc[:, 1:2], 1.0)
        fw.memset(self.epsc[:, 2:3], 6.4e-4)
        fw.memset(self.epsc[:, 3:4], 0.0)
        try:
            for layer in range(self.nlayers):
                self.layer(layer)
                if self.stop_after is not None and self.stop_after[0] == layer:
                    break
        except _Stop:
            pass
        fw.wait_tokens("sp", self.final)
        fw.emit()

    def layer(self, layer):
        fw = self.fw
        last = (layer == L - 1)
        fw.dma(self.pv.full(), self.pv_d[layer])
        fw.dma(self.rowt.full(), self.rowt_d[layer])
        fw.dma(self.lwt.full().w(lambda ap: ap.rearrange("p a b -> p (a b)")), self.lw_d[layer])
        fw.dma(self.rup.full().w(lambda ap: ap.rearrange("p a b -> p (a b)")), self.rup_d[layer])
        self.modulation(layer)
        self.phase_a(layer)
        if self.stop_after == (layer, "a"):
            return
        st = self.wst[self.wst_i % 3]
        self.wst_i += 1
        stv = st.full().w(lambda ap: ap.rearrange("p a b -> p (a b)")[:, 0:192])
        self.fw.dma(stv, self.wba_d[layer])
        self.fw.copy(self.wba.full().w(lambda ap: ap.rearrange("p a b -> p (a b)")), stv, eng="pool")
        if "gdn" not in self.skip:
            self.gdn(layer)
        if self.stop_after == (layer, "gdn"):
            return
        if "lru" not in self.skip:
            self.lru(layer)
        if self.stop_after == (layer, "lru"):
            return
        if "rwkv" not in self.skip:
            self.rwkv(layer)
        if self.stop_after == (layer, "rwkv"):
            return
        self.phase_c(layer, last)

    def modulation(self, layer):
        fw = self.fw
        mp = self.pb[7]
        modp = mp[:, 0:96].w(lambda ap: ap.rearrange("p (a b) -> p a b", b=2))
        for j in range(48):
            wt = self.wst[self.wst_i % 3]
            self.wst_i += 1
            fw.dma(wt.full().w(lambda ap: ap.rearrange("p a b -> p (a b)")), self.adaw[layer * 48 + j])
            for kc in range(8):
                fw.mm(mp[:, 2 * j:2 * j + 2], wt[:, kc, :], self.sc[:, kc, :], start=(kc == 0), stop=(kc == 7))
        o, w = PV_OFF["ada_b"]
        bb = _bc(self.pv[:, o:o + 48], [[1, 48], [0, 2]])
        fw.tt(self.mod.full(), modp, bb, ALU.add)
        self.dump("mod%d" % layer, self.mod.full(), [128, 48, 2])

        def gcol(name):
            o, w = PV_OFF[name]
            return _bc(self.pv[:, o:o + 8], [[1, 8], [0, 2]])
        fw.stt(self.gs[:, 0, :, :], self.mod[:, 8:16, :], 1.0, gcol("g_pre"), ALU.add, ALU.mult)
        fw.tt(self.gs[:, 1, :, :], self.mod[:, 16:24, :], gcol("g_post"), ALU.mult)
        fw.stt(self.gs[:, 2, :, :], self.mod[:, 32:40, :], 1.0, gcol("g_fpre"), ALU.add, ALU.mult)
        fw.tt(self.gs[:, 3, :, :], self.mod[:, 40:48, :], gcol("g_fpost"), ALU.mult)

    def rms_rstd(self, src3, sq3, n, dst, ps):
        fw = self.fw
        fw.act(sq3, src3, AF.Square)
        for kc in range(8):
            fw.mm(ps, self.C(C_ONES), self._k(sq3, kc), start=(kc == 0), stop=(kc == 7))
        fw.act(dst, ps, AF.Ln, bias=self.epsc[:, 0:1], scale=1.0 / 1024.0)
        fw.act(dst, dst, AF.Exp, scale=-0.5)

    def _k(self, v3, kc):
        return v3.w(lambda ap: ap[:, kc, :])

    def phase_a(self, layer):
        fw = self.fw
        src = self.xin if layer == 0 else self.xr
        xt = self.small[:, 0:2048].w(lambda ap: ap.rearrange("p (a b) -> p a b", a=8))
        sq = self.small[:, 2048:4096].w(lambda ap: ap.rearrange("p (a b) -> p a b", a=8))
        for pc in range(NPIECE):
            which = 1 if pc == 0 else 0
            fw.dma(xt, src[pc].w(lambda ap: ap.rearrange("p (a b) -> p a b", a=8)))
            rstd = self.misc[:, 0:PW_]
            self.rms_rstd(xt, sq, PW_, rstd, self.pb[6][:, 0:PW_])
            for kc in range(8):
                tmp = self.misc[:, (1 + kc % 2) * PW_:(2 + kc % 2) * PW_]
                fw.tt(tmp, self._k(xt, kc), rstd, ALU.mult)
                fw.ts(self.H[:, kc, pc * PW_:(pc + 1) * PW_], tmp, self.gs[:, 0, kc, which:which + 1], ALU.mult,
                      self.mod[:, kc, which:which + 1], ALU.add, eng=("pool" if kc % 2 else "dve"))
        if ("h%d" % layer) in self.dbg:
            for kc in range(8):
                fw.copy(self.row(0), self.H[:, kc, :])
                self.dump_row("h%d" % layer, kc, 8, self.row(0))

    def inproj(self, layer, j, dst_row):
        fw = self.fw
        wt = self.next_wbf()
        self.load_w(wt.full(), self.win[layer * 28 + j])
        for ti, (c0, c1) in enumerate(TOK_TILES):
            ps = self.pb[ti % 2][:, 0:c1 - c0]
            for kc in range(8):
                fw.mm(ps, wt[:, kc, :], self.H[:, kc, c0:c1], start=(kc == 0), stop=(kc == 7))
            fw.copy(self.rowc(dst_row, c0, c1), ps, eng=("act" if ti % 2 == 0 else "dve"))

    def store_o(self, ch, row_view):
        self.fw.copy(self.obf.full(), row_view, eng="act")
        self.fw.dma(self.od[ch], self.obf.full())

    def conv4(self, dst_row, src_row, wname, j, segs=SEGS):
        fw = self.fw
        o, w = PV_OFF[wname]
        wc = lambda k: self.pv[:, o + 4 * j + k:o + 4 * j + k + 1]
        fw.ts(self.row(dst_row), self.row(src_row), wc(2), ALU.mult)
        for (a, b) in segs:
            for k, sh in ((0, -2), (1, -1), (3, 1)):
                lo = max(a, a - sh)
                hi = min(b, b - sh)
                fw.stt(self.rowc(dst_row, lo, hi), self.rowc(src_row, lo + sh, hi + sh), wc(k),
                       self.rowc(dst_row, lo, hi), ALU.mult, ALU.add)

    def headstat(self, dst_row, src_row, eps_col, scale=1.0, square=True):
        fw = self.fw
        for ti, (c0, c1) in enumerate(TOK_TILES):
            n = c1 - c0
            sq = self.small[:, (ti % 2) * 512:(ti % 2) * 512 + n]
            fw.act(sq, self.rowc(src_row, c0, c1), AF.Square)
            ps = self.pb[2 + (ti % 2)][:, 0:n]
            fw.mm(ps, self.C(C_BO), sq)
            fw.act(self.rowc(dst_row, c0, c1), ps, AF.Ln, bias=self.epsc[:, eps_col:eps_col + 1], scale=scale)
        fw.act(self.row(dst_row), self.row(dst_row), AF.Exp, scale=-0.5)

    def neumann(self, P, nb, C, sign, nlev, XY, G, bf=False):
        fw = self.fw
        PX, PY, PG = self.pb[2], self.pb[3], self.pb[4]
        pv3 = lambda pbk: pbk[0:P, 0:nb * C].w(lambda ap: ap.rearrange("p (a b) -> p a b", b=C))
        identb = _bc(self.C(C_ID, 0, P, 0, C), [[0, nb], [1, C]])
        fw.tt(G[:, 0, :, :], identb, XY[:, 0, 1, :, :], ALU.add if sign > 0 else ALU.subtract)
        for p in range(1, nlev + 1):
            s, d = (p - 1) % 2, p % 2
            for u in range(nb):
                fw.mm(PX[0:P, u * C:(u + 1) * C], XY[:, s, 1, u, :], XY[:, s, 0, u, :])
            fw.copy(XY[:, d, 0, :, :], pv3(PX), eng="dve")
            if p < nlev:
                for u in range(nb):
                    fw.mm(PY[0:P, u * C:(u + 1) * C], XY[:, s, 0, u, :], XY[:, s, 1, u, :])
                fw.copy(XY[:, d, 1, :, :], pv3(PY), eng="act")
            for u in range(nb):
                fw.mm(PG[0:P, u * C:(u + 1) * C], XY[:, d, 0, u, :], G[:, s, u, :])
            fw.tt(G[:, d, :, :], pv3(PG), G[:, s, :, :], ALU.add)
        return nlev % 2

    def gdn_tables(self, layer):
        fw = self.fw
        gt = self.gt
        pba = self.pb[7][:, 0:432].w(lambda ap: ap.rearrange("p (a b) -> p a b", b=24))
        for i in range(18):
            for kc in range(8):
                fw.mm(self.pb[7][:, i * 24:(i + 1) * 24], self.H[:, kc, i * 128:(i + 1) * 128], self.wba[:, kc, :],
                      start=(kc == 0), stop=(kc == 7))
        braw = pba.w(lambda ap: ap[:, :, 0:12])
        araw = pba.w(lambda ap: ap[:, :, 12:24])
        T_G, T_LB, T_GC, T_GCL, T_NGC, T_B, T_BG, T_KD, T_EGL, T_TMP = range(10)
        one = self.epsc[:, 1:2]
        fw.act(gt[:, T_TMP], braw, AF.Exp, scale=-1.0)
        fw.act(gt[:, T_TMP], gt[:, T_TMP], AF.Ln, bias=one)
        fw.ts(gt[:, T_LB], gt[:, T_TMP], -1.0, ALU.mult)
        fw.tt(gt[:, T_TMP], araw, _bc(self.rowt[:, 12:24], [[0, 18], [1, 12]]), ALU.add)
        fw.act(gt[:, T_TMP], gt[:, T_TMP], AF.Exp)
        fw.act(gt[:, T_TMP], gt[:, T_TMP], AF.Ln, bias=one)
        na = self.misc[:, 0:12]
        fw.act(na, self.rowt[:, 0:12], AF.Exp)
        fw.stt(gt[:, T_G], gt[:, T_TMP], -1.0, _bc(na, [[0, 18], [1, 12]]), ALU.mult, ALU.mult)
        pg = self.pb[6][:, 0:216].w(lambda ap: ap.rearrange("p (a b) -> p a b", b=12))
        pl = self.pb[6][:, 256:472].w(lambda ap: ap.rearrange("p (a b) -> p a b", b=12))
        for i in range(18):
            for d in range(2):
                fw.mm(self.pb[6][:, i * 12 + d * 6:i * 12 + d * 6 + 6], self.C(C_UI if d == 0 else C_LI),
                      gt[:, T_G, i, d * 6:d * 6 + 6])
            fw.mm(self.pb[6][:, 256 + i * 12:256 + i * 12 + 12], self.C(C_ONES), gt[:, T_G, i, :])
        fw.copy(gt[:, T_GC], pg)
        fw.tt(gt[:, T_GCL], gt[:, T_GC], gt[:, T_LB], ALU.add)
        fw.ts(gt[:, T_NGC], gt[:, T_GC], -1.0, ALU.mult)
        fw.act(gt[:, T_B], gt[:, T_LB], AF.Exp)
        fw.act(gt[:, T_BG], gt[:, T_GCL], AF.Exp)
        fw.tt(gt[:, T_KD], pl, gt[:, T_GC], ALU.subtract)
        fw.act(gt[:, T_KD], gt[:, T_KD], AF.Exp)
        fw.act(gt[:, T_EGL], pl, AF.Exp)
        if ("gdn_g%d" % layer) in self.dbg:
            self.dump("gdn_g%d" % layer, gt[:, T_G], [128, 18, 12])
            self.dump("gdn_lb%d" % layer, gt[:, T_LB], [128, 18, 12])
            self.dump("gdn_gc%d" % layer, gt[:, T_GC], [128, 18, 12])

    def gdn(self, layer):
        fw = self.fw
        self.gdn_tables(layer)
        self.stop("gdn_t")
        T_G, T_LB, T_GC, T_GCL, T_NGC, T_B, T_BG, T_KD, T_EGL, T_TMP = range(10)
        gt = self.gt
        RQ, RK, RV, RT, RO0, RO1 = 0, 1, 2, 3, 4, 5
        base = 6 * TA
        WSt = self.WS
        gb = TA_(WSt, base, [128, 4, 128])
        lbb = TA_(WSt, base + 512, [128, 4, 128])
        dec = TA_(WSt, base + 1024, [128, 3, 4, 128])
        XY = TA_(WSt, base + 2560, [128, 2, 2, 4, 128])
        G = TA_(WSt, base + 4608, [128, 2, 4, 128])
        AT = TA_(WSt, base + 5632, [128, 2, 4, 128])
        TOK = TA_(WSt, base + 6656, [128, 2, 3, 4, 64])
        TOKB = TOK
        U = TA_(WSt, base + 8192, [128, 2, 4, 64])
        WT = TA_(WSt, base + 8704, [128, 2, 2, 128])
        QG = TA_(self.small, 0, [128, 2, 2, 128])
        EG = TA_(self.small, 512, [128, 4, 128])
        S = TA_(self.small, 1024, [128, 64])
        VN = TA_(self.small, 1088, [128, 2, 128])
        pb = self.pb
        for hp in range(3):
            for (j, dst) in ((hp, RQ), (3 + hp, RK), (6 + hp, RV)):
                self.inproj(layer, j, RT)
                self.conv4(dst, RT, "gconv", j)
                fw.act(self.row(dst), self.row(dst), AF.Silu)
            self.headstat(RT, RQ, 0)
            fw.stt(self.row(RQ), self.row(RQ), 0.125, self.row(RT), ALU.mult, ALU.mult)
            self.headstat(RT, RK, 0)
            fw.tt(self.row(RK), self.row(RK), self.row(RT), ALU.mult)
            if layer == 0 or True:
                self.dump_row("gdn_q%d" % layer, hp, 3, self.row(RQ))
                self.dump_row("gdn_k%d" % layer, hp, 3, self.row(RK))
                self.dump_row("gdn_v%d" % layer, hp, 3, self.row(RV))
            self.stop("gdn_p")
            for d in range(2):
                RO = RO0 + d
                fw.memset(S.full(), 0.0)
                order = GDN_ORDER[d]
                mincl = C_UI if d == 0 else C_LI
                m1 = C_PUI if d == 0 else C_PLI
                m2 = C_NLI if d == 0 else C_NUI
                m3 = C_NSL if d == 0 else C_NSU
                for b in range(9):
                    rr = b % 2
                    tiles = (order[2 * b], order[2 * b + 1])
                    units = [(hh, ts) for hh in range(2) for ts in range(2)]

                    def cidx(hh):
                        return d * 6 + 2 * hp + hh
                    for hh in range(2):
                        for ts in range(2):
                            i = tiles[ts]
                            u = hh * 2 + ts
                            ci = cidx(hh)
                            fw.copy(gb[:, u, :], _bc(gt[:, T_G, i, ci:ci + 1], [[0, 128]]), eng="pool")
                            fw.copy(lbb[:, u, :], _bc(gt[:, T_LB, i, ci:ci + 1], [[0, 128]]), eng="pool")
                    for u, (hh, ts) in enumerate(units):
                        sl = slice(u * 128, (u + 1) * 128)
                        fw.mm(pb[2][:, sl], gb[:, u, :], self.C(mincl), start=True, stop=False)
                        fw.mm(pb[2][:, sl], self.C(C_ID), self.C(m1), start=False, stop=True)
                        fw.mm(pb[3][:, sl], gb[:, u, :], self.C(mincl), start=True, stop=False)
                        fw.mm(pb[3][:, sl], lbb[:, u, :], self.C(C_ID), start=False, stop=False)
                        fw.mm(pb[3][:, sl], self.C(C_ID), self.C(m2), start=False, stop=True)
                        fw.mm(pb[4][:, sl], gb[:, u, :], self.C(mincl), start=True, stop=False)
                        fw.mm(pb[4][:, sl], self.C(C_ID), self.C(m3), start=False, stop=True)
                        fw.mm(pb[5][:, sl], gb[:, u, :], self.C(mincl), start=True, stop=True)
                    for u, (hh, ts) in enumerate(units):
                        i = tiles[ts]
                        sl = slice(u * 128, (u + 1) * 128)
                        ci = cidx(hh)
                        fw.act(dec[:, 0, u, :], pb[2][:, sl], AF.Exp, bias=gt[:, T_GCL, i, ci:ci + 1], scale=-1.0)
                        fw.act(dec[:, 1, u, :], pb[3][:, sl], AF.Exp, bias=gt[:, T_NGC, i, ci:ci + 1], scale=1.0)
                        fw.act(dec[:, 2, u, :], pb[4][:, sl], AF.Exp, bias=gt[:, T_NGC, i, ci:ci + 1], scale=1.0)
                    fw.act(EG.full(), pb[5][:, 0:512].w(lambda ap: ap.rearrange("p (a b) -> p a b", b=128)), AF.Exp)
                    for u, (hh, ts) in enumerate(units):
                        i = tiles[ts]
                        hb = hh * 64
                        cs = (i * 128, (i + 1) * 128)
                        kT = self.rowc(RK, cs[0], cs[1], hb, hb + 64)
                        qT = self.rowc(RQ, cs[0], cs[1], hb, hb + 64)
                        vT = self.rowc(RV, cs[0], cs[1], hb, hb + 64)
                        idn = self.C(C_ID, hb, hb + 64, hb, hb + 64)
                        fw.mm(pb[6 + hh][:, ts * 128:(ts + 1) * 128], kT, kT)
                        fw.mm(pb[6 + hh][:, 256 + ts * 128:256 + (ts + 1) * 128], kT, qT)
                        fw.tr(pb[hh][:, ts * 64:(ts + 1) * 64], kT, idn)
                        fw.tr(pb[hh][:, 128 + ts * 64:128 + (ts + 1) * 64], vT, idn)
                    for hh in range(2):
                        kk3 = pb[6 + hh][:, 0:256].w(lambda ap: ap.rearrange("p (a b) -> p a b", b=128))
                        kq3 = pb[6 + hh][:, 256:512].w(lambda ap: ap.rearrange("p (a b) -> p a b", b=128))
                        us = slice(hh * 2, hh * 2 + 2)
                        fw.tt(XY[:, 0, 0, us, :], kk3, dec[:, 0, us, :], ALU.mult)
                        fw.tt(XY[:, 0, 1, us, :], kk3, dec[:, 1, us, :], ALU.mult)
                        fw.tt(AT[:, rr, us, :], kq3, dec[:, 2, us, :], ALU.mult)
                    for u, (hh, ts) in enumerate(units):
                        i = tiles[ts]
                        ci = cidx(hh)
                        ktk = pb[hh][:, ts * 64:(ts + 1) * 64]
                        vtk = pb[hh][:, 128 + ts * 64:128 + (ts + 1) * 64]
                        fw.ts(TOKB[:, rr, 0, u, :], vtk, gt[:, T_B, i, ci:ci + 1], ALU.mult)
                        fw.ts(TOKB[:, rr, 1, u, :], ktk, gt[:, T_BG, i, ci:ci + 1], ALU.mult)
                        fw.ts(TOK[:, rr, 2, u, :], ktk, gt[:, T_KD, i, ci:ci + 1], ALU.mult)
                    for u, (hh, ts) in enumerate(units):
                        i = tiles[ts]
                        hb = hh * 64
                        fw.tt(QG[hb:hb + 64, rr, ts, :], self.rowc(RQ, i * 128, (i + 1) * 128, hb, hb + 64),
                              EG[hb:hb + 64, u, :], ALU.mult, eng="pool")
                    gi = self.neumann(128, 4, 128, -1, 6, XY, G)
                    for u, (hh, ts) in enumerate(units):
                        hb = hh * 64
                        fw.mm(pb[5][:, u * 64:(u + 1) * 64], G[:, gi, u, :], TOKB[:, rr, 0, u, :])
                        fw.mm(pb[6][hb:hb + 64, ts * 128:(ts + 1) * 128], TOKB[:, rr, 1, u, :], G[:, gi, u, :])
                    fw.copy(U[:, rr, :, :], pb[5][:, 0:256].w(lambda ap: ap.rearrange("p (a b) -> p a b", b=64)), eng="act")
                    fw.copy(WT[:, rr, :, :], pb[6][:, 0:256].w(lambda ap: ap.rearrange("p (a b) -> p a b", b=128)), eng="dve")
                    self.stop("gdn_b0")
                    for ts in range(2):
                        i = tiles[ts]
                        vr = ts
                        for hh in range(2):
                            hb = hh * 64
                            fw.mm(pb[hh][:, 0:64], WT[hb:hb + 64, rr, ts, :], S[hb:hb + 64, :])
                        for hh in range(2):
                            fw.tt(VN[:, vr, hh * 64:(hh + 1) * 64], U[:, rr, hh * 2 + ts, :], pb[hh][:, 0:64], ALU.subtract)
                        self.stop("gdn_ra")
                        for hh in range(2):
                            hb = hh * 64
                            u = hh * 2 + ts
                            fw.mm(pb[2][hb:hb + 64, 0:128], S[hb:hb + 64, :], QG[hb:hb + 64, rr, ts, :])
                            fw.mm(pb[3][hb:hb + 64, 0:128], VN[:, vr, hh * 64:(hh + 1) * 64], AT[:, rr, u, :])
                        fw.copy(self.rowc(RO, i * 128, (i + 1) * 128), pb[2][:, 0:128], eng="act")
                        fw.tt(self.rowc(RO, i * 128, (i + 1) * 128), self.rowc(RO, i * 128, (i + 1) * 128), pb[3][:, 0:128], ALU.add)
                        self.stop("gdn_rb")
                        for hh in range(2):
                            hb = hh * 64
                            u = hh * 2 + ts
                            fw.mm(pb[4][hb:hb + 64, 0:64], TOK[:, rr, 2, u, :], VN[:, vr, hh * 64:(hh + 1) * 64])
                        self.stop("gdn_rc")
                        for hh in range(2):
                            hb = hh * 64
                            ci = cidx(hh)
                            fw.stt(S[hb:hb + 64, :], S[hb:hb + 64, :], gt[hb:hb + 64, T_EGL, i, ci:ci + 1],
                                   pb[4][hb:hb + 64, 0:64], ALU.mult, ALU.add)
                    self.stop("gdn_r0")
                self.stop("gdn_d0")
            fw.tt(self.row(RO0), self.row(RO0), self.row(RO1), ALU.add)
            self.dump_row("gdn_o%d" % layer, hp, 3, self.row(RO0))
            self.headstat(RT, RO0, 0, scale=1.0 / 64.0)
            fw.stt(self.row(RO0), self.row(RO0), self.pvc("gnorm"), self.row(RT), ALU.mult, ALU.mult)
            self.inproj(layer, 9 + hp, RQ)
            fw.act(self.row(RQ), self.row(RQ), AF.Silu)
            fw.tt(self.row(RO0), self.row(RO0), self.row(RQ), ALU.mult)
            self.dump_row("mix%d" % layer, hp, 8, self.row(RO0))
            self.store_o(hp, self.row(RO0))

    def lru(self, layer):
        fw = self.fw
        RT, RP, RX, RA, RB, RH, RG, RI = 0, 1, 2, 3, 4, 5, 6, 7
        o, w = PV_OFF["llam"]
        c8 = self.misc[:, 1024:1028]
        c16 = self.misc[:, 1028:1032]
        fw.act(c8, self.pv[:, o:o + 4], AF.Exp, scale=-1.0)
        fw.act(c8, c8, AF.Ln, bias=self.epsc[:, 1:2])
        fw.ts(c16, c8, -16.0, ALU.mult)
        fw.ts(c8, c8, -8.0, ALU.mult)
        NL = TA - CTXN
        for jc in range(2):
            self.inproj(layer, 12 + jc, RT)
            fw.copy(self.rowc(RP, 0, CTXN), self.rowc(RT, 0, CTXN), eng="pool")
            fw.copy(self.rowc(RP, CTXN, TA), _bc(self.rowc(RT, CTXN, TA), [[1, 64], [64, 32]]), eng="pool")
            self.conv4(RX, RP, "lconv", jc)
            fw.ts(self.row(RX), self.row(RX), self.pvc("lconvb", jc), ALU.add)
            for d in range(2):
                for ti, (c0, c1) in enumerate(TOK_TILES):
                    n = c1 - c0
                    pa = self.pb[2 + ti % 2][:, 0:n]
                    px = self.pb[4 + ti % 2][:, 0:n]
                    fw.mm(pa, self.lwt[:, 0 * 4 + d * 2 + jc, :], self.rowc(RX, c0, c1))
                    fw.mm(px, self.lwt[:, 1 * 4 + d * 2 + jc, :], self.rowc(RX, c0, c1))
                    fw.act(self.rowc(RA, c0, c1), pa, AF.Sigmoid, bias=self.pvc("lba", d * 2 + jc))
                    fw.act(self.rowc(RI, c0, c1), px, AF.Sigmoid, bias=self.pvc("lbx", d * 2 + jc))
                k = d * 2 + jc
                fw.act(self.row(RB), self.row(RA), AF.Exp, scale=self.misc[:, 1028 + k:1029 + k])
                fw.ts(self.row(RB), self.row(RB), -1.0, ALU.mult, 1.0, ALU.add)
                fw.act(self.row(RB), self.row(RB), AF.Sqrt)
                fw.act(self.row(RA), self.row(RA), AF.Exp, scale=self.misc[:, 1024 + k:1025 + k])
                fw.tt(self.row(RI), self.row(RI), self.row(RX), ALU.mult, eng="pool")
                fw.tt(self.row(RB), self.row(RB), self.row(RI), ALU.mult)
                dst = RH if d == 0 else RG
                if d == 0:
                    fw.scan(self.row(dst), self.row(RA), self.row(RB), 0.0)
                else:
                    fw.scan(self.rowc(dst, 0, CTXN).w(lambda ap: _rev(ap, CTXN)),
                            self.rowc(RA, 0, CTXN).w(lambda ap: _rev(ap, CTXN)),
                            self.rowc(RB, 0, CTXN).w(lambda ap: _rev(ap, CTXN)), 0.0)
                    fw.scan(self.rowc(dst, CTXN, TA).w(lambda ap: _rev(ap, NL)),
                            self.rowc(RA, CTXN, TA).w(lambda ap: _rev(ap, NL)),
                            self.rowc(RB, CTXN, TA).w(lambda ap: _rev(ap, NL)), self.rowc(dst, 0, 1))
            fw.tt(self.row(RH), self.row(RH), self.row(RG), ALU.add)
            fw.copy(self.rowc(RP, 0, CTXN), self.rowc(RH, 0, CTXN), eng="pool")
            fw.copy(self.rowc(RP, CTXN, TA), _bc(self.rowc(RH, CTXN, TA), [[1, 32], [32, 64]]), eng="pool")
            self.inproj(layer, 14 + jc, RG)
            fw.act(self.row(RT), self.row(RG), AF.Square)
            fw.ts(self.row(RT), self.row(RT), 0.044715, ALU.mult, 1.0, ALU.add)
            fw.tt(self.row(RT), self.row(RT), self.row(RG), ALU.mult)
            fw.act(self.row(RT), self.row(RT), AF.Sigmoid, scale=1.5957691216057308)
            fw.tt(self.row(RT), self.row(RT), self.row(RG), ALU.mult)
            fw.tt(self.row(RP), self.row(RP), self.row(RT), ALU.mult)
            self.dump_row("mix%d" % layer, 3 + jc, 8, self.row(RP))
            self.store_o(3 + jc, self.row(RP))

    def phase_c(self, layer, last):
        fw = self.fw
        WSt = self.WS
        NS = 768
        hid = TA_(WSt, 0, [128, 32, NS], dtype=BF16)
        Y = TA_(WSt, 12288, [128, 8, NS])
        wdnb = [TA_(WSt, 18432 + 2048 * i, [128, 32, 128], dtype=BF16) for i in range(2)]
        xt = self.small[:, 0:2048].w(lambda ap: ap.rearrange("p (a b) -> p a b", a=8))
        sq = self.small[:, 2048:4096].w(lambda ap: ap.rearrange("p (a b) -> p a b", a=8))
        src = self.xin if layer == 0 else self.xr
        self.wcache = fw.dram("wcache%d" % layer, [72, 128, 1024], BF16)
        self.pending = []
        for st in range(3):
            first = (st == 0)
            pcs = [pc for pc in range(3 * st, 3 * st + 3) if not (last and pc == 0)]
            base = pcs[0] * PW_
            ncol = len(pcs) * PW_
            nt = []
            c = pcs[0] * PW_
            end = (pcs[-1] + 1) * PW_
            while c < end:
                n = min(512, end - c)
                if c < CTXN:
                    n = min(n, CTXN - c)
                nt.append((c - base, c - base + n))
                c += n
            Ot = self.H[:, :, 0:ncol]
            fw.dma(Ot, self.od[:, :, base:base + ncol].w(lambda ap: ap.rearrange("k p t -> p k t")))
            for m in range(8):
                wt = self.next_wbf()
                self.load_wc(wt.full(), self.wout[layer * 8 + m], m, first)
                for ti, (a, b) in enumerate(nt):
                    ps = self.pb[ti % 2][:, 0:b - a]
                    for kc in range(8):
                        fw.mm(ps, wt[:, kc, :], self.H[:, kc, a:b], start=(kc == 0), stop=(kc == 7))
                    fw.copy(Y[:, m, a:b], ps, eng=("act" if ti % 2 == 0 else "dve"))
            for pi, pc in enumerate(pcs):
                which = 1 if pc == 0 else 0
                lo = pi * PW_
                fw.dma(xt, src[pc].w(lambda ap: ap.rearrange("p (a b) -> p a b", a=8)))
                rstd = self.misc[:, 0:PW_]
                self.rms_rstd(Y[:, :, lo:lo + PW_], sq, PW_, rstd, self.pb[6][:, 0:PW_])
                for kc in range(8):
                    tmp = self.misc[:, (1 + kc % 2) * PW_:(2 + kc % 2) * PW_]
                    fw.tt(tmp, Y[:, kc, lo:lo + PW_], rstd, ALU.mult)
                    fw.stt(self._k(xt, kc), tmp, self.gs[:, 1, kc, which:which + 1], self._k(xt, kc), ALU.mult, ALU.add)
                if ("xmid%d" % layer) in self.dbg:
                    if ("xmid%d" % layer) not in self.dbg_out:
                        self.dbg_out["xmid%d" % layer] = fw.dram("dbg_xmid%d" % layer, [NPIECE, 128, 8 * PW_], F32, kind="ExternalOutput")
                    self.final.append(fw.dma(self.dbg_out["xmid%d" % layer][pc].w(lambda ap: ap.rearrange("p (a b) -> p a b", a=8)), xt))
                fw.dma(self.xr[pc].w(lambda ap: ap.rearrange("p (a b) -> p a b", a=8)), xt)
                rstd2 = self.misc[:, 3 * PW_:4 * PW_]
                self.rms_rstd(xt, sq, PW_, rstd2, self.pb[6][:, 0:PW_])
                for kc in range(8):
                    tmp = self.misc[:, (1 + kc % 2) * PW_:(2 + kc % 2) * PW_]
                    fw.tt(tmp, self._k(xt, kc), rstd2, ALU.mult)
                    fw.ts(self.H[:, kc, NS + lo:NS + lo + PW_], tmp, self.gs[:, 2, kc, which:which + 1], ALU.mult,
                          self.mod[:, 24 + kc, which:which + 1], ALU.add, eng=("pool" if kc % 2 else "dve"))
            for m in range(32):
                wt = self.next_wbf()
                self.load_wc(wt.full(), self.wup[layer * 32 + m], 8 + m, first)
                for ti, (a, b) in enumerate(nt):
                    ps = self.pb[ti % 2][:, 0:b - a]
                    for kc in range(8):
                        fw.mm(ps, wt[:, kc, :], self.H[:, kc, NS + a:NS + b], start=(kc == 0), stop=(kc == 7))
                    rl = self.small[:, 0:b - a] if ti % 2 == 0 else self.small[:, 512:512 + b - a]
                    fw.act(rl, ps, AF.Relu)
                    fw.tt(hid[:, m, a:b], rl, rl, ALU.mult, eng=("pool" if ti % 2 == 0 else "dve"))
            for m in range(8):
                wt = wdnb[m % 2]
                for kq in range(4):
                    self.load_wc(wt[:, 8 * kq:8 * kq + 8, :], self.wdn[layer * 8 + m, :, 1024 * kq:1024 * kq + 1024], 40 + 4 * m + kq, first)
                for ti, (a, b) in enumerate(nt):
                    ps = self.pb[ti % 2][:, 0:b - a]
                    for kc in range(32):
                        fw.mm(ps, wt[:, kc, :], hid[:, kc, a:b], start=(kc == 0), stop=(kc == 31))
                    fw.copy(Y[:, m, a:b], ps, eng=("act" if ti % 2 == 0 else "dve"))
            while self.pending:
                v, s = self.pending.pop(0)
                fw.dma(self.wcache[s].w(lambda ap: ap.rearrange("p (a b) -> p a b", a=8)), v)
            for pi, pc in enumerate(pcs):
                which = 1 if pc == 0 else 0
                lo = pi * PW_
                fw.dma(xt, self.xr[pc].w(lambda ap: ap.rearrange("p (a b) -> p a b", a=8)))
                rstd = self.misc[:, 0:PW_]
                self.rms_rstd(Y[:, :, lo:lo + PW_], sq, PW_, rstd, self.pb[6][:, 0:PW_])
                for kc in range(8):
                    tmp = self.misc[:, (1 + kc % 2) * PW_:(2 + kc % 2) * PW_]
                    fw.tt(tmp, Y[:, kc, lo:lo + PW_], rstd, ALU.mult)
                    fw.stt(self._k(xt, kc), tmp, self.gs[:, 3, kc, which:which + 1], self._k(xt, kc), ALU.mult, ALU.add)
                if last:
                    self.final.append(fw.dma(self.out_d[pc - 1].w(lambda ap: ap.rearrange("p (a b) -> p a b", a=8)), xt))
                else:
                    fw.dma(self.xr[pc].w(lambda ap: ap.rearrange("p (a b) -> p a b", a=8)), xt)

    def lerp(self, dst_row, raw_row, muidx):
        fw = self.fw
        for (a, b) in SEGS:
            fw.tt(self.rowc(dst_row, a + 1, b - 1), self.rowc(raw_row, a, b - 2), self.rowc(raw_row, a + 2, b), ALU.add)
            fw.copy(self.rowc(dst_row, a, a + 1), self.rowc(raw_row, a + 1, a + 2), eng="pool")
            fw.copy(self.rowc(dst_row, b - 1, b), self.rowc(raw_row, b - 2, b - 1), eng="pool")
        fw.ts(self.row(dst_row), self.row(dst_row), self.misc[:, 1040 + 12 + muidx:1040 + 13 + muidx], ALU.mult)
        fw.stt(self.row(dst_row), self.row(raw_row), self.misc[:, 1040 + muidx:1040 + muidx + 1], self.row(dst_row), ALU.mult, ALU.add)

    def rwkv(self, layer):
        fw = self.fw
        pb = self.pb
        RR, RV, RKK, RKA, RCW, RKD, RY, RBON, T1, T2 = range(10)
        o, w = PV_OFF["mu"]
        fw.ts(self.misc[:, 1040 + 0:1040 + 12], self.pv[:, o:o + 12], -1.0, ALU.mult, 1.0, ALU.add)
        fw.ts(self.misc[:, 1040 + 12:1040 + 24], self.pv[:, o:o + 12], 0.5, ALU.mult)
        o2, w2 = PV_OFF["ka"]
        fw.ts(self.misc[:, 1040 + 24:1040 + 27], self.pv[:, o2:o2 + 3], -1.0, ALU.mult, 1.0, ALU.add)
        spill = self.fw.dram("spill%d" % layer, [4, 128, TA], F32)
        WSt = self.WS
        bT = T1 * TA
        DER = TA_(WSt, bT, [128, 6, 4, 64])
        E1 = TA_(WSt, bT + 1536, [128, 4, 64])
        E2 = TA_(WSt, bT + 1792, [128, 4, 64])
        XY = TA_(WSt, bT + 2048, [64, 2, 2, 8, 64], dtype=BF16)
        ATKB = TA_(WSt, bT + 3072, [64, 8, 64], dtype=BF16)
        W1B = TA_(WSt, bT + 3328, [64, 8, 64], dtype=BF16)
        PC = TA_(WSt, bT + 4096, [128, 4])
        M = TA_(WSt, bT + 4104, [128, 64])
        US = TA_(WSt, bT + 4168, [64, 128])
        AHT = TA_(WSt, bT + 4296, [128, 4, 64])
        G = TA_(self.small, 0, [64, 2, 8, 64], dtype=BF16)
        MK = TA_(self.small, 1024, [64, 3, 8, 64])
        TK = TA_(self.small, 2560, [64, 3, 8, 64])
        VT = TA_(self.misc, 0, [64, 8, 64])
        W1 = TA_(self.misc, 512, [64, 8, 64])
        UV = TA_(self.misc, 1200, [64, 8, 64])
        p3 = lambda pbk: pbk[0:64, 0:512].w(lambda ap: ap.rearrange("p (a b) -> p a b", b=64))
        self.inproj(layer, 26, T1)
        self.lerp(T2, T1, 10)
        fw.dma(spill[1], self.row(T2))
        self.inproj(layer, 25, T1)
        self.lerp(T2, T1, 9)
        fw.act(self.row(T2), self.row(T2), AF.Tanh)
        fw.dma(spill[2], self.row(T2))
        self.inproj(layer, 27, T1)
        self.lerp(T2, T1, 11)
        fw.act(self.row(T2), self.row(T2), AF.Sigmoid)
        fw.dma(spill[3], self.row(T2))
        for hp in range(3):
            self.inproj(layer, 16 + hp, T1)
            self.lerp(RR, T1, hp)
            self.inproj(layer, 22 + hp, T1)
            self.lerp(RV, T1, 6 + hp)
            self.inproj(layer, 19 + hp, T1)
            self.lerp(T2, T1, 3 + hp)
            fw.dma(spill[0], self.row(T2))
            fw.ts(self.row(T1), self.row(T2), self.pvc("kk", hp), ALU.mult)
            self.headstat(RKK, T1, 0)
            fw.tt(self.row(RKK), self.row(RKK), self.row(T1), ALU.mult)
            for d in range(2):
                db = d * 64
                fw.dma(self.row(T2), spill[1])
                for ti, (c0, c1) in enumerate(TOK_TILES):
                    ps = pb[2 + ti % 2][:, 0:c1 - c0]
                    fw.mm(ps, self.rup[db:db + 64, 1, hp * 128:(hp + 1) * 128], self.rowc(T2, c0, c1, db, db + 64))
                    fw.act(self.rowc(RKA, c0, c1), ps, AF.Sigmoid, bias=self.pvc("a0", d * 3 + hp))
                fw.dma(self.row(T2), spill[0])
                fw.ts(self.row(T1), self.row(RKA), self.pvc("ka", hp), ALU.mult, self.misc[:, 1040 + 24 + hp:1040 + 25 + hp], ALU.add)
                fw.tt(self.row(RKD), self.row(T2), self.row(T1), ALU.mult)
                fw.tt(self.row(RKA), self.row(RKA), self.row(RKK), ALU.mult)
                if d == 0:
                    fw.stt(self.row(RBON), self.row(RR), self.pvc("rk", hp), self.row(RKD), ALU.mult, ALU.mult)
                else:
                    fw.stt(self.row(T1), self.row(RR), self.pvc("rk", hp), self.row(RKD), ALU.mult, ALU.mult)
                    fw.tt(self.row(RBON), self.row(RBON), self.row(T1), ALU.add)
                fw.dma(self.row(T2), spill[2])
                for ti, (c0, c1) in enumerate(TOK_TILES):
                    ps = pb[2 + ti % 2][:, 0:c1 - c0]
                    fw.mm(ps, self.rup[db:db + 64, 0, hp * 128:(hp + 1) * 128], self.rowc(T2, c0, c1, db, db + 64))
                    fw.act(self.rowc(T1, c0, c1), ps, AF.Sigmoid, bias=self.pvc("w0", d * 3 + hp))
                fw.ts(self.row(T1), self.row(T1), -0.6065306597126334, ALU.mult)
                onesb = _bc(self.epsc[:, 1:2], [[0, TA]])
                r3 = lambda ap: ap.rearrange("p (a b) -> p a b", b=64)
                if d == 0:
                    fw.scan(self.row(T2), onesb, self.row(T1), 0.0)
                    fw.tt(self.rowc(RCW, 64, TA).w(r3), self.rowc(T2, 64, TA).w(r3),
                          _bc(self.rowc(T2, 63, 64), [[64, 35], [0, 64]]), ALU.subtract)
                    fw.copy(self.rowc(RCW, 0, 64), self.rowc(T2, 0, 64), eng="pool")
                else:
                    fw.scan(self.row(T2).w(lambda ap: _rev(ap, TA)), onesb, self.row(T1).w(lambda ap: _rev(ap, TA)), 0.0)
                    fw.tt(self.rowc(RCW, 0, TA - 64).w(r3), self.rowc(T2, 0, TA - 64).w(r3),
                          _bc(self.rowc(T2, 64, 65), [[64, 35], [0, 64]]), ALU.subtract)
                    fw.copy(self.rowc(RCW, TA - 64, TA), self.rowc(T2, TA - 64, TA), eng="pool")
                if d == 0 and hp == 0:
                    self.dump_row("rw_cw%d" % layer, 0, 1, self.row(RCW))
                    self.dump_row("rw_kd%d" % layer, 0, 1, self.row(RKD))
                    self.dump_row("rw_ka%d" % layer, 0, 1, self.row(RKA))
                fw.memset(M.full(), 0.0)
                sgn_strict_ts = C_SL if d == 0 else C_SU
                sgn_strict_st = C_SU if d == 0 else C_SL
                incl_st = C_UI if d == 0 else C_LI
                for b in range(9):
                    if d == 0:
                        w0c = 256 * b
                    else:
                        w0c = 0 if b == 0 else TA - 256 * b
                    win = lambda r: self.rowc(r, w0c, w0c + 256).w(lambda ap: ap.rearrange("p (a b) -> p a b", b=64))
                    lastp = 63 if d == 0 else 0
                    cwl = _bc(self.rowc(RCW, w0c + lastp, w0c + lastp + 1), [[64, 4], [0, 64]])
                    fw.act(E1.full(), win(RCW), AF.Exp)
                    fw.act(E2.full(), win(RCW), AF.Exp, scale=-1.0)
                    fw.tt(DER[:, 3], win(RR), E1.full(), ALU.mult)
                    fw.stt(DER[:, 1], win(RKA), -1.0, E2.full(), ALU.mult, ALU.mult)
                    fw.tt(DER[:, 2], win(RKD), E2.full(), ALU.mult, eng="pool")
                    if d == 0:
                        fw.tt(DER[:, 0, :, 1:64], self.rowc(RKK, w0c, w0c + 256).w(lambda ap: ap.rearrange("p (a b) -> p a b", b=64)[:, :, 1:64]),
                              E1[:, :, 0:63], ALU.mult)
                        fw.copy(DER[:, 0, :, 0:1], self.rowc(RKK, w0c, w0c + 256).w(lambda ap: ap.rearrange("p (a b) -> p a b", b=64)[:, :, 0:1]), eng="pool")
                    else:
                        fw.tt(DER[:, 0, :, 0:63], self.rowc(RKK, w0c, w0c + 256).w(lambda ap: ap.rearrange("p (a b) -> p a b", b=64)[:, :, 0:63]),
                              E1[:, :, 1:64], ALU.mult)
                        fw.copy(DER[:, 0, :, 63:64], self.rowc(RKK, w0c, w0c + 256).w(lambda ap: ap.rearrange("p (a b) -> p a b", b=64)[:, :, 63:64]), eng="pool")
                    fw.tt(E2.full(), cwl, win(RCW), ALU.subtract)
                    fw.act(E2.full(), E2.full(), AF.Exp)
                    fw.tt(DER[:, 4], win(RKD), E2.full(), ALU.mult, eng="pool")
                    fw.stt(DER[:, 5], win(RKA), -1.0, E2.full(), ALU.mult, ALU.mult)
                    fw.act(PC.full(), _bc(self.rowc(RCW, w0c + lastp, w0c + lastp + 1), [[64, 4]]), AF.Exp)
                    wcs = [cs if d == 0 else 3 - cs for cs in range(4)]
                    units = [(cs, hh) for cs in range(4) for hh in range(2)]
                    fm = lambda q, wc, hb: DER[hb:hb + 64, q, wc, :]
                    for u, (cs, hh) in sorted(enumerate(units), key=lambda t: (t[1][1], t[1][0])):
                        wc, hb = wcs[cs], hh * 64
                        sl = slice(u * 64, (u + 1) * 64)
                        fw.mm(pb[5][0:64, sl], fm(0, wc, hb), fm(1, wc, hb))
                        fw.mm(pb[6][0:64, sl], fm(1, wc, hb), fm(0, wc, hb))
                        fw.mm(pb[7][0:64, sl], fm(2, wc, hb), fm(0, wc, hb))
                    mk = lambda ci: _bc(self.C(ci, 0, 64, 0, 64), [[0, 8], [1, 64]])
                    fw.tt(XY[:, 0, 0, :, :], p3(pb[5]), mk(sgn_strict_ts), ALU.mult)
                    fw.tt(XY[:, 0, 1, :, :], p3(pb[6]), mk(sgn_strict_st), ALU.mult)
                    fw.tt(MK[:, 0, :, :], p3(pb[7]), mk(sgn_strict_st), ALU.mult)
                    for u, (cs, hh) in sorted(enumerate(units), key=lambda t: (t[1][1], t[1][0])):
                        wc, hb = wcs[cs], hh * 64
                        sl = slice(u * 64, (u + 1) * 64)
                        fw.mm(pb[5][0:64, sl], fm(1, wc, hb), fm(3, wc, hb))
                        fw.mm(pb[6][0:64, sl], fm(2, wc, hb), fm(3, wc, hb))
                    fw.tt(MK[:, 1, :, :], p3(pb[5]), mk(incl_st), ALU.mult)
                    fw.tt(MK[:, 2, :, :], p3(pb[6]), mk(incl_st), ALU.mult)
                    for u, (cs, hh) in sorted(enumerate(units), key=lambda t: (t[1][1], t[1][0])):
                        wc, hb = wcs[cs], hh * 64
                        sl = slice(u * 64, (u + 1) * 64)
                        idn = self.C(C_ID, hb, hb + 64, hb, hb + 64)
                        fw.tr(pb[0][0:64, sl], fm(0, wc, hb), idn)
                        fw.tr(pb[1][0:64, sl], fm(4, wc, hb), idn)
                        fw.tr(pb[7][0:64, sl], fm(5, wc, hb), idn)
                        c0 = w0c + wc * 64
                        fw.tr(pb[5][0:64, sl], self.rowc(RV, c0, c0 + 64, hb, hb + 64), idn)
                    fw.copy(ATKB.full(), p3(pb[0]), eng="act")
                    fw.copy(TK[:, 1, :, :], p3(pb[1]), eng="dve")
                    fw.copy(TK[:, 2, :, :], p3(pb[7]), eng="act")
                    fw.copy(VT.full(), p3(pb[5]), eng="dve")
                    gi = self.neumann(64, 8, 64, +1, 5, XY, G, bf=True)
                    for u in range(8):
                        fw.mm(pb[5][0:64, u * 64:(u + 1) * 64], MK[:, 0, u, :], VT[:, u, :])
                    fw.copy(W1B.full(), p3(pb[5]), eng="act")
                    for u, (cs, hh) in sorted(enumerate(units), key=lambda t: (t[1][1], t[1][0])):
                        hb = hh * 64
                        fw.mm(pb[6][0:64, u * 64:(u + 1) * 64], G[:, gi, u, :], W1B[:, u, :])
                        fw.mm(pb[7][hb:hb + 64, cs * 64:(cs + 1) * 64], ATKB[:, u, :], G[:, gi, u, :])
                    fw.copy(UV.full(), p3(pb[6]), eng="dve")
                    fw.copy(AHT.full(), pb[7][:, 0:256].w(lambda ap: ap.rearrange("p (a b) -> p a b", b=64)), eng="act")
                    for cs in range(4):
                        wc = wcs[cs]
                        c0 = w0c + wc * 64
                        for hh in range(2):
                            hb = hh * 64
                            fw.mm(pb[0][0:64, hh * 64:(hh + 1) * 64], AHT[hb:hb + 64, cs, :], M[hb:hb + 64, :])
                        fw.tt(US.full(), UV[:, 2 * cs:2 * cs + 2, :].w(lambda ap: ap.rearrange("p a b -> p (a b)")), pb[0][0:64, 0:128], ALU.add)
                        for hh in range(2):
                            hb = hh * 64
                            u = cs * 2 + hh
                            fw.mm(pb[1][hb:hb + 64, 0:64], M[hb:hb + 64, :], fm(3, wc, hb), start=True, stop=False)
                            fw.mm(pb[1][hb:hb + 64, 0:64], US[:, hh * 64:(hh + 1) * 64], MK[:, 1, u, :], start=False, stop=False)
                            fw.mm(pb[1][hb:hb + 64, 0:64], VT[:, u, :], MK[:, 2, u, :], start=False, stop=True)
                        if d == 0:
                            fw.copy(self.rowc(RY, c0, c0 + 64), pb[1][:, 0:64], eng="act")
                        else:
                            fw.tt(self.rowc(RY, c0, c0 + 64), self.rowc(RY, c0, c0 + 64), pb[1][:, 0:64], ALU.add)
                        for hh in range(2):
                            hb = hh * 64
                            u = cs * 2 + hh
                            fw.mm(pb[0][hb:hb + 64, 256:320], TK[:, 1, u, :], VT[:, u, :], start=True, stop=False)
                            fw.mm(pb[0][hb:hb + 64, 256:320], TK[:, 2, u, :], US[:, hh * 64:(hh + 1) * 64], start=False, stop=True)
                        fw.stt(M.full(), M.full(), PC[:, wc:wc + 1], pb[0][:, 256:320], ALU.mult, ALU.add)
            self.dump_row("rw_y%d" % layer, hp, 3, self.row(RY))
            for ti, (c0, c1) in enumerate(TOK_TILES):
                n = c1 - c0
                ps = pb[2 + ti % 2][:, 0:n]
                fw.mm(ps, self.C(C_BO), self.rowc(RY, c0, c1))
                fw.stt(self.rowc(T1, c0, c1), ps, -1.0 / 64.0, self.rowc(RY, c0, c1), ALU.mult, ALU.add)
            self.headstat(T2, T1, 2, scale=1.0 / 64.0)
            fw.tt(self.row(T1), self.row(T1), self.row(T2), ALU.mult)
            fw.ts(self.row(T1), self.row(T1), self.pvc("gnw", hp), ALU.mult, self.pvc("gnb", hp), ALU.add)
            for ti, (c0, c1) in enumerate(TOK_TILES):
                n = c1 - c0
                ps = pb[2 + ti % 2][:, 0:n]
                fw.mm(ps, self.C(C_BO), self.rowc(RBON, c0, c1))
                fw.tt(self.rowc(T2, c0, c1), ps, self.rowc(RV, c0, c1), ALU.mult)
            fw.tt(self.row(T1), self.row(T1), self.row(T2), ALU.add)
            fw.dma(self.row(T2), spill[3])
            for ti, (c0, c1) in enumerate(TOK_TILES):
                n = c1 - c0
                ps = pb[2 + ti % 2][:, 0:n]
                fw.mm(ps, self.rup[:, 2, hp * 128:(hp + 1) * 128], self.rowc(T2, c0, c1))
                fw.tt(self.rowc(T1, c0, c1), self.rowc(T1, c0, c1), ps, ALU.mult)
            self.dump_row("mix%d" % layer, 5 + hp, 8, self.row(T1))
            self.store_o(5 + hp, self.row(T1))


def kernel(**inputs):
    inp = {k: np.asarray(v) for k, v in inputs.items()}
    shared = host_shared(inp)
    nc = bass.Bass("TRN2", target_bir_lowering=False)
    prog = Prog(nc)
    prog.build()
    in_maps = []
    for b in range(8):
        m = dict(shared)
        m.update(host_core(inp, b))
        in_maps.append(m)
    res = run_bass_kernel_spmd(nc, in_maps, core_ids=list(range(8)))
    out = np.empty((8, 2048, 1024), np.float32)
    for b in range(8):
        o = np.asarray(res.results[b]["out"]).reshape(8, 128, 8, PW_)
        out[b] = o.transpose(0, 3, 2, 1).reshape(2048, 1024)
    return out
```
